# Optimizing a Trainium2 kernel written in Bass

```python
import math
import jax, jax.numpy as jnp
from jax import lax
import numpy as np

D_MODEL = 1024
BATCH = 8
SEQ = 4096
DEPTH = 4

N_MEM = 256
HEAD_DIM = 64
MIX_WIDTH = 3 * D_MODEL // 4
MIX_HEADS = MIX_WIDTH // HEAD_DIM
MEM_HEADS = 4
MEM_WIDTH = MEM_HEADS * HEAD_DIM
NSA_KV_GROUPS = 4
NSA_GROUP_SIZE = MIX_HEADS // NSA_KV_GROUPS
KV_WIDTH = NSA_KV_GROUPS * HEAD_DIM
CMP_BLOCK = 32
CMP_STRIDE = 16
CMP_HIDDEN = 2 * HEAD_DIM
SEL_BLOCK = 64
SEL_TOPN = 16
WINDOW = 512
NSA_Q_BLOCK = 64
FORCE_SCORE = 1.0e4
NSA_IN = MIX_WIDTH + 6 * KV_WIDTH + 3 * MIX_HEADS + MEM_WIDTH
DECAY_LORA = 64
ICLR_LORA = 64
GATE_LORA = 160
RW_SHIFT = 3 * MIX_WIDTH + DECAY_LORA + ICLR_LORA + GATE_LORA
RW_IN = RW_SHIFT + MEM_WIDTH
GN_EPS = 64e-5
FFN_HIDDEN = -(-8 * D_MODEL // (3 * 256)) * 256
N_NSA = (DEPTH + 1) // 2
N_RWKV = DEPTH // 2
RMS_EPS = 1e-6
NEG_INF = -1e30

kernel_name = "nsa_rwkv7_memxattn_hybrid_trunk"


def _rmsnorm(x, g):
    xf = x.astype(jnp.float32)
    y = xf * lax.rsqrt(jnp.mean(xf * xf, axis=-1, keepdims=True) + RMS_EPS)
    return (y * g.astype(jnp.float32)).astype(x.dtype)


def _split(x, sizes):
    idx = [int(i) for i in np.cumsum(sizes)[:-1]]
    return jnp.split(x, idx, axis=-1)


def _alibi_slopes(n):
    def pow2(m):
        start = 2.0 ** (-8.0 / m)
        return [start ** (i + 1) for i in range(m)]
    c = 2 ** int(math.floor(math.log2(n)))
    s = pow2(c)
    if c < n:
        s = s + pow2(2 * c)[0::2][: n - c]
    return np.asarray(s, dtype=np.float32)


def _masked_softmax(s, mask):
    p = jax.nn.softmax(jnp.where(mask, s, NEG_INF), axis=-1)
    return p * mask


def _compress(kv, pos, w1, w2):
    S = kv.shape[1]
    n_cmp = (S - CMP_BLOCK) // CMP_STRIDE + 1
    idx = jnp.arange(n_cmp)[:, None] * CMP_STRIDE + jnp.arange(CMP_BLOCK)[None, :]
    blocks = kv[:, idx] + pos[None, None, :, None, :]
    hid = jax.nn.gelu(jnp.einsum('bnlgd,lde->bnge', blocks, w1))
    return jnp.einsum('bnge,ed->bngd', hid, w2)


def _nsa_mixer(h, w_in, gate_b, cmp_pos, cmp_w1, cmp_w2):
    B, S, _ = h.shape
    G, R, Dh, QB = NSA_KV_GROUPS, NSA_GROUP_SIZE, HEAD_DIM, NSA_Q_BLOCK
    proj = h @ w_in
    q, kc, vc, ks, vs, kw, vw, gl, q_mem = _split(
        proj, [MIX_WIDTH] + [KV_WIDTH] * 6 + [3 * MIX_HEADS, MEM_WIDTH])
    q = q.reshape(B, S, G, R, Dh)
    kc, vc, ks, vs, kw, vw = (t.reshape(B, S, G, Dh) for t in (kc, vc, ks, vs, kw, vw))
    gates = jax.nn.sigmoid(gl + gate_b).reshape(B, S, 3, G, R)

    kc_b = _compress(kc, cmp_pos[0], cmp_w1[0], cmp_w2[0])
    vc_b = _compress(vc, cmp_pos[1], cmp_w1[1], cmp_w2[1])
    n_cmp = kc_b.shape[1]
    n_blk = S // SEL_BLOCK
    n_sel = min(SEL_TOPN, n_blk)
    ks_blk = ks.reshape(B, n_blk, SEL_BLOCK, G, Dh).transpose(0, 3, 1, 2, 4)
    vs_blk = vs.reshape(B, n_blk, SEL_BLOCK, G, Dh).transpose(0, 3, 1, 2, 4)
    kw_pad = jnp.pad(kw, ((0, 0), (WINDOW, 0), (0, 0), (0, 0)))
    vw_pad = jnp.pad(vw, ((0, 0), (WINDOW, 0), (0, 0), (0, 0)))

    slopes = jnp.asarray(_alibi_slopes(MIX_HEADS).reshape(G, R))[None, :, :, None, None]
    cmp_start = jnp.arange(n_cmp) * CMP_STRIDE
    cmp_end = cmp_start + CMP_BLOCK - 1
    sel_start = jnp.arange(n_blk) * SEL_BLOCK
    overlap = ((cmp_start[:, None] <= sel_start[None, :] + SEL_BLOCK - 1)
               & (cmp_end[:, None] >= sel_start[None, :])).astype(jnp.float32)
    blk_ids = jnp.arange(n_blk)
    b_idx = jnp.arange(B)[:, None, None, None]
    g_idx = jnp.arange(G)[None, :, None, None]
    scale = HEAD_DIM ** -0.5

    def chunk(c0):
        t = c0 + jnp.arange(QB)
        qc = lax.dynamic_slice_in_dim(q, c0, QB, axis=1)
        dist_c = t[:, None] - cmp_end[None, :]
        s_c = jnp.einsum('bqgrd,bngd->bgrqn', qc, kc_b).astype(jnp.float32) * scale \
            - slopes * dist_c.astype(jnp.float32)
        p_c = _masked_softmax(s_c, dist_c >= 0)
        o_c = jnp.einsum('bgrqn,bngd->bqgrd', p_c.astype(vc_b.dtype), vc_b)
        imp = jnp.einsum('bgrqn,nj->bgqj', p_c, overlap)
        tb = (t // SEL_BLOCK)[:, None]
        valid = blk_ids[None, :] <= tb
        forced = (blk_ids[None, :] == 0) | (blk_ids[None, :] == tb) | (blk_ids[None, :] == tb - 1)
        score = jnp.where(valid, jnp.where(forced, FORCE_SCORE, imp), -jnp.inf)
        top_s, top_i = lax.top_k(score, n_sel)
        kg = ks_blk[b_idx, g_idx, top_i].reshape(B, G, QB, n_sel * SEL_BLOCK, Dh)
        vg = vs_blk[b_idx, g_idx, top_i].reshape(B, G, QB, n_sel * SEL_BLOCK, Dh)
        pos = (top_i[..., None] * SEL_BLOCK + jnp.arange(SEL_BLOCK)).reshape(B, G, QB, -1)
        ok = jnp.broadcast_to(jnp.isfinite(top_s)[..., None],
                              (B, G, QB, n_sel, SEL_BLOCK)).reshape(B, G, QB, -1)
        dist_s = t[None, None, :, None] - pos
        mask_s = ok & (dist_s >= 0)
        s_s = jnp.einsum('bqgrd,bgqkd->bgrqk', qc, kg).astype(jnp.float32) * scale \
            - slopes * dist_s[:, :, None].astype(jnp.float32)
        p_s = _masked_softmax(s_s, mask_s[:, :, None])
        o_s = jnp.einsum('bgrqk,bgqkd->bqgrd', p_s.astype(vg.dtype), vg)
        kwc = lax.dynamic_slice_in_dim(kw_pad, c0, WINDOW + QB, axis=1)
        vwc = lax.dynamic_slice_in_dim(vw_pad, c0, WINDOW + QB, axis=1)
        kp = c0 - WINDOW + jnp.arange(WINDOW + QB)
        dist_w = t[:, None] - kp[None, :]
        mask_w = (kp[None, :] >= 0) & (dist_w >= 0) & (dist_w < WINDOW)
        s_w = jnp.einsum('bqgrd,bkgd->bgrqk', qc, kwc).astype(jnp.float32) * scale \
            - slopes * dist_w.astype(jnp.float32)
        p_w = _masked_softmax(s_w, mask_w)
        o_w = jnp.einsum('bgrqk,bkgd->bqgrd', p_w.astype(vwc.dtype), vwc)
        gc = lax.dynamic_slice_in_dim(gates, c0, QB, axis=1)[..., None]
        return gc[:, :, 0] * o_c + gc[:, :, 1] * o_s + gc[:, :, 2] * o_w

    starts = jnp.arange(S // QB) * QB
    o = lax.map(chunk, starts)
    o = jnp.moveaxis(o, 0, 1).reshape(B, S, MIX_WIDTH)
    return o, q_mem


def _wkv7_scan(r, w, k, v, a, b):
    B, S, H, N = r.shape

    def step(state, inp):
        r_t, w_t, k_t, v_t, a_t, b_t = inp
        sa = jnp.einsum('bhvk,bhk->bhv', state, a_t)
        state = state * w_t[:, :, None, :] + sa[..., None] * b_t[:, :, None, :] \
            + v_t[..., None] * k_t[:, :, None, :]
        return state, jnp.einsum('bhvk,bhk->bhv', state, r_t)

    xs = tuple(jnp.moveaxis(t, 1, 0) for t in (r, w, k, v, a, b))
    s0 = jnp.zeros((B, H, N, N), jnp.float32)
    _, y = lax.scan(step, s0, xs)
    return jnp.moveaxis(y, 0, 1)


def _rwkv7_mixer(h, w_in, mu, w0, w2, a0, a2, g2, k_k, k_a, r_k, lnx_w, lnx_b):
    B, S, _ = h.shape
    H, N = MIX_HEADS, HEAD_DIM
    proj = h @ w_in
    z, q_mem = proj[..., :RW_SHIFT], proj[..., RW_SHIFT:]
    z_prev = jnp.pad(z, ((0, 0), (1, 0), (0, 0)))[:, :-1]
    z = z + (z_prev - z) * mu
    r, k, v, zw, za, zg = _split(z, [MIX_WIDTH] * 3 + [DECAY_LORA, ICLR_LORA, GATE_LORA])
    w_log = -jax.nn.softplus(-(w0 + jnp.tanh(zw) @ w2)) - 0.5
    decay = jnp.exp(-jnp.exp(w_log.astype(jnp.float32)))
    a = jax.nn.sigmoid(a0 + za @ a2)
    g = jax.nn.sigmoid(zg) @ g2
    kk = (k * k_k).astype(jnp.float32).reshape(B, S, H, N)
    kk = kk / jnp.maximum(jnp.sqrt(jnp.sum(kk * kk, axis=-1, keepdims=True)), 1e-12)
    k = k * (1 + (a - 1) * k_a)

    def heads(t):
        return t.astype(jnp.float32).reshape(B, S, H, N)

    rh, wh, kh, vh, ah = heads(r), heads(decay), heads(k), heads(v), heads(a)
    y = _wkv7_scan(rh, wh, kh, vh, -kk, kk * ah)
    mean = jnp.mean(y, axis=-1, keepdims=True)
    var = jnp.mean(jnp.square(y - mean), axis=-1, keepdims=True)
    y = ((y - mean) * lax.rsqrt(var + GN_EPS)).reshape(B, S, MIX_WIDTH)
    y = y * lnx_w.astype(jnp.float32) + lnx_b.astype(jnp.float32)
    bonus = jnp.sum(rh * kh * r_k.astype(jnp.float32), axis=-1, keepdims=True) * vh
    out = (y + bonus.reshape(B, S, MIX_WIDTH)) * g.astype(jnp.float32)
    return out.astype(h.dtype), q_mem


def _mem_attention(q_mem, mem_n, w_kv):
    B, S, _ = q_mem.shape
    M = mem_n.shape[1]
    k, v = jnp.split(mem_n @ w_kv, 2, axis=-1)
    q = q_mem.reshape(B, S, MEM_HEADS, HEAD_DIM)
    k = k.reshape(B, M, MEM_HEADS, HEAD_DIM)
    v = v.reshape(B, M, MEM_HEADS, HEAD_DIM)
    s = jnp.einsum('bshd,bmhd->bhsm', q, k).astype(jnp.float32) * (HEAD_DIM ** -0.5)
    p = jax.nn.softmax(s, axis=-1).astype(v.dtype)
    return jnp.einsum('bhsm,bmhd->bshd', p, v).reshape(B, S, MEM_WIDTH)


def _swiglu(h, w_in, w_out):
    gate, up = jnp.split(h @ w_in, 2, axis=-1)
    return (jax.nn.silu(gate) * up) @ w_out


def setup_inputs(seed: int = 0) -> dict:
    key = jax.random.key(seed)
    ks = jax.random.split(key, 32)
    f32 = jnp.float32

    def nrm(k, shape, scale):
        return jax.random.normal(k, shape, f32) * scale

    D = D_MODEL
    return {
        'x': nrm(ks[0], (BATCH, SEQ, D), 1.0),
        'mem': nrm(ks[1], (BATCH, N_MEM, D), 1.0),
        'norm1': 1.0 + nrm(ks[2], (DEPTH, D), 0.05),
        'norm_mem': 1.0 + nrm(ks[3], (DEPTH, D), 0.05),
        'w_mem_kv': nrm(ks[4], (DEPTH, D, 2 * MEM_WIDTH), D ** -0.5),
        'w_o': nrm(ks[5], (DEPTH, MIX_WIDTH + MEM_WIDTH, D), (MIX_WIDTH + MEM_WIDTH) ** -0.5),
        'norm2': 1.0 + nrm(ks[6], (DEPTH, D), 0.05),
        'w_ffn_in': nrm(ks[7], (DEPTH, D, 2 * FFN_HIDDEN), D ** -0.5),
        'w_ffn_out': nrm(ks[8], (DEPTH, FFN_HIDDEN, D), FFN_HIDDEN ** -0.5),
        'nsa_w_in': nrm(ks[9], (N_NSA, D, NSA_IN), D ** -0.5),
        'nsa_gate_b': nrm(ks[10], (N_NSA, 3 * MIX_HEADS), 0.1),
        'nsa_cmp_pos': nrm(ks[11], (N_NSA, 2, CMP_BLOCK, HEAD_DIM), 0.1),
        'nsa_cmp_w1': nrm(ks[12], (N_NSA, 2, CMP_BLOCK, HEAD_DIM, CMP_HIDDEN), (CMP_BLOCK * HEAD_DIM) ** -0.5),
        'nsa_cmp_w2': nrm(ks[13], (N_NSA, 2, CMP_HIDDEN, HEAD_DIM), CMP_HIDDEN ** -0.5),
        'rw_w_in': nrm(ks[14], (N_RWKV, D, RW_IN), D ** -0.5),
        'rw_mu': jax.random.uniform(ks[15], (N_RWKV, RW_SHIFT), f32),
        'rw_w0': jax.random.uniform(ks[16], (N_RWKV, MIX_WIDTH), f32, -5.0, 1.0),
        'rw_w2': nrm(ks[17], (N_RWKV, DECAY_LORA, MIX_WIDTH), 0.1 * DECAY_LORA ** -0.5),
        'rw_a0': nrm(ks[18], (N_RWKV, MIX_WIDTH), 0.5),
        'rw_a2': nrm(ks[19], (N_RWKV, ICLR_LORA, MIX_WIDTH), 0.3 * ICLR_LORA ** -0.5),
        'rw_g2': nrm(ks[20], (N_RWKV, GATE_LORA, MIX_WIDTH), GATE_LORA ** -0.5),
        'rw_k_k': 0.85 + nrm(ks[21], (N_RWKV, MIX_WIDTH), 0.05),
        'rw_k_a': 1.0 + nrm(ks[22], (N_RWKV, MIX_WIDTH), 0.05),
        'rw_r_k': nrm(ks[23], (N_RWKV, MIX_HEADS, HEAD_DIM), 0.1),
        'rw_lnx_w': 1.0 + nrm(ks[24], (N_RWKV, MIX_WIDTH), 0.05),
        'rw_lnx_b': nrm(ks[25], (N_RWKV, MIX_WIDTH), 0.02),
        'final_norm': 1.0 + nrm(ks[26], (D,), 0.05),
    }


def reference(x, mem, norm1, norm_mem, w_mem_kv, w_o, norm2, w_ffn_in, w_ffn_out,
              nsa_w_in, nsa_gate_b, nsa_cmp_pos, nsa_cmp_w1, nsa_cmp_w2,
              rw_w_in, rw_mu, rw_w0, rw_w2, rw_a0, rw_a2, rw_g2, rw_k_k, rw_k_a, rw_r_k,
              rw_lnx_w, rw_lnx_b, final_norm):
    for i in range(DEPTH):
        j = i // 2
        hn = _rmsnorm(x, norm1[i])
        mn = _rmsnorm(mem, norm_mem[i])
        if i % 2 == 0:
            mix, q_mem = _nsa_mixer(hn, nsa_w_in[j], nsa_gate_b[j], nsa_cmp_pos[j],
                                    nsa_cmp_w1[j], nsa_cmp_w2[j])
        else:
            mix, q_mem = _rwkv7_mixer(hn, rw_w_in[j], rw_mu[j], rw_w0[j], rw_w2[j], rw_a0[j],
                                      rw_a2[j], rw_g2[j], rw_k_k[j], rw_k_a[j], rw_r_k[j],
                                      rw_lnx_w[j], rw_lnx_b[j])
        cross = _mem_attention(q_mem, mn, w_mem_kv[i])
        x = x + jnp.concatenate([mix, cross], axis=-1) @ w_o[i]
        x = x + _swiglu(_rmsnorm(x, norm2[i]), w_ffn_in[i], w_ffn_out[i])
    return _rmsnorm(x, final_norm)
```

```python
import contextlib
import numpy as np
import concourse.bass as bass
import concourse.mybir as mybir
from concourse.bass_utils import run_bass_kernel_spmd

F32 = mybir.dt.float32
BF16 = mybir.dt.bfloat16
AF = mybir.ActivationFunctionType
ALU = mybir.AluOpType
AX = mybir.AxisListType

D = 1024
FFN_H = 2816
RMS_EPS = 1e-6
NDMA = 24


class Res:
    __slots__ = ("w", "r")

    def __init__(self):
        self.w = None
        self.r = {}


class Eng:
    def __init__(self, name, eng, sem):
        self.name, self.eng, self.sem = name, eng, sem
        self.count = 0
        self.seen = {}


class Tile:
    def __init__(self, t, parts=1):
        self.t = t
        self.parts = [Res() for _ in range(parts)]

    def __getitem__(self, idx):
        return self.t[idx]

    def p(self, i=0):
        return self.parts[i]

    @property
    def all(self):
        return list(self.parts)


class K:
    def __init__(self, nc, stack):
        self.nc = nc
        self.st = stack
        self.E = {}
        for nm, eng in (("pe", nc.tensor), ("dve", nc.vector), ("act", nc.scalar),
                        ("pool", nc.gpsimd), ("sp", nc.sync)):
            sem = stack.enter_context(nc.semaphore("sem_" + nm))
            self.E[nm] = Eng(nm, eng, sem)
        self.slots = []
        for i in range(NDMA):
            sem = stack.enter_context(nc.semaphore("dsem%d" % i))
            self.slots.append([("dma%d" % i), sem, 0])
        self.rr = 0
        self.ps = [Tile(stack.enter_context(nc.psum_tensor("ps%d" % i, [128, 512], F32)))
                   for i in range(8)]
        self.uid = 0

    def sb(self, ph, shape, dtype, name, parts=1):
        self.uid += 1
        return Tile(ph.enter_context(self.nc.sbuf_tensor("%s_%d" % (name, self.uid), shape, dtype)), parts)

    def dram(self, name, shape, dtype, kind="Internal", parts=1):
        return Tile(self.nc.dram_tensor(name, shape, dtype, kind=kind), parts)

    def _wait(self, E, tok):
        key, sem, val = tok
        if E.seen.get(key, 0) < val:
            E.eng.wait_ge(sem, val)
            E.seen[key] = val

    def _deps(self, E, reads, writes):
        pe = E.name == "pe"
        for r in reads:
            if r.w is not None and not (pe and r.w[0] == "pe"):
                self._wait(E, r.w)
        for w in writes:
            if w.w is not None and not (pe and w.w[0] == "pe"):
                self._wait(E, w.w)
            for key, tok in w.r.items():
                if key != E.name:
                    self._wait(E, tok)

    def _mark(self, tok, reads, writes):
        for r in reads:
            r.r[tok[0]] = tok
        for w in writes:
            w.w = tok
            w.r = {}

    def op(self, en, fn, reads=(), writes=()):
        E = self.E[en]
        self._deps(E, reads, writes)
        inst = fn(E.eng)
        E.count += 1
        inst.then_inc(E.sem, 1)
        tok = (E.name, E.sem, E.count)
        E.seen[E.name] = E.seen.get(E.name, 0)
        self._mark(tok, reads, writes)
        return tok

    def mm(self, out_res, mms, reads=()):
        E = self.E["pe"]
        self._deps(E, reads, [out_res])
        n = len(mms)
        inst = None
        for i, (o, l, r) in enumerate(mms):
            inst = E.eng.matmul(o, l, r, start=(i == 0), stop=(i == n - 1))
        E.count += 1
        inst.then_inc(E.sem, 1)
        tok = (E.name, E.sem, E.count)
        self._mark(tok, reads, [out_res])
        return tok

    def mmx(self, out_res, mms, reads=()):
        E = self.E["pe"]
        self._deps(E, reads, [out_res])
        inst = None
        for (o, l, r, st, sp) in mms:
            inst = E.eng.matmul(o, l, r, start=st, stop=sp, skip_group_check=True)
        E.count += 1
        inst.then_inc(E.sem, 1)
        tok = (E.name, E.sem, E.count)
        self._mark(tok, reads, [out_res])
        return tok

    def dma(self, qn, out_ap, in_ap, reads=(), writes=(), **kw):
        E = self.E[qn]
        self._deps(E, reads, writes)
        slot = self.slots[self.rr]
        self.rr = (self.rr + 1) % NDMA
        if slot[2] > 0:
            self._wait(E, (slot[0], slot[1], slot[2]))
        inst = E.eng.dma_start(out=out_ap, in_=in_ap, **kw)
        slot[2] += 16
        inst.then_inc(slot[1], 16)
        tok = (slot[0], slot[1], slot[2])
        self._mark(tok, reads, writes)
        return tok

    def barrier(self):
        toks = [(e.name, e.sem, e.count) for e in self.E.values() if e.count > 0]
        toks += [(s[0], s[1], s[2]) for s in self.slots if s[2] > 0]
        for E in self.E.values():
            for t in toks:
                if t[0] != E.name:
                    self._wait(E, t)


def make_consts(k, ph):
    nc = k.nc
    c = {}
    idf = k.sb(ph, [128, 128], F32, "identf")
    k.op("pool", lambda e: e.memset(idf[:], 1.0), writes=idf.all)
    k.op("pool", lambda e: e.affine_select(idf[:], idf[:], pattern=[[-1, 128]], compare_op=ALU.is_equal,
                                           fill=0.0, base=0, channel_multiplier=1),
         reads=idf.all, writes=idf.all)
    idb = k.sb(ph, [128, 128], BF16, "identb")
    k.op("dve", lambda e: e.tensor_copy(idb[:], idf[:]), reads=idf.all, writes=idb.all)
    c["idf"], c["idb"] = idf, idb
    return c


def load_vec_bcast(k, ph, dram_ap_1d, n, name, q="sp"):
    t = k.sb(ph, [128, n], F32, name)
    k.dma(q, t[:], dram_ap_1d.partition_broadcast(128), writes=t.all)
    return t


def load_vec_col(k, ph, dram_ap_1d, nch, name, q="sp"):
    t = k.sb(ph, [128, nch], F32, name)
    with k.nc.allow_non_contiguous_dma(reason="small vector load"):
        k.dma(q, t[:], dram_ap_1d.rearrange("(c p) -> p c", p=128), writes=t.all)
    return t


def norm_block(k, c, xts, gcol, hT, hparts, scr, psbase=0, col0=0, eps=RMS_EPS):
    n = len(xts)
    sq, ss, rstd = scr["sq"], scr["ss"], scr["rstd"]
    for i, x in enumerate(xts):
        k.op("act", lambda e: e.activation(sq[:], x[:], AF.Square, accum_out=ss[:, i:i + 1]),
             reads=x.all, writes=sq.all + ss.all)
    k.op("act", lambda e: e.activation(rstd[:, 0:n], ss[:, 0:n], AF.Ln, scale=1.0 / D, bias=scr["eps"][:, 0:1]),
         reads=ss.all + scr["eps"].all, writes=rstd.all)
    k.op("act", lambda e: e.activation(rstd[:, 0:n], rstd[:, 0:n], AF.Exp, scale=-0.5),
         reads=rstd.all, writes=rstd.all)
    for i, x in enumerate(xts):
        xn = scr["xn"][i % 2]
        k.op("dve", lambda e: e.tensor_scalar(xn[:], x[:], rstd[:, i:i + 1], None, op0=ALU.mult),
             reads=x.all + rstd.all, writes=xn.all)
        for g4 in range(2):
            ps = k.ps[psbase + (i % 2) * 2 + g4]
            for j in range(4):
                ch = g4 * 4 + j
                k.op("pe", lambda e: e.transpose(ps[:, j * 128:(j + 1) * 128],
                                                 xn[:, ch * 128:(ch + 1) * 128], c["idf"][:]),
                     reads=xn.all + c["idf"].all, writes=ps.all)
            for j in range(4):
                ch = g4 * 4 + j
                if j % 2 == 0:
                    k.op("dve", lambda e: e.tensor_scalar(hT[:, ch, col0 + i * 128:col0 + (i + 1) * 128],
                                                          ps[:, j * 128:(j + 1) * 128],
                                                          gcol[:, ch:ch + 1], None, op0=ALU.mult),
                         reads=ps.all + gcol.all, writes=[hparts[i]])
                else:
                    k.op("act", lambda e: e.activation(hT[:, ch, col0 + i * 128:col0 + (i + 1) * 128],
                                                       ps[:, j * 128:(j + 1) * 128],
                                                       AF.Copy, scale=gcol[:, ch:ch + 1]),
                         reads=ps.all + gcol.all, writes=[hparts[i]])


def norm_scratch(k, ph, n):
    scr = dict(sq=k.sb(ph, [128, 1024], F32, "sq"), ss=k.sb(ph, [128, n], F32, "ss"),
               rstd=k.sb(ph, [128, n], F32, "rstd"),
               xn=[k.sb(ph, [128, 1024], F32, "xn%d" % i) for i in range(2)],
               eps=k.sb(ph, [128, 1], F32, "epsc"))
    k.op("pool", lambda e: e.memset(scr["eps"][:], RMS_EPS), writes=scr["eps"].all)
    return scr


def phase_ffn(k, c, X, Xout, g2_ap, w_in_ap, w_out_ap, S):
    nc = k.nc
    TB = 1024
    nblk = S // TB
    with contextlib.ExitStack() as ph:
        gcol = load_vec_col(k, ph, g2_ap, 8, "g2col")
        wout = k.sb(ph, [128, 22, 1024], BF16, "wout")
        k.dma("pool", wout[:], w_out_ap.rearrange("(c p) m -> p c m", p=128), writes=wout.all)
        hT = k.sb(ph, [128, 8, TB], BF16, "hT", parts=8)
        actT = k.sb(ph, [128, 22, TB], BF16, "actT", parts=44)
        xt = [k.sb(ph, [128, 1024], F32, "xt%d" % i) for i in range(8)]
        scr = norm_scratch(k, ph, 8)
        wg = [k.sb(ph, [128, 8, 256], BF16, "wg%d" % i) for i in range(2)]
        wu = [k.sb(ph, [128, 8, 256], BF16, "wu%d" % i) for i in range(2)]
        sg = [k.sb(ph, [128, 512], F32, "sg%d" % i) for i in range(2)]
        xo = [k.sb(ph, [128, 1024], F32, "xo%d" % i) for i in range(2)]
        w_in_v = w_in_ap.rearrange("(c p) m -> p c m", p=128)
        for blk in range(nblk):
            t0 = blk * TB
            for tt in range(TB // 128):
                k.dma("sp", xt[tt][:], X[t0 + tt * 128:t0 + (tt + 1) * 128, :], writes=xt[tt].all)
            norm_block(k, c, xt, gcol, hT, hT.parts, scr, psbase=0)
            it = 0
            for j in range(11):
                b = j % 2
                k.dma("pool", wg[b][:], w_in_v[:, :, j * 256:(j + 1) * 256], writes=wg[b].all)
                k.dma("pool", wu[b][:], w_in_v[:, :, FFN_H + j * 256:FFN_H + (j + 1) * 256], writes=wu[b].all)
                for half in range(2):
                    mch = j * 2 + half
                    for th in range(TB // 512):
                        pg = k.ps[4 + (it % 2) * 2]
                        pu = k.ps[5 + (it % 2) * 2]
                        s = sg[it % 2]
                        it += 1
                        hreads = [hT.p(th * 4 + i) for i in range(4)]
                        k.mm(pg.p(), [(pg[:, :], wg[b][:, kc, half * 128:(half + 1) * 128],
                                       hT[:, kc, th * 512:(th + 1) * 512]) for kc in range(8)],
                             reads=hreads + wg[b].all)
                        k.mm(pu.p(), [(pu[:, :], wu[b][:, kc, half * 128:(half + 1) * 128],
                                       hT[:, kc, th * 512:(th + 1) * 512]) for kc in range(8)],
                             reads=hreads + wu[b].all)
                        k.op("act", lambda e: e.activation(s[:], pg[:, :], AF.Silu),
                             reads=pg.all, writes=s.all)
                        ar = actT.p(mch * 2 + th)
                        k.op("dve", lambda e: e.tensor_tensor(actT[:, mch, th * 512:(th + 1) * 512], s[:], pu[:, :],
                                                              op=ALU.mult),
                             reads=s.all + pu.all, writes=[ar])
            for tt in range(TB // 128):
                x = xt[tt]
                o = xo[tt % 2]
                for chh in range(2):
                    po = k.ps[(tt % 2) * 2 + chh]
                    th = tt // 4
                    k.mm(po.p(), [(po[:, :], actT[:, kc, tt * 128:(tt + 1) * 128],
                                   wout[:, kc, chh * 512:(chh + 1) * 512]) for kc in range(22)],
                         reads=[actT.p(kc * 2 + th) for kc in range(22)] + wout.all)
                    k.op("dve", lambda e: e.tensor_tensor(o[:, chh * 512:(chh + 1) * 512],
                                                          x[:, chh * 512:(chh + 1) * 512], po[:, :], op=ALU.add),
                         reads=x.all + po.all, writes=o.all)
                k.dma("sp", Xout[t0 + tt * 128:t0 + (tt + 1) * 128, :], o[:], reads=o.all)
        k.barrier()


def phase_final_norm(k, c, X, OUT, g_ap, S):
    with contextlib.ExitStack() as ph:
        gb = load_vec_bcast(k, ph, g_ap, 1024, "gfin")
        epsc = k.sb(ph, [128, 1], F32, "epsf")
        k.op("pool", lambda e: e.memset(epsc[:], RMS_EPS), writes=epsc.all)
        xt = [k.sb(ph, [128, 1024], F32, "fx%d" % i) for i in range(2)]
        sq = [k.sb(ph, [128, 1024], F32, "fsq%d" % i) for i in range(2)]
        ss = [k.sb(ph, [128, 1], F32, "fss%d" % i) for i in range(2)]
        xo = [k.sb(ph, [128, 1024], F32, "fo%d" % i) for i in range(2)]
        for tt in range(S // 128):
            b = tt % 2
            x, q, s, o = xt[b], sq[b], ss[b], xo[b]
            k.dma("sp", x[:], X[tt * 128:(tt + 1) * 128, :], writes=x.all)
            k.op("act", lambda e: e.activation(q[:], x[:], AF.Square, accum_out=s[:]),
                 reads=x.all, writes=q.all + s.all)
            k.op("act", lambda e: e.activation(s[:], s[:], AF.Ln, scale=1.0 / D, bias=epsc[:, 0:1]),
                 reads=s.all + epsc.all, writes=s.all)
            k.op("act", lambda e: e.activation(s[:], s[:], AF.Exp, scale=-0.5),
                 reads=s.all, writes=s.all)
            k.op("dve", lambda e: e.scalar_tensor_tensor(o[:], x[:], s[:, 0:1], gb[:], op0=ALU.mult, op1=ALU.mult),
                 reads=x.all + s.all + gb.all, writes=o.all)
            k.dma("pool", OUT[tt * 128:(tt + 1) * 128, :], o[:], reads=o.all)
        k.barrier()


def phase_proj(k, c, X, g1_ap, W_ap, outs, S):
    with contextlib.ExitStack() as ph:
        gcol = load_vec_col(k, ph, g1_ap, 8, "g1col")
        hT = k.sb(ph, [128, 8, S], BF16, "hnT", parts=S // 128)
        xt = [k.sb(ph, [128, 1024], F32, "pxt%d" % i) for i in range(8)]
        scr = norm_scratch(k, ph, 8)
        for blk in range(S // 1024):
            t0 = blk * 1024
            for tt in range(8):
                k.dma("sp", xt[tt][:], X[t0 + tt * 128:t0 + (tt + 1) * 128, :], writes=xt[tt].all)
            norm_block(k, c, xt, gcol, hT, hT.parts[blk * 8:(blk + 1) * 8], scr, psbase=0, col0=t0)
        wsl = [k.sb(ph, [128, 8, 512], BF16, "wsl%d" % i) for i in range(2)]
        stg = {}
        W_v = W_ap.rearrange("(c p) m -> p c m", p=128)
        si = 0
        ei = 0
        for o in outs:
            c0, n, mode, dst, dt = o["c0"], o["n"], o["mode"], o["dst"], o["dt"]
            for s0 in range(0, n, 512):
                sn = min(512, n - s0)
                w = wsl[si % 2]
                si += 1
                k.dma("pool", w[:, :, 0:sn], W_v[:, :, c0 + s0:c0 + s0 + sn], writes=w.all)
                if mode == "fm":
                    for m0 in range(0, sn, 128):
                        mn = min(128, sn - m0)
                        key = ("fm", dt)
                        if key not in stg:
                            stg[key] = [[k.sb(ph, [128, S], dt, "stgfm%d" % i) for i in range(2)], 0]
                        st = stg[key][0][stg[key][1] % 2]
                        stg[key][1] += 1
                        for tb in range(S // 512):
                            ps = k.ps[4 + ei % 4]
                            k.mm(ps.p(), [(ps[0:mn, :], w[:, kc, m0:m0 + mn], hT[:, kc, tb * 512:(tb + 1) * 512])
                                          for kc in range(8)],
                                 reads=hT.parts[tb * 4:(tb + 1) * 4] + w.all)
                            if ei % 2 == 0:
                                k.op("dve", lambda e: e.tensor_copy(st[0:mn, tb * 512:(tb + 1) * 512], ps[0:mn, :]),
                                     reads=ps.all, writes=st.all)
                            else:
                                k.op("act", lambda e: e.activation(st[0:mn, tb * 512:(tb + 1) * 512], ps[0:mn, :],
                                                                   AF.Copy),
                                     reads=ps.all, writes=st.all)
                            ei += 1
                        k.dma("sp", dst[s0 + m0:s0 + m0 + mn, :], st[0:mn, :], reads=st.all)
                else:
                    key = ("tm", dt)
                    if key not in stg:
                        stg[key] = [[k.sb(ph, [128, 512], dt, "stgtm%d" % i) for i in range(4)], 0]
                    for tt in range(S // 128):
                        st = stg[key][0][stg[key][1] % 4]
                        stg[key][1] += 1
                        ps = k.ps[4 + ei % 4]
                        k.mm(ps.p(), [(ps[:, 0:sn], hT[:, kc, tt * 128:(tt + 1) * 128], w[:, kc, 0:sn])
                                      for kc in range(8)],
                             reads=[hT.p(tt)] + w.all)
                        if ei % 2 == 0:
                            k.op("dve", lambda e: e.tensor_copy(st[:, 0:sn], ps[:, 0:sn]), reads=ps.all, writes=st.all)
                        else:
                            k.op("act", lambda e: e.activation(st[:, 0:sn], ps[:, 0:sn], AF.Copy),
                                 reads=ps.all, writes=st.all)
                        ei += 1
                        k.dma("sp", dst[tt * 128:(tt + 1) * 128, s0:s0 + sn], st[:, 0:sn], reads=st.all)
        k.barrier()


def phase_oproj(k, c, X, Xout, MIX, wo_ap, S):
    with contextlib.ExitStack() as ph:
        wo = k.sb(ph, [128, 8, 1024], BF16, "wo")
        k.dma("pool", wo[:], wo_ap.rearrange("(c p) m -> p c m", p=128), writes=wo.all)
        mt = [k.sb(ph, [128, 1024], F32, "mixt%d" % i) for i in range(2)]
        mT = [k.sb(ph, [128, 8, 128], BF16, "mixT%d" % i) for i in range(2)]
        xt = [k.sb(ph, [128, 1024], F32, "oxt%d" % i) for i in range(2)]
        xo = [k.sb(ph, [128, 1024], F32, "oxo%d" % i) for i in range(2)]
        for tt in range(S // 128):
            b = tt % 2
            k.dma("sp", mt[b][:], MIX[tt * 128:(tt + 1) * 128, :], writes=mt[b].all)
            k.dma("sp", xt[b][:], X[tt * 128:(tt + 1) * 128, :], writes=xt[b].all)
            for g4 in range(2):
                ps = k.ps[b * 2 + g4]
                for j in range(4):
                    ch = g4 * 4 + j
                    k.op("pe", lambda e: e.transpose(ps[:, j * 128:(j + 1) * 128],
                                                     mt[b][:, ch * 128:(ch + 1) * 128], c["idf"][:]),
                         reads=mt[b].all + c["idf"].all, writes=ps.all)
                if g4 == 0:
                    k.op("dve", lambda e: e.tensor_copy(mT[b][:, 0:4, :],
                                                        ps[:, :].rearrange("p (c t) -> p c t", c=4)),
                         reads=ps.all, writes=mT[b].all)
                else:
                    k.op("act", lambda e: e.activation(mT[b][:, 4:8, :],
                                                       ps[:, :].rearrange("p (c t) -> p c t", c=4), AF.Copy),
                         reads=ps.all, writes=mT[b].all)
            for chh in range(2):
                po = k.ps[4 + b * 2 + chh]
                k.mm(po.p(), [(po[:, :], mT[b][:, kc, :], wo[:, kc, chh * 512:(chh + 1) * 512]) for kc in range(8)],
                     reads=mT[b].all + wo.all)
                k.op("dve", lambda e: e.tensor_tensor(xo[b][:, chh * 512:(chh + 1) * 512],
                                                      xt[b][:, chh * 512:(chh + 1) * 512], po[:, :], op=ALU.add),
                     reads=xt[b].all + po.all, writes=xo[b].all)
            k.dma("sp", Xout[tt * 128:(tt + 1) * 128, :], xo[b][:], reads=xo[b].all)
        k.barrier()


def alibi_slopes12():
    def pow2(m):
        start = 2.0 ** (-8.0 / m)
        return [start ** (i + 1) for i in range(m)]
    s = pow2(8)
    s = s + pow2(16)[0::2][:4]
    return [float(np.float32(v)) for v in s]


class AttnBufs:
    def __init__(self, k, ph, n=3):
        self.n = n
        self.u = [k.sb(ph, [128, 512], F32, "au%d" % i) for i in range(n)]
        self.E = [k.sb(ph, [128, 512], BF16, "aE%d" % i) for i in range(n)]
        self.i = 0


def score_unit(k, ab, mms, reads, Dt=None, Dres=(), slope=0.0, bias=0.0):
    i = ab.i % ab.n
    ab.i += 1
    ps, u, E = k.ps[i], ab.u[i], ab.E[i]
    k.mm(ps.p(), [(ps[:, :], l, r) for (l, r) in mms], reads=reads)
    if Dt is not None:
        k.op("dve", lambda e: e.scalar_tensor_tensor(u[:], Dt, float(slope), ps[:, :], op0=ALU.mult, op1=ALU.add),
             reads=ps.all + list(Dres), writes=u.all)
        k.op("act", lambda e: e.activation(E[:], u[:], AF.Exp, scale=0.125, bias=float(bias)),
             reads=u.all, writes=E.all)
    else:
        k.op("act", lambda e: e.activation(E[:], ps[:, :], AF.Exp, scale=0.125), reads=ps.all, writes=E.all)
    return E


def pv_accum(k, acc, accv, E, rhs, rhs_res, first, last, nsub=4):
    mms = []
    for qs in range(nsub):
        mms.append((accv[:, qs, :], E[:, qs * 128:(qs + 1) * 128], rhs,
                    bool(first and qs == 0), bool(last and qs == nsub - 1)))
    k.mmx(acc.p(), mms, reads=E.all + list(rhs_res))


def phase_memattn(k, c, MEM, gm_ap, wkv_ap, QMT, MIX, S):
    with contextlib.ExitStack() as ph:
        gcol = load_vec_col(k, ph, gm_ap, 8, "gmcol")
        mt = [k.sb(ph, [128, 1024], F32, "memt%d" % i) for i in range(2)]
        scr = norm_scratch(k, ph, 2)
        mnT = k.sb(ph, [128, 8, 256], BF16, "mnT", parts=2)
        for i in range(2):
            k.dma("sp", mt[i][:], MEM[i * 128:(i + 1) * 128, :], writes=mt[i].all)
        norm_block(k, c, mt, gcol, mnT, mnT.parts, scr, psbase=4)
        wkv = k.sb(ph, [128, 8, 512], BF16, "wkv")
        k.dma("pool", wkv[:], wkv_ap.rearrange("(c p) m -> p c m", p=128), writes=wkv.all)
        kmT = k.sb(ph, [64, 4, 256], BF16, "kmT")
        vm = k.sb(ph, [128, 2, 4, 65], BF16, "vmaug")
        k.op("pool", lambda e: e.memset(vm[:], 1.0), writes=vm.all)
        for h in range(4):
            ps = k.ps[4 + h % 2]
            k.mm(ps.p(), [(ps[0:64, 0:256], wkv[:, kc, h * 64:(h + 1) * 64], mnT[:, kc, :]) for kc in range(8)],
                 reads=mnT.all + wkv.all)
            k.op("dve", lambda e: e.tensor_copy(kmT[:, h, :], ps[0:64, 0:256]), reads=ps.all, writes=kmT.all)
        for m2 in range(2):
            ps = k.ps[6 + m2]
            k.mm(ps.p(), [(ps[:, 0:256], mnT[:, kc, m2 * 128:(m2 + 1) * 128], wkv[:, kc, 256:512])
                          for kc in range(8)], reads=mnT.all + wkv.all)
            k.op("dve", lambda e: e.tensor_copy(vm[:, m2, :, 0:64],
                                                ps[:, 0:256].rearrange("p (h d) -> p h d", h=4)),
                 reads=ps.all, writes=vm.all)
        ab = AttnBufs(k, ph)
        qT = [k.sb(ph, [64, S], BF16, "qmT%d" % i) for i in range(2)]
        cross = k.sb(ph, [128, S // 128, 256], F32, "crossall")
        rec = k.sb(ph, [128, 4, 1], F32, "mrec")
        ai = 0
        for h in range(4):
            q = qT[h % 2]
            k.dma("sp", q[:], QMT[h * 64:(h + 1) * 64, :], writes=q.all)
            for qt in range(S // 512):
                acc = k.ps[3 + ai % 2]
                ai += 1
                accv = acc[:, 0:260].rearrange("p (s w) -> p s w", s=4)
                for m2 in range(2):
                    E = score_unit(k, ab, [(kmT[:, h, m2 * 128:(m2 + 1) * 128], q[:, qt * 512:(qt + 1) * 512])],
                                   reads=kmT.all + q.all)
                    pv_accum(k, acc, accv, E, vm[:, m2, h, :], vm.all, first=(m2 == 0), last=(m2 == 1))
                k.op("dve", lambda e: e.reciprocal(rec[:], accv[:, :, 64:65]), reads=acc.all, writes=rec.all)
                k.op("dve", lambda e: e.tensor_tensor(cross[:, qt * 4:(qt + 1) * 4, h * 64:(h + 1) * 64],
                                                      accv[:, :, 0:64], rec[:].to_broadcast([128, 4, 64]),
                                                      op=ALU.mult),
                     reads=acc.all + rec.all, writes=cross.all)
        k.dma("sp", MIX[:, 768:1024].rearrange("(t p) c -> p t c", p=128), cross[:], reads=cross.all)
        k.barrier()


NEGBIG = -1.0e9


def phase_nsa(k, c, sc, gate_b_ap, pos_ap, w1_ap, w2_ap, MIX, S, cache=None):
    QT, KCT, VCT, KST, KWT, VS, VW, GL = (sc[n] for n in ("QT", "KCT", "VCT", "KST", "KWT", "VS", "VW", "GL"))
    slopes = alibi_slopes12()
    NQT = S // 512
    NKT = S // 128
    NTT = S // 128
    NCMP = (S - 32) // 16 + 1
    NNT = (NCMP + 127) // 128
    nc = k.nc
    with contextlib.ExitStack() as ph:
        cache = {} if cache is None else cache
        build = "done" not in cache

        def ctile(name, shape, dt):
            t = k.sb(ph, shape, dt, name)
            if not build:
                k.dma("sp", t[:], cache[name], writes=t.all)
            return t

        def cstore(name, t, shape, dt):
            cache[name] = nc.dram_tensor("nsac_" + name, list(shape), dt, kind="Internal").ap()
            k.dma("sp", cache[name], t[:], reads=t.all)

        Dwin = [ctile("Dwin%d" % j, [128, 512], F32) for j in range(8)]
        Dgen = ctile("Dgen", [128, 512], F32)
        Dcm = {}
        for qt in range(NQT):
            for nt in range(NNT):
                base = 512 * qt - 2048 * nt - 31
                if base + 511 < 0:
                    continue
                Dcm[(qt, nt)] = ctile("Dcm%d_%d" % (qt, nt), [128, 512], F32)
        ovl = ctile("ovl", [128, NNT, 64], F32)
        keepG = ctile("keepG", [128, 128], F32)
        addG = ctile("addG", [128, 128], F32)
        Ex = ctile("Ex", [64, NKT, 128], BF16)
        if build:
            for j in range(8):
                koff = 128 * j - 512
                t = Dwin[j]
                k.op("pool", lambda e: e.iota(t[:], pattern=[[1, 512]], base=-koff, channel_multiplier=-1,
                                              allow_small_or_imprecise_dtypes=True), writes=t.all)
                k.op("pool", lambda e: e.tensor_scalar(t[:], t[:], -8.0, None, op0=ALU.mult), reads=t.all, writes=t.all)
                k.op("pool", lambda e: e.affine_select(t[:], t[:], pattern=[[1, 512]], compare_op=ALU.is_ge,
                                                       fill=NEGBIG, base=-koff, channel_multiplier=-1),
                     reads=t.all, writes=t.all)
                k.op("pool", lambda e: e.affine_select(t[:], t[:], pattern=[[-1, 512]], compare_op=ALU.is_ge,
                                                       fill=NEGBIG, base=511 + koff, channel_multiplier=1),
                     reads=t.all, writes=t.all)
            k.op("pool", lambda e: e.iota(Dgen[:], pattern=[[1, 512]], base=0, channel_multiplier=-1,
                                          allow_small_or_imprecise_dtypes=True), writes=Dgen.all)
            k.op("pool", lambda e: e.tensor_scalar(Dgen[:], Dgen[:], -8.0, None, op0=ALU.mult),
                 reads=Dgen.all, writes=Dgen.all)
            for (qt, nt), t in Dcm.items():
                base = 512 * qt - 2048 * nt - 31
                k.op("pool", lambda e: e.iota(t[:], pattern=[[1, 512]], base=base, channel_multiplier=-16,
                                              allow_small_or_imprecise_dtypes=True), writes=t.all)
                k.op("pool", lambda e: e.tensor_scalar(t[:], t[:], -8.0, None, op0=ALU.mult),
                     reads=t.all, writes=t.all)
                k.op("pool", lambda e: e.affine_select(t[:], t[:], pattern=[[1, 512]], compare_op=ALU.is_ge,
                                                       fill=NEGBIG, base=base, channel_multiplier=-16),
                     reads=t.all, writes=t.all)
            k.op("pool", lambda e: e.memset(ovl[:], 1.0), writes=ovl.all)
            k.op("pool", lambda e: e.affine_select(ovl[:], ovl[:], pattern=[[128, NNT], [-4, 64]], compare_op=ALU.is_ge,
                                                   fill=0.0, base=1, channel_multiplier=1), reads=ovl.all, writes=ovl.all)
            k.op("pool", lambda e: e.affine_select(ovl[:], ovl[:], pattern=[[-128, NNT], [4, 64]], compare_op=ALU.is_ge,
                                                   fill=0.0, base=3, channel_multiplier=-1), reads=ovl.all, writes=ovl.all)
            k.op("pool", lambda e: e.memset(keepG[:], 1.0), writes=keepG.all)
            k.op("pool", lambda e: e.memset(addG[:], 1.0e4), writes=addG.all)
            for hp in range(2):
                rows = slice(hp * 64, hp * 64 + 64)
                k.op("pool", lambda e: e.affine_select(keepG[rows, :], keepG[rows, :], pattern=[[-1, 128]],
                                                       compare_op=ALU.is_ge, fill=0.0, base=62 + hp,
                                                       channel_multiplier=0), reads=keepG.all, writes=keepG.all)
                k.op("pool", lambda e: e.affine_select(addG[rows, :], addG[rows, :], pattern=[[1, 128]],
                                                       compare_op=ALU.is_ge, fill=0.0, base=-63 - hp,
                                                       channel_multiplier=0), reads=addG.all, writes=addG.all)
                k.op("pool", lambda e: e.affine_select(addG[rows, :], addG[rows, :], pattern=[[-1, 128]],
                                                       compare_op=ALU.is_ge, fill=0.0, base=64 + hp,
                                                       channel_multiplier=0), reads=addG.all, writes=addG.all)
            tmpst = contextlib.ExitStack()
            Exf = k.sb(tmpst, [64, NKT * 128], F32, "Exf")
            k.op("pool", lambda e: e.memset(Exf[:], 1.0), writes=Exf.all)
            k.op("pool", lambda e: e.affine_select(Exf[:].rearrange("p (a b c) -> p a b c", b=2, c=64),
                                                   Exf[:].rearrange("p (a b c) -> p a b c", b=2, c=64),
                                                   pattern=[[-2, NKT], [-1, 2], [0, 64]], compare_op=ALU.is_equal,
                                                   fill=0.0, base=0, channel_multiplier=1),
                 reads=Exf.all, writes=Exf.all)
            k.op("dve", lambda e: e.tensor_copy(Ex[:], Exf[:].rearrange("p (a c) -> p a c", c=128)),
                 reads=Exf.all, writes=Ex.all)
            k.barrier()
            tmpst.close()
            for j in range(8):
                cstore("Dwin%d" % j, Dwin[j], [128, 512], F32)
            cstore("Dgen", Dgen, [128, 512], F32)
            for (qt, nt), t in Dcm.items():
                cstore("Dcm%d_%d" % (qt, nt), t, [128, 512], F32)
            cstore("ovl", ovl, [128, NNT, 64], F32)
            cstore("keepG", keepG, [128, 128], F32)
            cstore("addG", addG, [128, 128], F32)
            cstore("Ex", Ex, [64, NKT, 128], BF16)
            cache["done"] = True
        gates = k.sb(ph, [128, NTT, 36], F32, "gates")
        gb = load_vec_bcast(k, ph, gate_b_ap, 36, "gateb")
        k.dma("sp", gates[:], GL.rearrange("(t p) c -> p t c", p=128), writes=gates.all)
        k.op("dve", lambda e: e.tensor_tensor(gates[:], gates[:], gb[:].rearrange("p (o c) -> p o c", o=1)
                                              .to_broadcast([128, NTT, 36]), op=ALU.add),
             reads=gates.all + gb.all, writes=gates.all)
        k.op("act", lambda e: e.activation(gates[:], gates[:], AF.Exp, scale=-1.0), reads=gates.all, writes=gates.all)
        k.op("dve", lambda e: e.tensor_scalar(gates[:], gates[:], 1.0, None, op0=ALU.add),
             reads=gates.all, writes=gates.all)
        k.op("dve", lambda e: e.reciprocal(gates[:], gates[:]), reads=gates.all, writes=gates.all)
        w1 = k.sb(ph, [64, 2, 32, 128], BF16, "cw1")
        w2 = k.sb(ph, [128, 2, 64], BF16, "cw2")
        posT = k.sb(ph, [64, 2, 32], BF16, "cposT")
        for kv in range(2):
            k.dma("pool", w1[:, kv, :, :], w1_ap[kv].rearrange("l d e -> d l e"), writes=w1.all)
            k.dma("pool", w2[:, kv, :], w2_ap[kv], writes=w2.all)
        with nc.allow_non_contiguous_dma(reason="tiny pos table"):
            k.dma("pool", posT[:], pos_ap.rearrange("v l d -> d v l"), writes=posT.all)
        cbias = k.sb(ph, [128, 2], F32, "cbias")
        for kv in range(2):
            ps = k.ps[6]
            k.mm(ps.p(), [(ps[:, 0:1], w1[:, kv, l, :], posT[:, kv, l:l + 1]) for l in range(32)],
                 reads=w1.all + posT.all)
            k.op("dve", lambda e: e.tensor_copy(cbias[:, kv:kv + 1], ps[:, 0:1]), reads=ps.all, writes=cbias.all)
        qT = k.sb(ph, [64, 3, S], BF16, "qT")
        ksT = k.sb(ph, [64, S], BF16, "ksT")
        kwT = k.sb(ph, [64, S], BF16, "kwT")
        cT = [k.sb(ph, [64, S], BF16, "cT%d" % i) for i in range(2)]
        vs = k.sb(ph, [128, NKT, 65], BF16, "vsaug")
        vw = k.sb(ph, [128, NKT, 65], BF16, "vwaug")
        k.op("pool", lambda e: e.memset(vs[:], 1.0), writes=vs.all)
        k.op("pool", lambda e: e.memset(vw[:], 1.0), writes=vw.all)
        kcbT = k.sb(ph, [64, NNT * 128], BF16, "kcbT")
        vcb = k.sb(ph, [128, NNT, 129], BF16, "vcbaug")
        k.op("pool", lambda e: e.memset(kcbT[:], 0.0), writes=kcbT.all)
        k.op("pool", lambda e: e.memset(vcb[:], 0.0), writes=vcb.all)
        k.op("pool", lambda e: e.memset(vcb[:, :, 128:129], 1.0), writes=vcb.all)
        k.op("dve", lambda e: e.tensor_copy(vcb[:, :, 64:128], ovl[:]), reads=ovl.all, writes=vcb.all)
        hx = k.sb(ph, [128, 256], F32, "hx")
        hy = k.sb(ph, [128, 256], F32, "hy")
        hact = k.sb(ph, [128, 256], BF16, "hact")
        k.op("pool", lambda e: e.memset(hact[:], 0.0), writes=hact.all)
        ab = AttnBufs(k, ph)
        oc = k.sb(ph, [128, 4, 3, 64], F32, "oc")
        imp = k.sb(ph, [128, 4, 64], F32, "imp")
        rec = k.sb(ph, [128, 4, 1], F32, "nrec")
        gs = k.sb(ph, [128, 4, 1], F32, "ngs")
        scq = k.sb(ph, [128, 4, 64], F32, "scq")
        wk = k.sb(ph, [128, 4, 64], F32, "wk")
        m8 = k.sb(ph, [128, 8], F32, "m8")
        nmT = k.sb(ph, [64, 512], BF16, "nmT")
        t1 = k.sb(ph, [128, 4, 64], F32, "t1")
        mixt = [k.sb(ph, [128, 4, 192], F32, "mixt%d" % i) for i in range(2)]
        mi = 0
        NC1 = NCMP
        for g in range(4):
            for r in range(3):
                k.dma("sp", qT[:, r, :], QT[(g * 3 + r) * 64:(g * 3 + r + 1) * 64, :], writes=qT.all)
            k.dma("sp", ksT[:], KST[g * 64:(g + 1) * 64, :], writes=ksT.all)
            k.dma("sp", kwT[:], KWT[g * 64:(g + 1) * 64, :], writes=kwT.all)
            k.dma("sp", cT[0][:], KCT[g * 64:(g + 1) * 64, :], writes=cT[0].all)
            k.dma("sp", cT[1][:], VCT[g * 64:(g + 1) * 64, :], writes=cT[1].all)
            with nc.allow_non_contiguous_dma(reason="v head slice"):
                k.dma("sp", vs[:, :, 0:64], VS[:, g * 64:(g + 1) * 64].rearrange("(t p) d -> p t d", p=128),
                      writes=vs.all)
                k.dma("sp", vw[:, :, 0:64], VW[:, g * 64:(g + 1) * 64].rearrange("(t p) d -> p t d", p=128),
                      writes=vw.all)
            for kv in range(2):
                src3 = cT[kv][:].rearrange("p (n s) -> p n s", s=16)
                ps = k.ps[6]
                k.mm(ps.p(), [(ps[:, 0:NC1], w1[:, kv, l, :], src3[:, (l // 16):(l // 16) + NC1, l % 16])
                              for l in range(32)], reads=w1.all + cT[kv].all)
                k.op("dve", lambda e: e.tensor_scalar(hx[:, 0:NC1], ps[:, 0:NC1], cbias[:, kv:kv + 1], None,
                                                      op0=ALU.add), reads=ps.all + cbias.all, writes=hx.all)
                k.op("dve", lambda e: e.tensor_tensor(hy[:, 0:NC1], hx[:, 0:NC1], hx[:, 0:NC1], op=ALU.mult),
                     reads=hx.all, writes=hy.all)
                k.op("dve", lambda e: e.tensor_scalar(hy[:, 0:NC1], hy[:, 0:NC1], 0.044715, 1.0,
                                                      op0=ALU.mult, op1=ALU.add), reads=hy.all, writes=hy.all)
                k.op("dve", lambda e: e.tensor_tensor(hy[:, 0:NC1], hy[:, 0:NC1], hx[:, 0:NC1], op=ALU.mult),
                     reads=hy.all + hx.all, writes=hy.all)
                k.op("act", lambda e: e.activation(hy[:, 0:NC1], hy[:, 0:NC1], AF.Exp, scale=-1.5957691216057308),
                     reads=hy.all, writes=hy.all)
                k.op("dve", lambda e: e.tensor_scalar(hy[:, 0:NC1], hy[:, 0:NC1], 1.0, None, op0=ALU.add),
                     reads=hy.all, writes=hy.all)
                k.op("dve", lambda e: e.reciprocal(hy[:, 0:NC1], hy[:, 0:NC1]), reads=hy.all, writes=hy.all)
                k.op("dve", lambda e: e.tensor_tensor(hact[:, 0:NC1], hx[:, 0:NC1], hy[:, 0:NC1], op=ALU.mult),
                     reads=hy.all + hx.all, writes=hact.all)
                if kv == 0:
                    ps2 = k.ps[7]
                    k.mm(ps2.p(), [(ps2[0:64, 0:NC1], w2[:, 0, :], hact[:, 0:NC1])], reads=w2.all + hact.all)
                    k.op("dve", lambda e: e.tensor_copy(kcbT[:, 0:NC1], ps2[0:64, 0:NC1]),
                         reads=ps2.all, writes=kcbT.all)
                else:
                    for nt in range(NNT):
                        nn = min(128, NC1 - nt * 128)
                        ps2 = k.ps[7]
                        k.mm(ps2.p(), [(ps2[0:nn, 0:64], hact[:, nt * 128:nt * 128 + nn], w2[:, 1, :])],
                             reads=w2.all + hact.all)
                        k.op("dve", lambda e: e.tensor_copy(vcb[0:nn, nt, 0:64], ps2[0:nn, 0:64]),
                             reads=ps2.all, writes=vcb.all)
            for qt in range(NQT):
                q0 = qt * 512
                for r in range(3):
                    h = g * 3 + r
                    nts = [nt for nt in range(NNT) if (qt, nt) in Dcm]
                    accA, accB = k.ps[3], k.ps[4]
                    vA = accA[:, 0:258].rearrange("p (s w) -> p s w", s=2)
                    vB = accB[:, 0:258].rearrange("p (s w) -> p s w", s=2)
                    for ii, nt in enumerate(nts):
                        Dt = Dcm[(qt, nt)]
                        E = score_unit(k, ab, [(kcbT[:, nt * 128:(nt + 1) * 128], qT[:, r, q0:q0 + 512])],
                                       reads=kcbT.all + qT.all, Dt=Dt[:], Dres=Dt.all, slope=slopes[h], bias=0.0)
                        first, last = ii == 0, ii == len(nts) - 1
                        k.mmx(accA.p(), [(vA[:, qs, :], E[:, qs * 128:(qs + 1) * 128], vcb[:, nt, :],
                                          bool(first and qs == 0), bool(last and qs == 1)) for qs in range(2)],
                              reads=E.all + vcb.all)
                        k.mmx(accB.p(), [(vB[:, qs, :], E[:, (qs + 2) * 128:(qs + 3) * 128], vcb[:, nt, :],
                                          bool(first and qs == 0), bool(last and qs == 1)) for qs in range(2)],
                              reads=E.all + vcb.all)
                    for half, (acc, av) in enumerate(((accA, vA), (accB, vB))):
                        sl = slice(half * 2, half * 2 + 2)
                        k.op("dve", lambda e: e.tensor_scalar(rec[:, sl, :], av[:, :, 128:129], 1e-30, None,
                                                              op0=ALU.add), reads=acc.all, writes=rec.all)
                        k.op("dve", lambda e: e.reciprocal(rec[:, sl, :], rec[:, sl, :]), reads=rec.all, writes=rec.all)
                        k.op("dve", lambda e: e.tensor_tensor(oc[:, sl, r, :], av[:, :, 0:64],
                                                              rec[:, sl, :].to_broadcast([128, 2, 64]), op=ALU.mult),
                             reads=acc.all + rec.all, writes=oc.all)
                        if r == 0:
                            k.op("dve", lambda e: e.tensor_tensor(imp[:, sl, :], av[:, :, 64:128],
                                                                  rec[:, sl, :].to_broadcast([128, 2, 64]),
                                                                  op=ALU.mult),
                                 reads=acc.all + rec.all, writes=imp.all)
                        else:
                            k.op("dve", lambda e: e.tensor_tensor(wk[:, sl, :], av[:, :, 64:128],
                                                                  rec[:, sl, :].to_broadcast([128, 2, 64]),
                                                                  op=ALU.mult),
                                 reads=acc.all + rec.all, writes=wk.all)
                            k.op("dve", lambda e: e.tensor_tensor(imp[:, sl, :], imp[:, sl, :], wk[:, sl, :],
                                                                  op=ALU.add),
                                 reads=imp.all + wk.all, writes=imp.all)
                pst = k.ps[5]
                for qs in range(4):
                    tt = qt * 4 + qs
                    lo = 64 - 2 * tt
                    k.op("dve", lambda e: e.scalar_tensor_tensor(scq[:, qs, :], imp[:, qs, :], 1.0,
                                                                 keepG[:, lo:lo + 64], op0=ALU.add, op1=ALU.mult),
                         reads=imp.all + keepG.all, writes=scq.all)
                    k.op("dve", lambda e: e.tensor_tensor(scq[:, qs, :], scq[:, qs, :], addG[:, lo:lo + 64],
                                                          op=ALU.add), reads=scq.all + addG.all, writes=scq.all)
                    k.op("dve", lambda e: e.memset(scq[:, qs, 0:1], 1.0e4), reads=scq.all, writes=scq.all)
                    k.op("dve", lambda e: e.max(m8[:], scq[:, qs, :]), reads=scq.all, writes=m8.all)
                    k.op("dve", lambda e: e.match_replace(wk[:, qs, :], m8[:], scq[:, qs, :], 0.0),
                         reads=scq.all + m8.all, writes=wk.all)
                    k.op("dve", lambda e: e.max(m8[:], wk[:, qs, :]), reads=wk.all, writes=m8.all)
                    k.op("dve", lambda e: e.match_replace(wk[:, qs, :], m8[:], wk[:, qs, :], 0.0),
                         reads=wk.all + m8.all, writes=wk.all)
                    k.op("dve", lambda e: e.tensor_tensor(wk[:, qs, :], scq[:, qs, :], wk[:, qs, :], op=ALU.subtract),
                         reads=wk.all + scq.all, writes=wk.all)
                    k.op("dve", lambda e: e.tensor_scalar(wk[:, qs, :], wk[:, qs, :], 1.0, 1.0,
                                                          op0=ALU.min, op1=ALU.subtract), reads=wk.all, writes=wk.all)
                    k.op("pe", lambda e: e.transpose(pst[0:64, qs * 128:(qs + 1) * 128], wk[:, qs, :], c["idf"][:]),
                         reads=wk.all + c["idf"].all, writes=pst.all)
                k.op("act", lambda e: e.activation(nmT[:], pst[0:64, :], AF.Copy, scale=30000.0),
                     reads=pst.all, writes=nmT.all)
                mx = mixt[mi % 2]
                mi += 1
                for r in range(3):
                    h = g * 3 + r
                    sl_h = slopes[h]
                    accS, accW = k.ps[6], k.ps[7]
                    vS = accS[:, 0:260].rearrange("p (s w) -> p s w", s=4)
                    vW = accW[:, 0:260].rearrange("p (s w) -> p s w", s=4)
                    nkt = 4 * qt + 4
                    for kt in range(nkt):
                        if kt >= 4 * qt:
                            Dt, bias = Dwin[4 + kt - 4 * qt], 0.0
                        else:
                            Dt, bias = Dgen, -sl_h * (q0 - kt * 128)
                        E = score_unit(k, ab, [(ksT[:, kt * 128:(kt + 1) * 128], qT[:, r, q0:q0 + 512]),
                                               (Ex[:, kt, :], nmT[:, :])],
                                       reads=ksT.all + qT.all + Ex.all + nmT.all, Dt=Dt[:], Dres=Dt.all,
                                       slope=sl_h, bias=bias)
                        pv_accum(k, accS, vS, E, vs[:, kt, :], vs.all, first=(kt == 0), last=(kt == nkt - 1))
                    kts = [kt for kt in range(4 * qt - 4, 4 * qt + 4) if kt >= 0]
                    for ii, kt in enumerate(kts):
                        Dt = Dwin[kt - 4 * qt + 4]
                        E = score_unit(k, ab, [(kwT[:, kt * 128:(kt + 1) * 128], qT[:, r, q0:q0 + 512])],
                                       reads=kwT.all + qT.all, Dt=Dt[:], Dres=Dt.all, slope=sl_h, bias=0.0)
                        pv_accum(k, accW, vW, E, vw[:, kt, :], vw.all, first=(ii == 0), last=(ii == len(kts) - 1))
                    gsl = slice(qt * 4, qt * 4 + 4)
                    k.op("dve", lambda e: e.reciprocal(rec[:], vS[:, :, 64:65]), reads=accS.all, writes=rec.all)
                    k.op("dve", lambda e: e.tensor_tensor(gs[:], rec[:], gates[:, gsl, 12 + h:13 + h], op=ALU.mult),
                         reads=rec.all + gates.all, writes=gs.all)
                    k.op("dve", lambda e: e.tensor_tensor(t1[:], vS[:, :, 0:64], gs[:].to_broadcast([128, 4, 64]),
                                                          op=ALU.mult), reads=accS.all + gs.all, writes=t1.all)
                    k.op("dve", lambda e: e.reciprocal(rec[:], vW[:, :, 64:65]), reads=accW.all, writes=rec.all)
                    k.op("dve", lambda e: e.tensor_tensor(gs[:], rec[:], gates[:, gsl, 24 + h:25 + h], op=ALU.mult),
                         reads=rec.all + gates.all, writes=gs.all)
                    k.op("dve", lambda e: e.tensor_tensor(wk[:], vW[:, :, 0:64], gs[:].to_broadcast([128, 4, 64]),
                                                          op=ALU.mult), reads=accW.all + gs.all, writes=wk.all)
                    k.op("dve", lambda e: e.tensor_tensor(t1[:], t1[:], wk[:], op=ALU.add),
                         reads=t1.all + wk.all, writes=t1.all)
                    k.op("dve", lambda e: e.tensor_tensor(wk[:], oc[:, :, r, :],
                                                          gates[:, gsl, h:h + 1].to_broadcast([128, 4, 64]),
                                                          op=ALU.mult), reads=oc.all + gates.all, writes=wk.all)
                    k.op("dve", lambda e: e.tensor_tensor(mx[:, :, r * 64:(r + 1) * 64], t1[:], wk[:], op=ALU.add),
                         reads=t1.all + wk.all, writes=mx.all)
                with nc.allow_non_contiguous_dma(reason="mix slice"):
                    k.dma("sp", MIX[q0:q0 + 512, g * 192:(g + 1) * 192].rearrange("(s p) c -> p s c", p=128),
                          mx[:], reads=mx.all)
        k.barrier()


GN_EPS = 64e-5
TBK = 256


def phase_rwkv(k, c, ZT, prm, MIX, S, dbg=99):
    nc = k.nc
    NB = S // TBK if dbg >= 99 else 1
    NCH = TBK // 64
    NPR = TBK // 128
    W6 = [128, 6, TBK]

    def V3(t):
        return t[:].rearrange("p (c t) -> p c t", c=6)

    with contextlib.ExitStack() as ph:
        def colvec(name, ap, n=6):
            return load_vec_col(k, ph, ap, n, name)
        mu_c = k.sb(ph, [128, 21], F32, "mu_c")
        with nc.allow_non_contiguous_dma(reason="small vector load"):
            k.dma("sp", mu_c[:, 0:20], prm["mu"][0:2560].rearrange("(c p) -> p c", p=128), writes=mu_c.all)
            k.dma("sp", mu_c[0:32, 20:21], prm["mu"][2560:2592].rearrange("(c p) -> p c", p=32), writes=mu_c.all)
        w0n = colvec("w0n", prm["w0"])
        a0n = colvec("a0n", prm["a0"])
        kk_c = colvec("kk_c", prm["k_k"])
        ka_c = colvec("ka_c", prm["k_a"])
        rk_c = colvec("rk_c", prm["r_k"].rearrange("h n -> (h n)"))
        omka = k.sb(ph, [128, 6], F32, "omka")
        k.op("dve", lambda e: e.tensor_scalar(omka[:], ka_c[:], -1.0, 1.0, op0=ALU.mult, op1=ALU.add),
             reads=ka_c.all, writes=omka.all)
        k.op("dve", lambda e: e.tensor_scalar(w0n[:], w0n[:], -1.0, None, op0=ALU.mult), reads=w0n.all, writes=w0n.all)
        k.op("dve", lambda e: e.tensor_scalar(a0n[:], a0n[:], -1.0, None, op0=ALU.mult), reads=a0n.all, writes=a0n.all)
        lnw = load_vec_bcast(k, ph, prm["lnx_w"], 768, "lnw")
        lnb = load_vec_bcast(k, ph, prm["lnx_b"], 768, "lnb")
        w2a2 = k.sb(ph, [128, 768], BF16, "w2a2")
        k.dma("pool", w2a2[0:64, :], prm["w2"], writes=w2a2.all)
        k.dma("pool", w2a2[64:128, :], prm["a2"], writes=w2a2.all)
        g2a = k.sb(ph, [128, 768], BF16, "g2a")
        g2b = k.sb(ph, [32, 768], BF16, "g2b")
        k.dma("pool", g2a[:], prm["g2"][0:128, :], writes=g2a.all)
        k.dma("pool", g2b[:], prm["g2"][128:160, :], writes=g2b.all)
        tiny = k.sb(ph, [128, 1], F32, "tiny")
        k.op("pool", lambda e: e.memset(tiny[:], 1e-24), writes=tiny.all)
        gneps = k.sb(ph, [128, 1], F32, "gneps")
        k.op("pool", lambda e: e.memset(gneps[:], GN_EPS), writes=gneps.all)
        msT = k.sb(ph, [128, 128], F32, "msT")
        miT = k.sb(ph, [128, 128], F32, "miT")
        msL = k.sb(ph, [128, 128], F32, "msL")
        for (t, pat, base, cm) in ((msT, [[1, 128]], -1, -1), (miT, [[1, 128]], 0, -1), (msL, [[-1, 128]], -1, 1)):
            k.op("pool", lambda e: e.memset(t[:], 1.0), writes=t.all)
            k.op("pool", lambda e: e.affine_select(t[:], t[:], pattern=pat, compare_op=ALU.is_ge, fill=0.0,
                                                   base=base, channel_multiplier=cm), reads=t.all, writes=t.all)
            k.op("pool", lambda e: e.memset(t[0:64, 64:128], 0.0), reads=t.all, writes=t.all)
            k.op("pool", lambda e: e.memset(t[64:128, 0:64], 0.0), reads=t.all, writes=t.all)
        MASK1 = k.sb(ph, [128, 2, 256], F32, "MASK1")
        MASK3 = k.sb(ph, [128, 4, 128], F32, "MASK3")
        for hh in range(2):
            k.op("pool", lambda e: e.tensor_copy(MASK1[:, hh, 0:128], msT[:]), reads=msT.all, writes=MASK1.all)
            k.op("pool", lambda e: e.tensor_copy(MASK1[:, hh, 128:256], miT[:]), reads=miT.all, writes=MASK1.all)
        for hh in range(4):
            k.op("pool", lambda e: e.tensor_copy(MASK3[:, hh, :], msL[:]), reads=msL.all, writes=MASK3.all)
        bones = k.sb(ph, [128, 128], F32, "bones")
        k.op("pool", lambda e: e.memset(bones[:], 1.0), writes=bones.all)
        k.op("pool", lambda e: e.memset(bones[0:64, 64:128], 0.0), reads=bones.all, writes=bones.all)
        k.op("pool", lambda e: e.memset(bones[64:128, 0:64], 0.0), reads=bones.all, writes=bones.all)
        bsel = k.sb(ph, [128, 2], F32, "bsel")
        k.op("pool", lambda e: e.memset(bsel[:], 0.0), writes=bsel.all)
        k.op("pool", lambda e: e.memset(bsel[0:64, 0:1], 1.0), reads=bsel.all, writes=bsel.all)
        k.op("pool", lambda e: e.memset(bsel[64:128, 1:2], 1.0), reads=bsel.all, writes=bsel.all)
        Mscan = k.sb(ph, [128, TBK], F32, "Mscan")
        k.op("pool", lambda e: e.memset(Mscan[:], 1.0), writes=Mscan.all)
        k.op("pool", lambda e: e.memset(Mscan[:].rearrange("p (n s) -> p n s", s=64)[:, :, 0:1], 0.0),
             reads=Mscan.all, writes=Mscan.all)
        ZL = k.sb(ph, [128, 6, TBK + 1], F32, "ZL")
        ZS = k.sb(ph, [128, 3, TBK + 1], F32, "ZS")
        bufs = [k.sb(ph, [128, 6 * TBK], F32, "rb%d" % i) for i in range(7)]
        b1, b2, b3, b4, b5, b6, b7 = bufs
        sm = [k.sb(ph, [128, TBK], F32, "rsm%d" % i) for i in range(3)]
        twz = k.sb(ph, [128, TBK], BF16, "twz")
        sgA = k.sb(ph, [128, TBK], BF16, "sgA")
        sgB = k.sb(ph, [32, TBK], BF16, "sgB")
        CL = k.sb(ph, [128, 6, NCH], F32, "CL")
        PC = k.sb(ph, [128, 6, NCH], F32, "PC")
        AR = k.sb(ph, [128, 6, 2, TBK], BF16, "AR")
        BK = k.sb(ph, [128, 6, 2, TBK], BF16, "BK")
        Vt = k.sb(ph, [128, NPR, 768], BF16, "Vt")
        BBt = k.sb(ph, [128, NPR, 768], BF16, "BBt")
        KBt = k.sb(ph, [128, NPR, 768], BF16, "KBt")
        Gt = k.sb(ph, [128, NPR, 768], F32, "Gt")
        bon = k.sb(ph, [128, NPR, 12], F32, "bon")
        T1 = k.sb(ph, [128, 6, 2, 256], BF16, "T1")
        T2 = k.sb(ph, [128, 6, 2, 256], BF16, "T2")
        X = [k.sb(ph, [128, 3, 128], F32, "nX%d" % i) for i in range(2)]
        XT = [k.sb(ph, [128, 3, 128], F32, "nXT%d" % i) for i in range(2)]
        TT = [k.sb(ph, [128, 3, 128], F32, "nTT%d" % i) for i in range(2)]
        TTb = k.sb(ph, [128, 6, 2, 128], BF16, "TTb")
        U0 = k.sb(ph, [128, 768], F32, "U0")
        ST = k.sb(ph, [128, 6, 64], F32, "ST")
        STb = k.sb(ph, [128, 6, 64], BF16, "STb")
        k.op("pool", lambda e: e.memset(ST[:], 0.0), writes=ST.all)
        k.op("pool", lambda e: e.memset(STb[:], 0.0), writes=STb.all)
        Wsb = k.sb(ph, [128, 768], BF16, "Wsb")
        Usb = k.sb(ph, [128, 768], BF16, "Usb")
        Ysb = k.sb(ph, [128, 768], F32, "Ysb")
        Yw = [k.sb(ph, [128, 768], F32, "Yw%d" % i) for i in range(2)]
        st12 = [k.sb(ph, [128, 12], F32, "st12_%d" % i) for i in range(4)]
        Stmp = k.sb(ph, [128, 6, 64], F32, "Stmp")
        Yo = [k.sb(ph, [128, 768], F32, "Yo%d" % i) for i in range(2)]

        def load_group(dst, row0, nchunks, t0, rows=128):
            src = ZT[row0:row0 + nchunks * 128, :].rearrange("(c p) t -> p c t", p=128) if rows == 128 else None
            if t0 == 0:
                k.op("pool", lambda e: e.memset(dst[0:rows, 0:nchunks, 0:1], 0.0), writes=dst.all)
                k.dma("sp", dst[0:rows, 0:nchunks, 1:TBK + 1], src[:, :, 0:TBK], writes=dst.all)
            else:
                k.dma("sp", dst[0:rows, 0:nchunks, :], src[:, :, t0 - 1:t0 + TBK], writes=dst.all)

        def shift_mix(dst3, src, mu0, nchunks, tmp3, rows=128):
            k.op("dve", lambda e: e.tensor_tensor(tmp3[0:rows, 0:nchunks, :], src[0:rows, 0:nchunks, 0:TBK],
                                                  src[0:rows, 0:nchunks, 1:TBK + 1], op=ALU.subtract),
                 reads=src.all, writes=[tmp3_res[0]])
            for cc in range(nchunks):
                k.op("dve",
                     lambda e: e.scalar_tensor_tensor(dst3[0:rows, cc, :], tmp3[0:rows, cc, :],
                                                      mu_c[0:rows, mu0 + cc:mu0 + cc + 1],
                                                      src[0:rows, cc, 1:TBK + 1], op0=ALU.mult, op1=ALU.add),
                     reads=[tmp3_res[0]] + src.all + mu_c.all, writes=[dst3_res[0]])

        mixq = 0
        for blk in range(NB):
            t0 = blk * TBK
            if t0 == 0:
                k.op("pool", lambda e: e.memset(ZS[:, :, 0:1], 0.0), writes=ZS.all)
            lo = 1 if t0 == 0 else 0
            k.dma("sp", ZS[:, 0:2, lo:TBK + 1],
                  ZT[2304:2560, :].rearrange("(c p) t -> p c t", p=128)[:, :, t0 - 1 + lo:t0 + TBK], writes=ZS.all)
            k.dma("sp", ZS[0:32, 2, lo:TBK + 1], ZT[2560:2592, t0 - 1 + lo:t0 + TBK], writes=ZS.all)
            zsm = sm[0], sm[1], sm[2]
            for cc, rows in ((0, 128), (1, 128), (2, 32)):
                d = zsm[cc]
                k.op("dve", lambda e: e.tensor_tensor(d[0:rows, :], ZS[0:rows, cc, 0:TBK], ZS[0:rows, cc, 1:TBK + 1],
                                                      op=ALU.subtract), reads=ZS.all, writes=d.all)
                k.op("dve", lambda e: e.scalar_tensor_tensor(d[0:rows, :], d[0:rows, :],
                                                             mu_c[0:rows, 18 + cc:19 + cc], ZS[0:rows, cc, 1:TBK + 1],
                                                             op0=ALU.mult, op1=ALU.add),
                     reads=d.all + ZS.all + mu_c.all, writes=d.all)
            zwza = sm[0]
            k.op("act", lambda e: e.activation(zwza[0:64, :], zwza[0:64, :], AF.Exp, scale=2.0),
                 reads=zwza.all, writes=zwza.all)
            k.op("dve", lambda e: e.tensor_scalar(zwza[0:64, :], zwza[0:64, :], 1.0, None, op0=ALU.add),
                 reads=zwza.all, writes=zwza.all)
            k.op("dve", lambda e: e.reciprocal(zwza[0:64, :], zwza[0:64, :]), reads=zwza.all, writes=zwza.all)
            k.op("dve", lambda e: e.tensor_scalar(twz[0:64, :], zwza[0:64, :], -2.0, 1.0, op0=ALU.mult, op1=ALU.add),
                 reads=zwza.all, writes=twz.all)
            k.op("dve", lambda e: e.tensor_copy(twz[64:128, :], zwza[64:128, :]), reads=zwza.all, writes=twz.all)
            for (src_, dst_, rows) in ((sm[1], sgA, 128), (sm[2], sgB, 32)):
                k.op("act", lambda e: e.activation(src_[0:rows, :], src_[0:rows, :], AF.Exp, scale=-1.0),
                     reads=src_.all, writes=src_.all)
                k.op("dve", lambda e: e.tensor_scalar(src_[0:rows, :], src_[0:rows, :], 1.0, None, op0=ALU.add),
                     reads=src_.all, writes=src_.all)
                k.op("dve", lambda e: e.reciprocal(src_[0:rows, :], src_[0:rows, :]), reads=src_.all, writes=src_.all)
                k.op("dve", lambda e: e.tensor_copy(dst_[0:rows, :], src_[0:rows, :]), reads=src_.all, writes=dst_.all)
            if dbg <= 1:
                break
            LW, CUM, A3 = V3(b1), V3(b2), V3(b3)
            for (dst3, dres, rows, bcol) in ((LW, b1, slice(0, 64), w0n), (A3, b3, slice(64, 128), a0n)):
                for fc in range(6):
                    ps = k.ps[fc % 4]
                    k.mm(ps.p(), [(ps[:, 0:TBK], w2a2[rows, fc * 128:(fc + 1) * 128], twz[rows, :])],
                         reads=w2a2.all + twz.all)
                    k.op("act", lambda e: e.activation(dst3[:, fc, :], ps[:, 0:TBK], AF.Exp, scale=-1.0,
                                                       bias=bcol[:, fc:fc + 1]),
                         reads=ps.all + bcol.all, writes=dres.all)
                k.op("dve", lambda e: e.tensor_scalar(dres[:], dres[:], 1.0, None, op0=ALU.add),
                     reads=dres.all, writes=dres.all)
                k.op("dve", lambda e: e.reciprocal(dres[:], dres[:]), reads=dres.all, writes=dres.all)
            k.op("dve", lambda e: e.tensor_scalar(b1[:], b1[:], -0.6065306597126334, None, op0=ALU.mult),
                 reads=b1.all, writes=b1.all)
            for fc in range(6):
                k.op("dve", lambda e: e.tensor_tensor_scan(CUM[:, fc, :], Mscan[:], LW[:, fc, :], 0.0,
                                                           op0=ALU.mult, op1=ALU.add),
                     reads=Mscan.all + b1.all, writes=b2.all)
            CUM4 = b2[:].rearrange("p (c n s) -> p c n s", c=6, s=64)
            k.op("dve", lambda e: e.tensor_copy(CL[:], CUM4[:, :, :, 63]), reads=b2.all, writes=CL.all)
            k.op("act", lambda e: e.activation(PC[:], CL[:], AF.Exp), reads=CL.all, writes=PC.all)
            if dbg <= 2:
                break
            tmp3_res, dst3_res = [b7.p()], [b4.p()]
            load_group(ZL, 1536, 6, t0)
            shift_mix(V3(b4), ZL, 12, 6, V3(b7))
            VM = V3(b4)

            def to_token_major(src3, sres, dstT):
                for pr in range(NPR):
                    for half in range(2):
                        ps = k.ps[4 + (pr * 2 + half) % 4]
                        for j in range(3):
                            fc = half * 3 + j
                            k.op("pe", lambda e: e.transpose(ps[:, j * 128:(j + 1) * 128],
                                                             src3[:, fc, pr * 128:(pr + 1) * 128], c["idf"][:]),
                                 reads=sres.all + c["idf"].all, writes=ps.all)
                        if half == 0:
                            k.op("dve", lambda e: e.tensor_copy(dstT[:, pr, 0:384], ps[:, 0:384]),
                                 reads=ps.all, writes=dstT.all)
                        else:
                            k.op("act", lambda e: e.activation(dstT[:, pr, 384:768], ps[:, 0:384], AF.Copy),
                                 reads=ps.all, writes=dstT.all)
            to_token_major(VM, b4, Vt)
            if dbg <= 3:
                break
            dst3_res = [b4.p()]
            load_group(ZL, 768, 6, t0)
            shift_mix(V3(b4), ZL, 6, 6, V3(b7))
            KM, KK, B3, T7 = V3(b4), V3(b5), V3(b6), V3(b7)
            k.op("dve", lambda e: e.tensor_tensor(KK, KM, kk_c[:].rearrange("p (c o) -> p c o", o=1)
                                                  .to_broadcast(W6), op=ALU.mult),
                 reads=b4.all + kk_c.all, writes=b5.all)
            k.op("pool", lambda e: e.tensor_tensor(B3, KK, KK, op=ALU.mult), reads=b5.all, writes=b6.all)
            for fc in range(6):
                ps = k.ps[fc % 4]
                k.mm(ps.p(), [(ps[:, 0:TBK], bones[:], B3[:, fc, :])], reads=bones.all + b6.all)
                k.op("act", lambda e: e.activation(T7[:, fc, :], ps[:, 0:TBK], AF.Ln, bias=tiny[:, 0:1]),
                     reads=ps.all + tiny.all, writes=b7.all)
            k.op("act", lambda e: e.activation(b7[:], b7[:], AF.Exp, scale=-0.5), reads=b7.all, writes=b7.all)
            k.op("dve", lambda e: e.tensor_tensor(b5[:], b5[:], b7[:], op=ALU.mult),
                 reads=b5.all + b7.all, writes=b5.all)
            for fc in range(6):
                k.op("dve",
                     lambda e: e.tensor_scalar(T7[:, fc, :], A3[:, fc, :], ka_c[:, fc:fc + 1], omka[:, fc:fc + 1],
                                               op0=ALU.mult, op1=ALU.add),
                     reads=b3.all + ka_c.all + omka.all, writes=b7.all)
            k.op("dve", lambda e: e.tensor_tensor(b4[:], b4[:], b7[:], op=ALU.mult),
                 reads=b4.all + b7.all, writes=b4.all)
            k.op("pool", lambda e: e.tensor_tensor(b6[:], b5[:], b3[:], op=ALU.mult),
                 reads=b5.all + b3.all, writes=b6.all)
            if dbg <= 4:
                break
            BK4, AR4 = BK, AR
            k.op("act", lambda e: e.activation(b3[:], b2[:], AF.Exp, scale=-1.0), reads=b2.all, writes=b3.all)
            k.op("dve", lambda e: e.tensor_tensor(BK4[:, :, 0, :], B3, V3(b3), op=ALU.mult),
                 reads=b6.all + b3.all, writes=BK.all)
            k.op("pool", lambda e: e.tensor_tensor(BK4[:, :, 1, :], KM, V3(b3), op=ALU.mult),
                 reads=b4.all + b3.all, writes=BK.all)
            k.op("dve", lambda e: e.scalar_tensor_tensor(
                b3[:].rearrange("p (c n s) -> p c n s", c=6, s=64), CUM4, -1.0,
                CL[:].rearrange("p c (n o) -> p c n o", o=1).to_broadcast([128, 6, NCH, 64]),
                op0=ALU.mult, op1=ALU.add), reads=b2.all + CL.all, writes=b3.all)
            k.op("act", lambda e: e.activation(b3[:], b3[:], AF.Exp), reads=b3.all, writes=b3.all)
            k.op("dve", lambda e: e.tensor_tensor(b6[:], b6[:], b3[:], op=ALU.mult),
                 reads=b6.all + b3.all, writes=b6.all)
            k.op("pool", lambda e: e.tensor_tensor(b7[:], b4[:], b3[:], op=ALU.mult),
                 reads=b4.all + b3.all, writes=b7.all)
            to_token_major(V3(b6), b6, BBt)
            to_token_major(V3(b7), b7, KBt)
            k.op("dve", lambda e: e.tensor_tensor(b3[:], b2[:], b1[:], op=ALU.subtract),
                 reads=b2.all + b1.all, writes=b3.all)
            k.op("act", lambda e: e.activation(b3[:], b3[:], AF.Exp), reads=b3.all, writes=b3.all)
            k.op("dve", lambda e: e.scalar_tensor_tensor(AR4[:, :, 0, :], KK, -1.0, V3(b3), op0=ALU.mult, op1=ALU.mult),
                 reads=b5.all + b3.all, writes=AR.all)
            if dbg <= 5:
                break
            tmp3_res, dst3_res = [b7.p()], [b1.p()]
            load_group(ZL, 0, 6, t0)
            shift_mix(V3(b1), ZL, 0, 6, V3(b7))
            RM = V3(b1)
            k.op("act", lambda e: e.activation(b3[:], b2[:], AF.Exp), reads=b2.all, writes=b3.all)
            k.op("dve", lambda e: e.tensor_tensor(AR4[:, :, 1, :], RM, V3(b3), op=ALU.mult),
                 reads=b1.all + b3.all, writes=AR.all)
            k.op("pool", lambda e: e.tensor_tensor(b7[:], b1[:], b4[:], op=ALU.mult),
                 reads=b1.all + b4.all, writes=b7.all)
            k.op("dve", lambda e: e.tensor_tensor(V3(b7), V3(b7), rk_c[:].rearrange("p (c o) -> p c o", o=1)
                                                  .to_broadcast(W6), op=ALU.mult),
                 reads=b7.all + rk_c.all, writes=b7.all)
            for pr in range(NPR):
                ps = k.ps[pr % 4]
                k.mmx(ps.p(), [(ps[:, fc * 2:fc * 2 + 2], V3(b7)[:, fc, pr * 128:(pr + 1) * 128], bsel[:],
                                True, True) for fc in range(6)], reads=b7.all + bsel.all)
                k.op("dve", lambda e: e.tensor_copy(bon[:, pr, :], ps[:, 0:12]), reads=ps.all, writes=bon.all)
            if dbg <= 6:
                break
            for pr in range(NPR):
                for half in range(2):
                    ps = k.ps[4 + (pr * 2 + half) % 4]
                    k.mm(ps.p(), [(ps[:, 0:384], sgA[:, pr * 128:(pr + 1) * 128], g2a[:, half * 384:(half + 1) * 384]),
                                  (ps[:, 0:384], sgB[:, pr * 128:(pr + 1) * 128], g2b[:, half * 384:(half + 1) * 384])],
                         reads=sgA.all + sgB.all + g2a.all + g2b.all)
                    k.op("act", lambda e: e.activation(Gt[:, pr, half * 384:(half + 1) * 384], ps[:, 0:384], AF.Copy),
                         reads=ps.all, writes=Gt.all)
            if dbg <= 7:
                break
            for pr in range(NPR):
                tc = slice(pr * 128, (pr + 1) * 128)
                bi = 0
                for which, dstT in ((0, T1), (1, T2)):
                    for hp in range(2):
                        rows = slice(hp * 64, hp * 64 + 64)
                        for f0 in range(0, 6, 2):
                            ps = k.ps[bi % 4]
                            bi += 1
                            mms = []
                            for ff in range(2):
                                fc = f0 + ff
                                mms.append((ps[:, ff * 256:(ff + 1) * 256].rearrange("p (a t) -> p a t", a=2),
                                            BK[rows, fc, which, tc], AR[rows, fc, :, tc], True, True))
                            k.mmx(ps.p(), mms, reads=BK.all + AR.all)
                            k.op("dve", lambda e: e.tensor_tensor(dstT[:, f0:f0 + 2, hp, :],
                                                                  ps[:, :].rearrange("p (h c) -> p h c", h=2), MASK1[:],
                                                                  op=ALU.mult),
                                 reads=ps.all + MASK1.all, writes=dstT.all)
                if dbg <= 8:
                    break
                for hp in range(2):
                    rows = slice(hp * 64, hp * 64 + 64)
                    for f0 in (0, 3):
                        cur = 0
                        psL, psT = k.ps[4], k.ps[5]
                        k.mmx(psL.p(), [(psL[:, ff * 128:(ff + 1) * 128], AR[rows, f0 + ff, 0, tc], BK[rows, f0 + ff, 0, tc],
                                         True, True) for ff in range(3)], reads=BK.all + AR.all)
                        k.mmx(psT.p(), [(psT[:, ff * 128:(ff + 1) * 128], BK[rows, f0 + ff, 0, tc], AR[rows, f0 + ff, 0, tc],
                                         True, True) for ff in range(3)], reads=BK.all + AR.all)
                        k.op("dve", lambda e: e.tensor_tensor(X[0][:], psL[:, 0:384].rearrange("p (h c) -> p h c", h=3),
                                                              MASK3[:, 0:3, :], op=ALU.mult),
                             reads=psL.all + MASK3.all, writes=X[0].all)
                        k.op("dve", lambda e: e.tensor_tensor(XT[0][:], psT[:, 0:384].rearrange("p (h c) -> p h c", h=3),
                                                              msT[:].rearrange("p (o c) -> p o c", o=1)
                                                              .to_broadcast([128, 3, 128]), op=ALU.mult),
                             reads=psT.all + msT.all, writes=XT[0].all)
                        k.op("pool", lambda e: e.tensor_tensor(TT[0][:], XT[0][:],
                                                               c["idf"][:].rearrange("p (o c) -> p o c", o=1)
                                                               .to_broadcast([128, 3, 128]), op=ALU.add),
                             reads=XT[0].all + c["idf"].all, writes=TT[0].all)
                        for lvl in range(5):
                            nxt = 1 - cur
                            pa, pb, pc2 = k.ps[4 + (lvl % 2) * 2], k.ps[5 + (lvl % 2) * 2], k.ps[6 - (lvl % 2) * 2]
                            k.mmx(pa.p(), [(pa[:, hh * 128:(hh + 1) * 128], XT[cur][:, hh, :], X[cur][:, hh, :], True, True)
                                           for hh in range(3)], reads=X[cur].all + XT[cur].all)
                            k.op("dve", lambda e: e.tensor_copy(X[nxt][:], pa[:, 0:384].rearrange("p (h c) -> p h c", h=3)),
                                 reads=pa.all, writes=X[nxt].all)
                            if lvl < 4:
                                k.mmx(pb.p(), [(pb[:, hh * 128:(hh + 1) * 128], X[cur][:, hh, :], XT[cur][:, hh, :],
                                                True, True) for hh in range(3)], reads=X[cur].all + XT[cur].all)
                                k.op("act", lambda e: e.activation(XT[nxt][:],
                                                                   pb[:, 0:384].rearrange("p (h c) -> p h c", h=3),
                                                                   AF.Copy), reads=pb.all, writes=XT[nxt].all)
                            k.mmx(pc2.p(), [(pc2[:, hh * 128:(hh + 1) * 128], X[nxt][:, hh, :], TT[cur][:, hh, :],
                                             True, True) for hh in range(3)], reads=X[nxt].all + TT[cur].all)
                            k.op("dve", lambda e: e.tensor_tensor(TT[nxt][:], TT[cur][:],
                                                                  pc2[:, 0:384].rearrange("p (h c) -> p h c", h=3),
                                                                  op=ALU.add),
                                 reads=pc2.all + TT[cur].all, writes=TT[nxt].all)
                            cur = nxt
                        k.op("act", lambda e: e.activation(TTb[:, f0:f0 + 3, hp, :], TT[cur][:], AF.Copy),
                             reads=TT[cur].all, writes=TTb.all)
                if dbg <= 9:
                    break
                for ci in range(2):
                    prs = slice(ci * 64, ci * 64 + 64)
                    tl = slice(ci * 64, ci * 64 + 64)
                    pw = [k.ps[4], k.ps[5]]
                    for half in range(2):
                        k.mmx(pw[half].p(), [(pw[half][prs, hh * 64:(hh + 1) * 64],
                                              T2[prs, (half * 6 + hh) // 2, (half * 6 + hh) % 2, tl],
                                              Vt[prs, pr, (half * 6 + hh) * 64:(half * 6 + hh + 1) * 64],
                                              hh == 0, hh == 5) for hh in range(6)], reads=T2.all + Vt.all)
                        k.op("act", lambda e: e.activation(Wsb[prs, half * 384:(half + 1) * 384], pw[half][prs, 0:384],
                                                           AF.Copy), reads=pw[half].all, writes=Wsb.all)
                    pu = [k.ps[6], k.ps[7]]
                    for half in range(2):
                        k.mmx(pu[half].p(), [(pu[half][prs, hh * 64:(hh + 1) * 64],
                                              TTb[prs, (half * 6 + hh) // 2, (half * 6 + hh) % 2, tl],
                                              Wsb[prs, (half * 6 + hh) * 64:(half * 6 + hh + 1) * 64],
                                              hh == 0, hh == 5) for hh in range(6)], reads=TTb.all + Wsb.all)
                        k.op("dve", lambda e: e.tensor_copy(U0[prs, half * 384:(half + 1) * 384], pu[half][prs, 0:384]),
                             reads=pu[half].all, writes=U0.all)
                for ci in range(2):
                    ch = pr * 2 + ci
                    prs = slice(ci * 64, ci * 64 + 64)
                    tcc = slice(pr * 128 + ci * 64, pr * 128 + ci * 64 + 64)
                    tl = slice(ci * 64, ci * 64 + 64)
                    pw = [k.ps[0], k.ps[1]]
                    py = [k.ps[2], k.ps[3]]
                    for (pbk, which) in ((pw, 0), (py, 1)):
                        for hp in range(2):
                            rows = slice(hp * 64, hp * 64 + 64)
                            k.mmx(pbk[hp].p(), [(pbk[hp][prs, fc * 64:(fc + 1) * 64], AR[rows, fc, which, tcc],
                                                 STb[rows, fc, :], fc == 0, fc == 5) for fc in range(6)],
                                  reads=AR.all + STb.all)
                    Wv = Wsb[:].rearrange("p (f h v) -> p f h v", f=6, h=2)
                    for hp in range(2):
                        k.op("dve" if hp == 0 else "act",
                             (lambda e: e.tensor_copy(Wv[prs, :, hp, :], pw[hp][prs, 0:384].rearrange("p (f v) -> p f v", f=6)))
                             if hp == 0 else
                             (lambda e: e.activation(Wv[prs, :, hp, :], pw[hp][prs, 0:384].rearrange("p (f v) -> p f v", f=6),
                                                     AF.Copy)),
                             reads=pw[hp].all, writes=Wsb.all)
                    pu = [k.ps[4], k.ps[5]]
                    for half in range(2):
                        k.mmx(pu[half].p(), [(pu[half][prs, hh * 64:(hh + 1) * 64],
                                              TTb[prs, (half * 6 + hh) // 2, (half * 6 + hh) % 2, tl],
                                              Wsb[prs, (half * 6 + hh) * 64:(half * 6 + hh + 1) * 64],
                                              hh == 0, hh == 5) for hh in range(6)], reads=TTb.all + Wsb.all)
                        k.op("dve", lambda e: e.tensor_tensor(Usb[prs, half * 384:(half + 1) * 384],
                                                              U0[prs, half * 384:(half + 1) * 384], pu[half][prs, 0:384],
                                                              op=ALU.add),
                             reads=pu[half].all + U0.all, writes=Usb.all)
                    pya = [k.ps[6], k.ps[7]]
                    for half in range(2):
                        mms = []
                        for hh in range(6):
                            h = half * 6 + hh
                            o = pya[half][prs, hh * 64:(hh + 1) * 64]
                            mms.append((o, T1[prs, h // 2, h % 2, 128 + ci * 64:128 + ci * 64 + 64],
                                        Usb[prs, h * 64:(h + 1) * 64], hh == 0, False))
                            mms.append((o, T2[prs, h // 2, h % 2, 128 + ci * 64:128 + ci * 64 + 64],
                                        Vt[prs, pr, h * 64:(h + 1) * 64], False, hh == 5))
                        k.mmx(pya[half].p(), mms, reads=T1.all + T2.all + Vt.all + Usb.all)
                    Yv = Ysb[:].rearrange("p (f h v) -> p f h v", f=6, h=2)
                    for hp in range(2):
                        k.op("act", lambda e: e.activation(Yv[prs, :, hp, :],
                                                           py[hp][prs, 0:384].rearrange("p (f v) -> p f v", f=6), AF.Copy),
                             reads=py[hp].all, writes=Ysb.all)
                    for half in range(2):
                        k.op("dve", lambda e: e.tensor_tensor(Ysb[prs, half * 384:(half + 1) * 384],
                                                              Ysb[prs, half * 384:(half + 1) * 384], pya[half][prs, 0:384],
                                                              op=ALU.add),
                             reads=pya[half].all + Ysb.all, writes=Ysb.all)
                    pss = k.ps[0]
                    mms = []
                    for h in range(12):
                        fc, hp = h // 2, h % 2
                        rows = slice(hp * 64, hp * 64 + 64)
                        o = pss[rows, fc * 64:(fc + 1) * 64]
                        mms.append((o, BBt[prs, pr, h * 64:(h + 1) * 64], Usb[prs, h * 64:(h + 1) * 64], h < 2, False))
                        mms.append((o, KBt[prs, pr, h * 64:(h + 1) * 64], Vt[prs, pr, h * 64:(h + 1) * 64], False, h == 11))
                    k.mmx(pss.p(), mms, reads=BBt.all + KBt.all + Vt.all + Usb.all)
                    k.op("dve", lambda e: e.tensor_tensor(Stmp[:], ST[:], PC[:, :, ch:ch + 1].to_broadcast([128, 6, 64]),
                                                          op=ALU.mult), reads=ST.all + PC.all, writes=Stmp.all)
                    k.op("dve", lambda e: e.tensor_tensor(ST[:], Stmp[:], pss[:, 0:384].rearrange("p (c v) -> p c v", c=6),
                                                          op=ALU.add), reads=Stmp.all + pss.all, writes=ST.all)
                    k.op("act", lambda e: e.activation(STb[:], ST[:], AF.Copy), reads=ST.all, writes=STb.all)
                if dbg <= 10:
                    break
                yw, yo = Yw[pr % 2], Yo[pr % 2]
                s1, s2, mean, rstd = st12
                Y3 = Ysb[:].rearrange("p (h v) -> p h v", h=12)
                k.op("dve", lambda e: e.reduce_sum(s1[:], Y3, axis=AX.X), reads=Ysb.all, writes=s1.all)
                k.op("pool", lambda e: e.tensor_tensor(yw[:], Ysb[:], Ysb[:], op=ALU.mult), reads=Ysb.all, writes=yw.all)
                k.op("dve", lambda e: e.reduce_sum(s2[:], yw[:].rearrange("p (h v) -> p h v", h=12), axis=AX.X),
                     reads=yw.all, writes=s2.all)
                k.op("dve", lambda e: e.tensor_scalar(mean[:], s1[:], 1.0 / 64, None, op0=ALU.mult),
                     reads=s1.all, writes=mean.all)
                k.op("dve", lambda e: e.tensor_tensor(s1[:], mean[:], mean[:], op=ALU.mult), reads=mean.all, writes=s1.all)
                k.op("dve", lambda e: e.scalar_tensor_tensor(s2[:], s2[:], 1.0 / 64, s1[:], op0=ALU.mult, op1=ALU.subtract),
                     reads=s2.all + s1.all, writes=s2.all)
                k.op("act", lambda e: e.activation(rstd[:], s2[:], AF.Ln, bias=gneps[:, 0:1]),
                     reads=s2.all + gneps.all, writes=rstd.all)
                k.op("act", lambda e: e.activation(rstd[:], rstd[:], AF.Exp, scale=-0.5), reads=rstd.all, writes=rstd.all)
                yw3 = yw[:].rearrange("p (h v) -> p h v", h=12)
                k.op("dve", lambda e: e.tensor_tensor(yw3, Y3, mean[:].rearrange("p (h o) -> p h o", o=1)
                                                      .to_broadcast([128, 12, 64]), op=ALU.subtract),
                     reads=Ysb.all + mean.all, writes=yw.all)
                k.op("dve", lambda e: e.tensor_tensor(yw3, yw3, rstd[:].rearrange("p (h o) -> p h o", o=1)
                                                      .to_broadcast([128, 12, 64]), op=ALU.mult),
                     reads=yw.all + rstd.all, writes=yw.all)
                k.op("pool", lambda e: e.tensor_tensor(yw[:], yw[:], lnw[:], op=ALU.mult),
                     reads=yw.all + lnw.all, writes=yw.all)
                k.op("pool", lambda e: e.tensor_tensor(yw[:], yw[:], lnb[:], op=ALU.add),
                     reads=yw.all + lnb.all, writes=yw.all)
                yo3 = yo[:].rearrange("p (h v) -> p h v", h=12)
                k.op("dve", lambda e: e.tensor_tensor(yo3, Vt[:, pr, :].rearrange("p (h v) -> p h v", h=12),
                                                      bon[:, pr, :].rearrange("p (h o) -> p h o", o=1)
                                                      .to_broadcast([128, 12, 64]), op=ALU.mult),
                     reads=Vt.all + bon.all, writes=yo.all)
                k.op("dve", lambda e: e.tensor_tensor(yo[:], yo[:], yw[:], op=ALU.add),
                     reads=yo.all + yw.all, writes=yo.all)
                k.op("dve", lambda e: e.tensor_tensor(yo[:], yo[:], Gt[:, pr, :], op=ALU.mult),
                     reads=yo.all + Gt.all, writes=yo.all)
                with nc.allow_non_contiguous_dma(reason="mix slice"):
                    k.dma("sp", MIX[t0 + pr * 128:t0 + (pr + 1) * 128, 0:768], yo[:], reads=yo.all)
        k.barrier()


SEQ = 4096
DEPTH = 4
PARAM_SHAPES = {
    "norm1": (4, 1024), "norm_mem": (4, 1024), "w_mem_kv": (4, 1024, 512), "w_o": (4, 1024, 1024),
    "norm2": (4, 1024), "w_ffn_in": (4, 1024, 5632), "w_ffn_out": (4, 2816, 1024),
    "nsa_w_in": (2, 1024, 2596), "nsa_gate_b": (2, 36), "nsa_cmp_pos": (2, 2, 32, 64),
    "nsa_cmp_w1": (2, 2, 32, 64, 128), "nsa_cmp_w2": (2, 2, 128, 64),
    "rw_w_in": (2, 1024, 2848), "rw_mu": (2, 2592), "rw_w0": (2, 768), "rw_w2": (2, 64, 768),
    "rw_a0": (2, 768), "rw_a2": (2, 64, 768), "rw_g2": (2, 160, 768), "rw_k_k": (2, 768), "rw_k_a": (2, 768),
    "rw_r_k": (2, 12, 64), "rw_lnx_w": (2, 768), "rw_lnx_b": (2, 768), "final_norm": (1024,),
}


def build_program(S=SEQ, depth=DEPTH):
    nc = bass.Bass("TRN2", target_bir_lowering=False)
    X0 = nc.dram_tensor("x", [S, 1024], F32, kind="ExternalInput").ap()
    MEM = nc.dram_tensor("mem", [256, 1024], F32, kind="ExternalInput").ap()
    P = {n: nc.dram_tensor(n, list(sh), F32, kind="ExternalInput").ap() for n, sh in PARAM_SHAPES.items()}
    OUT = nc.dram_tensor("out", [S, 1024], F32, kind="ExternalOutput").ap()

    def SC(name, shape, dt):
        return nc.dram_tensor(name, list(shape), dt, kind="Internal").ap()
    XA, XB = SC("XA", (S, 1024), F32), SC("XB", (S, 1024), F32)
    MIX = SC("MIX", (S, 1024), F32)
    sc = dict(QT=SC("QT", (768, S), BF16), KCT=SC("KCT", (256, S), BF16), VCT=SC("VCT", (256, S), BF16),
              KST=SC("KST", (256, S), BF16), KWT=SC("KWT", (256, S), BF16), VS=SC("VS", (S, 256), BF16),
              VW=SC("VW", (S, 256), BF16), GL=SC("GL", (S, 36), F32), QMT=SC("QMT", (256, S), BF16))
    ZT = SC("ZT", (2592, S), F32)
    with contextlib.ExitStack() as st:
        k = K(nc, st)
        with contextlib.ExitStack() as ph0:
            c = make_consts(k, ph0)
            Xcur = X0
            nsa_cache = {}
            for i in range(depth):
                j = i // 2
                if i % 2 == 0:
                    outs = [dict(c0=0, n=768, mode="fm", dst=sc["QT"], dt=BF16),
                            dict(c0=768, n=256, mode="fm", dst=sc["KCT"], dt=BF16),
                            dict(c0=1024, n=256, mode="fm", dst=sc["VCT"], dt=BF16),
                            dict(c0=1280, n=256, mode="fm", dst=sc["KST"], dt=BF16),
                            dict(c0=1536, n=256, mode="tm", dst=sc["VS"], dt=BF16),
                            dict(c0=1792, n=256, mode="fm", dst=sc["KWT"], dt=BF16),
                            dict(c0=2048, n=256, mode="tm", dst=sc["VW"], dt=BF16),
                            dict(c0=2304, n=36, mode="tm", dst=sc["GL"], dt=F32),
                            dict(c0=2340, n=256, mode="fm", dst=sc["QMT"], dt=BF16)]
                    phase_proj(k, c, Xcur, P["norm1"][i], P["nsa_w_in"][j], outs, S)
                    phase_nsa(k, c, sc, P["nsa_gate_b"][j], P["nsa_cmp_pos"][j], P["nsa_cmp_w1"][j],
                              P["nsa_cmp_w2"][j], MIX, S, cache=nsa_cache)
                else:
                    outs = [dict(c0=0, n=2592, mode="fm", dst=ZT, dt=F32),
                            dict(c0=2592, n=256, mode="fm", dst=sc["QMT"], dt=BF16)]
                    phase_proj(k, c, Xcur, P["norm1"][i], P["rw_w_in"][j], outs, S)
                    prm = dict(mu=P["rw_mu"][j], w0=P["rw_w0"][j], w2=P["rw_w2"][j], a0=P["rw_a0"][j],
                               a2=P["rw_a2"][j], g2=P["rw_g2"][j], k_k=P["rw_k_k"][j], k_a=P["rw_k_a"][j],
                               r_k=P["rw_r_k"][j], lnx_w=P["rw_lnx_w"][j], lnx_b=P["rw_lnx_b"][j])
                    phase_rwkv(k, c, ZT, prm, MIX, S)
                phase_memattn(k, c, MEM, P["norm_mem"][i], P["w_mem_kv"][i], sc["QMT"], MIX, S)
                phase_oproj(k, c, Xcur, XA, MIX, P["w_o"][i], S)
                phase_ffn(k, c, XA, XB, P["norm2"][i], P["w_ffn_in"][i], P["w_ffn_out"][i], S)
                Xcur = XB
            phase_final_norm(k, c, Xcur, OUT, P["final_norm"], S)
    return nc


def kernel(**inputs):
    x = np.ascontiguousarray(inputs["x"], dtype=np.float32)
    mem = np.ascontiguousarray(inputs["mem"], dtype=np.float32)
    B = x.shape[0]
    nc = build_program()
    params = {n: np.ascontiguousarray(inputs[n], dtype=np.float32) for n in PARAM_SHAPES}
    in_maps = []
    for b in range(B):
        d = dict(params)
        d["x"] = x[b]
        d["mem"] = mem[b]
        in_maps.append(d)
    res = run_bass_kernel_spmd(nc, in_maps, core_ids=list(range(B)))
    return np.stack([np.asarray(r["out"], dtype=np.float32) for r in res.results], axis=0)
```

```python
import contextlib
import numpy as np
import concourse.bass as bass
import concourse.mybir as mybir
from concourse.bass_utils import run_bass_kernel_spmd

F32 = mybir.dt.float32
BF16 = mybir.dt.bfloat16
AF = mybir.ActivationFunctionType
ALU = mybir.AluOpType
AX = mybir.AxisListType

D = 1024
FFN_H = 2816
RMS_EPS = 1e-6
NDMA = 24


class Res:
    __slots__ = ("w", "r")

    def __init__(self):
        self.w = None
        self.r = {}


class Eng:
    def __init__(self, name, eng, sem):
        self.name, self.eng, self.sem = name, eng, sem
        self.count = 0
        self.seen = {}


class Tile:
    def __init__(self, t, parts=1):
        self.t = t
        self.parts = [Res() for _ in range(parts)]

    def __getitem__(self, idx):
        return self.t[idx]

    def p(self, i=0):
        return self.parts[i]

    @property
    def all(self):
        return list(self.parts)


class K:
    def __init__(self, nc, stack):
        self.nc = nc
        self.st = stack
        self.E = {}
        for nm, eng in (("pe", nc.tensor), ("dve", nc.vector), ("act", nc.scalar),
                        ("pool", nc.gpsimd), ("sp", nc.sync)):
            sem = stack.enter_context(nc.semaphore("sem_" + nm))
            self.E[nm] = Eng(nm, eng, sem)
        self.slots = []
        for i in range(NDMA):
            sem = stack.enter_context(nc.semaphore("dsem%d" % i))
            self.slots.append([("dma%d" % i), sem, 0])
        self.rr = 0
        self.ps = [Tile(stack.enter_context(nc.psum_tensor("ps%d" % i, [128, 512], F32)))
                   for i in range(8)]
        self.uid = 0

    def sb(self, ph, shape, dtype, name, parts=1):
        self.uid += 1
        return Tile(ph.enter_context(self.nc.sbuf_tensor("%s_%d" % (name, self.uid), shape, dtype)), parts)

    def dram(self, name, shape, dtype, kind="Internal", parts=1):
        return Tile(self.nc.dram_tensor(name, shape, dtype, kind=kind), parts)

    def _wait(self, E, tok):
        key, sem, val = tok
        if E.seen.get(key, 0) < val:
            E.eng.wait_ge(sem, val)
            E.seen[key] = val

    def _deps(self, E, reads, writes):
        pe = E.name == "pe"
        for r in reads:
            if r.w is not None and not (pe and r.w[0] == "pe"):
                self._wait(E, r.w)
        for w in writes:
            if w.w is not None and not (pe and w.w[0] == "pe"):
                self._wait(E, w.w)
            for key, tok in w.r.items():
                if key != E.name:
                    self._wait(E, tok)

    def _mark(self, tok, reads, writes):
        for r in reads:
            r.r[tok[0]] = tok
        for w in writes:
            w.w = tok
            w.r = {}

    def op(self, en, fn, reads=(), writes=()):
        E = self.E[en]
        self._deps(E, reads, writes)
        inst = fn(E.eng)
        E.count += 1
        inst.then_inc(E.sem, 1)
        tok = (E.name, E.sem, E.count)
        E.seen[E.name] = E.seen.get(E.name, 0)
        self._mark(tok, reads, writes)
        return tok

    def mm(self, out_res, mms, reads=()):
        E = self.E["pe"]
        self._deps(E, reads, [out_res])
        n = len(mms)
        inst = None
        for i, (o, l, r) in enumerate(mms):
            inst = E.eng.matmul(o, l, r, start=(i == 0), stop=(i == n - 1))
        E.count += 1
        inst.then_inc(E.sem, 1)
        tok = (E.name, E.sem, E.count)
        self._mark(tok, reads, [out_res])
        return tok

    def mmx(self, out_res, mms, reads=()):
        E = self.E["pe"]
        self._deps(E, reads, [out_res])
        inst = None
        for (o, l, r, st, sp) in mms:
            inst = E.eng.matmul(o, l, r, start=st, stop=sp, skip_group_check=True)
        E.count += 1
        inst.then_inc(E.sem, 1)
        tok = (E.name, E.sem, E.count)
        self._mark(tok, reads, [out_res])
        return tok

    def dma(self, qn, out_ap, in_ap, reads=(), writes=(), **kw):
        E = self.E[qn]
        self._deps(E, reads, writes)
        slot = self.slots[self.rr]
        self.rr = (self.rr + 1) % NDMA
        if slot[2] > 0:
            self._wait(E, (slot[0], slot[1], slot[2]))
        inst = E.eng.dma_start(out=out_ap, in_=in_ap, **kw)
        slot[2] += 16
        inst.then_inc(slot[1], 16)
        tok = (slot[0], slot[1], slot[2])
        self._mark(tok, reads, writes)
        return tok

    def barrier(self):
        toks = [(e.name, e.sem, e.count) for e in self.E.values() if e.count > 0]
        toks += [(s[0], s[1], s[2]) for s in self.slots if s[2] > 0]
        for E in self.E.values():
            for t in toks:
                if t[0] != E.name:
                    self._wait(E, t)


def make_consts(k, ph):
    nc = k.nc
    c = {}
    idf = k.sb(ph, [128, 128], F32, "identf")
    k.op("pool", lambda e: e.memset(idf[:], 1.0), writes=idf.all)
    k.op("pool", lambda e: e.affine_select(idf[:], idf[:], pattern=[[-1, 128]], compare_op=ALU.is_equal,
                                           fill=0.0, base=0, channel_multiplier=1),
         reads=idf.all, writes=idf.all)
    idb = k.sb(ph, [128, 128], BF16, "identb")
    k.op("dve", lambda e: e.tensor_copy(idb[:], idf[:]), reads=idf.all, writes=idb.all)
    c["idf"], c["idb"] = idf, idb
    return c


def load_vec_bcast(k, ph, dram_ap_1d, n, name, q="sp"):
    t = k.sb(ph, [128, n], F32, name)
    k.dma(q, t[:], dram_ap_1d.partition_broadcast(128), writes=t.all)
    return t


def load_vec_col(k, ph, dram_ap_1d, nch, name, q="sp"):
    t = k.sb(ph, [128, nch], F32, name)
    with k.nc.allow_non_contiguous_dma(reason="small vector load"):
        k.dma(q, t[:], dram_ap_1d.rearrange("(c p) -> p c", p=128), writes=t.all)
    return t


def norm_block(k, c, xts, gcol, hT, hparts, scr, psbase=0, col0=0, eps=RMS_EPS):
    n = len(xts)
    sq, ss, rstd = scr["sq"], scr["ss"], scr["rstd"]
    for i, x in enumerate(xts):
        k.op("act", lambda e: e.activation(sq[:], x[:], AF.Square, accum_out=ss[:, i:i + 1]),
             reads=x.all, writes=sq.all + ss.all)
    k.op("act", lambda e: e.activation(rstd[:, 0:n], ss[:, 0:n], AF.Ln, scale=1.0 / D, bias=scr["eps"][:, 0:1]),
         reads=ss.all + scr["eps"].all, writes=rstd.all)
    k.op("act", lambda e: e.activation(rstd[:, 0:n], rstd[:, 0:n], AF.Exp, scale=-0.5),
         reads=rstd.all, writes=rstd.all)
    for i, x in enumerate(xts):
        xn = scr["xn"][i % 2]
        k.op("dve", lambda e: e.tensor_scalar(xn[:], x[:], rstd[:, i:i + 1], None, op0=ALU.mult),
             reads=x.all + rstd.all, writes=xn.all)
        for g4 in range(2):
            ps = k.ps[psbase + (i % 2) * 2 + g4]
            for j in range(4):
                ch = g4 * 4 + j
                k.op("pe", lambda e: e.transpose(ps[:, j * 128:(j + 1) * 128],
                                                 xn[:, ch * 128:(ch + 1) * 128], c["idf"][:]),
                     reads=xn.all + c["idf"].all, writes=ps.all)
            for j in range(4):
                ch = g4 * 4 + j
                if j % 2 == 0:
                    k.op("dve", lambda e: e.tensor_scalar(hT[:, ch, col0 + i * 128:col0 + (i + 1) * 128],
                                                          ps[:, j * 128:(j + 1) * 128],
                                                          gcol[:, ch:ch + 1], None, op0=ALU.mult),
                         reads=ps.all + gcol.all, writes=[hparts[i]])
                else:
                    k.op("act", lambda e: e.activation(hT[:, ch, col0 + i * 128:col0 + (i + 1) * 128],
                                                       ps[:, j * 128:(j + 1) * 128],
                                                       AF.Copy, scale=gcol[:, ch:ch + 1]),
                         reads=ps.all + gcol.all, writes=[hparts[i]])


def norm_scratch(k, ph, n):
    scr = dict(sq=k.sb(ph, [128, 1024], F32, "sq"), ss=k.sb(ph, [128, n], F32, "ss"),
               rstd=k.sb(ph, [128, n], F32, "rstd"),
               xn=[k.sb(ph, [128, 1024], F32, "xn%d" % i) for i in range(2)],
               eps=k.sb(ph, [128, 1], F32, "epsc"))
    k.op("pool", lambda e: e.memset(scr["eps"][:], RMS_EPS), writes=scr["eps"].all)
    return scr


def phase_ffn(k, c, X, Xout, g2_ap, w_in_ap, w_out_ap, S):
    nc = k.nc
    TB = 1024
    nblk = S // TB
    with contextlib.ExitStack() as ph:
        gcol = load_vec_col(k, ph, g2_ap, 8, "g2col")
        wout = k.sb(ph, [128, 22, 1024], BF16, "wout")
        k.dma("pool", wout[:], w_out_ap.rearrange("(c p) m -> p c m", p=128), writes=wout.all)
        hT = k.sb(ph, [128, 8, TB], BF16, "hT", parts=8)
        actT = k.sb(ph, [128, 22, TB], BF16, "actT", parts=44)
        xt = [k.sb(ph, [128, 1024], F32, "xt%d" % i) for i in range(8)]
        scr = norm_scratch(k, ph, 8)
        wg = [k.sb(ph, [128, 8, 256], BF16, "wg%d" % i) for i in range(2)]
        wu = [k.sb(ph, [128, 8, 256], BF16, "wu%d" % i) for i in range(2)]
        sg = [k.sb(ph, [128, 512], F32, "sg%d" % i) for i in range(2)]
        xo = [k.sb(ph, [128, 1024], F32, "xo%d" % i) for i in range(2)]
        w_in_v = w_in_ap.rearrange("(c p) m -> p c m", p=128)
        for blk in range(nblk):
            t0 = blk * TB
            for tt in range(TB // 128):
                k.dma("sp", xt[tt][:], X[t0 + tt * 128:t0 + (tt + 1) * 128, :], writes=xt[tt].all)
            norm_block(k, c, xt, gcol, hT, hT.parts, scr, psbase=0)
            it = 0
            for j in range(11):
                b = j % 2
                k.dma("pool", wg[b][:], w_in_v[:, :, j * 256:(j + 1) * 256], writes=wg[b].all)
                k.dma("pool", wu[b][:], w_in_v[:, :, FFN_H + j * 256:FFN_H + (j + 1) * 256], writes=wu[b].all)
                for half in range(2):
                    mch = j * 2 + half
                    for th in range(TB // 512):
                        pg = k.ps[4 + (it % 2) * 2]
                        pu = k.ps[5 + (it % 2) * 2]
                        s = sg[it % 2]
                        it += 1
                        hreads = [hT.p(th * 4 + i) for i in range(4)]
                        k.mm(pg.p(), [(pg[:, :], wg[b][:, kc, half * 128:(half + 1) * 128],
                                       hT[:, kc, th * 512:(th + 1) * 512]) for kc in range(8)],
                             reads=hreads + wg[b].all)
                        k.mm(pu.p(), [(pu[:, :], wu[b][:, kc, half * 128:(half + 1) * 128],
                                       hT[:, kc, th * 512:(th + 1) * 512]) for kc in range(8)],
                             reads=hreads + wu[b].all)
                        k.op("act", lambda e: e.activation(s[:], pg[:, :], AF.Silu),
                             reads=pg.all, writes=s.all)
                        ar = actT.p(mch * 2 + th)
                        k.op("dve", lambda e: e.tensor_tensor(actT[:, mch, th * 512:(th + 1) * 512], s[:], pu[:, :],
                                                              op=ALU.mult),
                             reads=s.all + pu.all, writes=[ar])
            for tt in range(TB // 128):
                x = xt[tt]
                o = xo[tt % 2]
                for chh in range(2):
                    po = k.ps[(tt % 2) * 2 + chh]
                    th = tt // 4
                    k.mm(po.p(), [(po[:, :], actT[:, kc, tt * 128:(tt + 1) * 128],
                                   wout[:, kc, chh * 512:(chh + 1) * 512]) for kc in range(22)],
                         reads=[actT.p(kc * 2 + th) for kc in range(22)] + wout.all)
                    k.op("dve", lambda e: e.tensor_tensor(o[:, chh * 512:(chh + 1) * 512],
                                                          x[:, chh * 512:(chh + 1) * 512], po[:, :], op=ALU.add),
                         reads=x.all + po.all, writes=o.all)
                k.dma("sp", Xout[t0 + tt * 128:t0 + (tt + 1) * 128, :], o[:], reads=o.all)
        k.barrier()


def phase_final_norm(k, c, X, OUT, g_ap, S):
    with contextlib.ExitStack() as ph:
        gb = load_vec_bcast(k, ph, g_ap, 1024, "gfin")
        epsc = k.sb(ph, [128, 1], F32, "epsf")
        k.op("pool", lambda e: e.memset(epsc[:], RMS_EPS), writes=epsc.all)
        xt = [k.sb(ph, [128, 1024], F32, "fx%d" % i) for i in range(2)]
        sq = [k.sb(ph, [128, 1024], F32, "fsq%d" % i) for i in range(2)]
        ss = [k.sb(ph, [128, 1], F32, "fss%d" % i) for i in range(2)]
        xo = [k.sb(ph, [128, 1024], F32, "fo%d" % i) for i in range(2)]
        for tt in range(S // 128):
            b = tt % 2
            x, q, s, o = xt[b], sq[b], ss[b], xo[b]
            k.dma("sp", x[:], X[tt * 128:(tt + 1) * 128, :], writes=x.all)
            k.op("act", lambda e: e.activation(q[:], x[:], AF.Square, accum_out=s[:]),
                 reads=x.all, writes=q.all + s.all)
            k.op("act", lambda e: e.activation(s[:], s[:], AF.Ln, scale=1.0 / D, bias=epsc[:, 0:1]),
                 reads=s.all + epsc.all, writes=s.all)
            k.op("act", lambda e: e.activation(s[:], s[:], AF.Exp, scale=-0.5),
                 reads=s.all, writes=s.all)
            k.op("dve", lambda e: e.scalar_tensor_tensor(o[:], x[:], s[:, 0:1], gb[:], op0=ALU.mult, op1=ALU.mult),
                 reads=x.all + s.all + gb.all, writes=o.all)
            k.dma("pool", OUT[tt * 128:(tt + 1) * 128, :], o[:], reads=o.all)
        k.barrier()


def phase_proj(k, c, X, g1_ap, W_ap, outs, S):
    with contextlib.ExitStack() as ph:
        gcol = load_vec_col(k, ph, g1_ap, 8, "g1col")
        hT = k.sb(ph, [128, 8, S], BF16, "hnT", parts=S // 128)
        xt = [k.sb(ph, [128, 1024], F32, "pxt%d" % i) for i in range(8)]
        scr = norm_scratch(k, ph, 8)
        for blk in range(S // 1024):
            t0 = blk * 1024
            for tt in range(8):
                k.dma("sp", xt[tt][:], X[t0 + tt * 128:t0 + (tt + 1) * 128, :], writes=xt[tt].all)
            norm_block(k, c, xt, gcol, hT, hT.parts[blk * 8:(blk + 1) * 8], scr, psbase=0, col0=t0)
        wsl = [k.sb(ph, [128, 8, 512], BF16, "wsl%d" % i) for i in range(2)]
        stg = {}
        W_v = W_ap.rearrange("(c p) m -> p c m", p=128)
        si = 0
        ei = 0
        for o in outs:
            c0, n, mode, dst, dt = o["c0"], o["n"], o["mode"], o["dst"], o["dt"]
            for s0 in range(0, n, 512):
                sn = min(512, n - s0)
                w = wsl[si % 2]
                si += 1
                k.dma("pool", w[:, :, 0:sn], W_v[:, :, c0 + s0:c0 + s0 + sn], writes=w.all)
                if mode == "fm":
                    for m0 in range(0, sn, 128):
                        mn = min(128, sn - m0)
                        key = ("fm", dt)
                        if key not in stg:
                            stg[key] = [[k.sb(ph, [128, S], dt, "stgfm%d" % i) for i in range(2)], 0]
                        st = stg[key][0][stg[key][1] % 2]
                        stg[key][1] += 1
                        for tb in range(S // 512):
                            ps = k.ps[4 + ei % 4]
                            k.mm(ps.p(), [(ps[0:mn, :], w[:, kc, m0:m0 + mn], hT[:, kc, tb * 512:(tb + 1) * 512])
                                          for kc in range(8)],
                                 reads=hT.parts[tb * 4:(tb + 1) * 4] + w.all)
                            if ei % 2 == 0:
                                k.op("dve", lambda e: e.tensor_copy(st[0:mn, tb * 512:(tb + 1) * 512], ps[0:mn, :]),
                                     reads=ps.all, writes=st.all)
                            else:
                                k.op("act", lambda e: e.activation(st[0:mn, tb * 512:(tb + 1) * 512], ps[0:mn, :],
                                                                   AF.Copy),
                                     reads=ps.all, writes=st.all)
                            ei += 1
                        k.dma("sp", dst[s0 + m0:s0 + m0 + mn, :], st[0:mn, :], reads=st.all)
                else:
                    key = ("tm", dt)
                    if key not in stg:
                        stg[key] = [[k.sb(ph, [128, 512], dt, "stgtm%d" % i) for i in range(4)], 0]
                    for tt in range(S // 128):
                        st = stg[key][0][stg[key][1] % 4]
                        stg[key][1] += 1
                        ps = k.ps[4 + ei % 4]
                        k.mm(ps.p(), [(ps[:, 0:sn], hT[:, kc, tt * 128:(tt + 1) * 128], w[:, kc, 0:sn])
                                      for kc in range(8)],
                             reads=[hT.p(tt)] + w.all)
                        if ei % 2 == 0:
                            k.op("dve", lambda e: e.tensor_copy(st[:, 0:sn], ps[:, 0:sn]), reads=ps.all, writes=st.all)
                        else:
                            k.op("act", lambda e: e.activation(st[:, 0:sn], ps[:, 0:sn], AF.Copy),
                                 reads=ps.all, writes=st.all)
                        ei += 1
                        k.dma("sp", dst[tt * 128:(tt + 1) * 128, s0:s0 + sn], st[:, 0:sn], reads=st.all)
        k.barrier()


def phase_oproj(k, c, X, Xout, MIX, wo_ap, S):
    with contextlib.ExitStack() as ph:
        wo = k.sb(ph, [128, 8, 1024], BF16, "wo")
        k.dma("pool", wo[:], wo_ap.rearrange("(c p) m -> p c m", p=128), writes=wo.all)
        mt = [k.sb(ph, [128, 1024], F32, "mixt%d" % i) for i in range(2)]
        mT = [k.sb(ph, [128, 8, 128], BF16, "mixT%d" % i) for i in range(2)]
        xt = [k.sb(ph, [128, 1024], F32, "oxt%d" % i) for i in range(2)]
        xo = [k.sb(ph, [128, 1024], F32, "oxo%d" % i) for i in range(2)]
        for tt in range(S // 128):
            b = tt % 2
            k.dma("sp", mt[b][:], MIX[tt * 128:(tt + 1) * 128, :], writes=mt[b].all)
            k.dma("sp", xt[b][:], X[tt * 128:(tt + 1) * 128, :], writes=xt[b].all)
            for g4 in range(2):
                ps = k.ps[b * 2 + g4]
                for j in range(4):
                    ch = g4 * 4 + j
                    k.op("pe", lambda e: e.transpose(ps[:, j * 128:(j + 1) * 128],
                                                     mt[b][:, ch * 128:(ch + 1) * 128], c["idf"][:]),
                         reads=mt[b].all + c["idf"].all, writes=ps.all)
                if g4 == 0:
                    k.op("dve", lambda e: e.tensor_copy(mT[b][:, 0:4, :],
                                                        ps[:, :].rearrange("p (c t) -> p c t", c=4)),
                         reads=ps.all, writes=mT[b].all)
                else:
                    k.op("act", lambda e: e.activation(mT[b][:, 4:8, :],
                                                       ps[:, :].rearrange("p (c t) -> p c t", c=4), AF.Copy),
                         reads=ps.all, writes=mT[b].all)
            for chh in range(2):
                po = k.ps[4 + b * 2 + chh]
                k.mm(po.p(), [(po[:, :], mT[b][:, kc, :], wo[:, kc, chh * 512:(chh + 1) * 512]) for kc in range(8)],
                     reads=mT[b].all + wo.all)
                k.op("dve", lambda e: e.tensor_tensor(xo[b][:, chh * 512:(chh + 1) * 512],
                                                      xt[b][:, chh * 512:(chh + 1) * 512], po[:, :], op=ALU.add),
                     reads=xt[b].all + po.all, writes=xo[b].all)
            k.dma("sp", Xout[tt * 128:(tt + 1) * 128, :], xo[b][:], reads=xo[b].all)
        k.barrier()


def alibi_slopes12():
    def pow2(m):
        start = 2.0 ** (-8.0 / m)
        return [start ** (i + 1) for i in range(m)]
    s = pow2(8)
    s = s + pow2(16)[0::2][:4]
    return [float(np.float32(v)) for v in s]


class AttnBufs:
    def __init__(self, k, ph, n=3):
        self.n = n
        self.u = [k.sb(ph, [128, 512], F32, "au%d" % i) for i in range(n)]
        self.E = [k.sb(ph, [128, 512], BF16, "aE%d" % i) for i in range(n)]
        self.i = 0


def score_unit(k, ab, mms, reads, Dt=None, Dres=(), slope=0.0, bias=0.0):
    i = ab.i % ab.n
    ab.i += 1
    ps, u, E = k.ps[i], ab.u[i], ab.E[i]
    k.mm(ps.p(), [(ps[:, :], l, r) for (l, r) in mms], reads=reads)
    if Dt is not None:
        k.op("dve", lambda e: e.scalar_tensor_tensor(u[:], Dt, float(slope), ps[:, :], op0=ALU.mult, op1=ALU.add),
             reads=ps.all + list(Dres), writes=u.all)
        k.op("act", lambda e: e.activation(E[:], u[:], AF.Exp, scale=0.125, bias=float(bias)),
             reads=u.all, writes=E.all)
    else:
        k.op("act", lambda e: e.activation(E[:], ps[:, :], AF.Exp, scale=0.125), reads=ps.all, writes=E.all)
    return E


SKIP_EXP = 150.0


def run_pipeline(units, score_fn, pv_fn, la=2):
    Es = [None] * len(units)
    for i in range(len(units) + la):
        if i < len(units):
            Es[i] = score_fn(units[i])
        j = i - la
        if j >= 0:
            pv_fn(units[j], Es[j])
            Es[j] = None


def pv_accum(k, acc, accv, E, rhs, rhs_res, first, last, nsub=4):
    mms = []
    for qs in range(nsub):
        mms.append((accv[:, qs, :], E[:, qs * 128:(qs + 1) * 128], rhs,
                    bool(first and qs == 0), bool(last and qs == nsub - 1)))
    k.mmx(acc.p(), mms, reads=E.all + list(rhs_res))


def phase_memattn(k, c, MEM, gm_ap, wkv_ap, QMT, MIX, S):
    with contextlib.ExitStack() as ph:
        gcol = load_vec_col(k, ph, gm_ap, 8, "gmcol")
        mt = [k.sb(ph, [128, 1024], F32, "memt%d" % i) for i in range(2)]
        scr = norm_scratch(k, ph, 2)
        mnT = k.sb(ph, [128, 8, 256], BF16, "mnT", parts=2)
        for i in range(2):
            k.dma("sp", mt[i][:], MEM[i * 128:(i + 1) * 128, :], writes=mt[i].all)
        norm_block(k, c, mt, gcol, mnT, mnT.parts, scr, psbase=4)
        wkv = k.sb(ph, [128, 8, 512], BF16, "wkv")
        k.dma("pool", wkv[:], wkv_ap.rearrange("(c p) m -> p c m", p=128), writes=wkv.all)
        kmT = k.sb(ph, [64, 4, 256], BF16, "kmT")
        vm = k.sb(ph, [128, 2, 4, 65], BF16, "vmaug")
        k.op("pool", lambda e: e.memset(vm[:], 1.0), writes=vm.all)
        for h in range(4):
            ps = k.ps[4 + h % 2]
            k.mm(ps.p(), [(ps[0:64, 0:256], wkv[:, kc, h * 64:(h + 1) * 64], mnT[:, kc, :]) for kc in range(8)],
                 reads=mnT.all + wkv.all)
            k.op("dve", lambda e: e.tensor_copy(kmT[:, h, :], ps[0:64, 0:256]), reads=ps.all, writes=kmT.all)
        for m2 in range(2):
            ps = k.ps[6 + m2]
            k.mm(ps.p(), [(ps[:, 0:256], mnT[:, kc, m2 * 128:(m2 + 1) * 128], wkv[:, kc, 256:512])
                          for kc in range(8)], reads=mnT.all + wkv.all)
            k.op("dve", lambda e: e.tensor_copy(vm[:, m2, :, 0:64],
                                                ps[:, 0:256].rearrange("p (h d) -> p h d", h=4)),
                 reads=ps.all, writes=vm.all)
        ab = AttnBufs(k, ph)
        qT = [k.sb(ph, [64, S], BF16, "qmT%d" % i) for i in range(2)]
        cross = k.sb(ph, [128, S // 128, 256], F32, "crossall")
        rec = k.sb(ph, [128, 4, 1], F32, "mrec")
        ai = 0
        for h in range(4):
            q = qT[h % 2]
            k.dma("sp", q[:], QMT[h * 64:(h + 1) * 64, :], writes=q.all)
            for qt in range(S // 512):
                acc = k.ps[3 + ai % 2]
                ai += 1
                accv = acc[:, 0:260].rearrange("p (s w) -> p s w", s=4)
                for m2 in range(2):
                    E = score_unit(k, ab, [(kmT[:, h, m2 * 128:(m2 + 1) * 128], q[:, qt * 512:(qt + 1) * 512])],
                                   reads=kmT.all + q.all)
                    pv_accum(k, acc, accv, E, vm[:, m2, h, :], vm.all, first=(m2 == 0), last=(m2 == 1))
                k.op("dve", lambda e: e.reciprocal(rec[:], accv[:, :, 64:65]), reads=acc.all, writes=rec.all)
                k.op("dve", lambda e: e.tensor_tensor(cross[:, qt * 4:(qt + 1) * 4, h * 64:(h + 1) * 64],
                                                      accv[:, :, 0:64], rec[:].to_broadcast([128, 4, 64]),
                                                      op=ALU.mult),
                     reads=acc.all + rec.all, writes=cross.all)
        k.dma("sp", MIX[:, 768:1024].rearrange("(t p) c -> p t c", p=128), cross[:], reads=cross.all)
        k.barrier()


NEGBIG = -1.0e9


def phase_nsa(k, c, sc, gate_b_ap, pos_ap, w1_ap, w2_ap, MIX, S, cache=None):
    QT, KCT, VCT, KST, KWT, VS, VW, GL = (sc[n] for n in ("QT", "KCT", "VCT", "KST", "KWT", "VS", "VW", "GL"))
    slopes = alibi_slopes12()
    NQT = S // 512
    NKT = S // 128
    NTT = S // 128
    NCMP = (S - 32) // 16 + 1
    NNT = (NCMP + 127) // 128
    nc = k.nc
    with contextlib.ExitStack() as ph:
        cache = {} if cache is None else cache
        build = "done" not in cache

        def ctile(name, shape, dt):
            t = k.sb(ph, shape, dt, name)
            if not build:
                k.dma("sp", t[:], cache[name], writes=t.all)
            return t

        def cstore(name, t, shape, dt):
            cache[name] = nc.dram_tensor("nsac_" + name, list(shape), dt, kind="Internal").ap()
            k.dma("sp", cache[name], t[:], reads=t.all)

        Dwin = [ctile("Dwin%d" % j, [128, 512], F32) for j in range(8)]
        Dgen = ctile("Dgen", [128, 512], F32)
        Dcm = {}
        for qt in range(NQT):
            for nt in range(NNT):
                base = 512 * qt - 2048 * nt - 31
                if base + 511 < 0:
                    continue
                Dcm[(qt, nt)] = ctile("Dcm%d_%d" % (qt, nt), [128, 512], F32)
        ovl = ctile("ovl", [128, NNT, 64], F32)
        keepG = ctile("keepG", [128, 128], F32)
        addG = ctile("addG", [128, 128], F32)
        Ex = ctile("Ex", [64, NKT, 128], BF16)
        if build:
            for j in range(8):
                koff = 128 * j - 512
                t = Dwin[j]
                k.op("pool", lambda e: e.iota(t[:], pattern=[[1, 512]], base=-koff, channel_multiplier=-1,
                                              allow_small_or_imprecise_dtypes=True), writes=t.all)
                k.op("pool", lambda e: e.tensor_scalar(t[:], t[:], -8.0, None, op0=ALU.mult), reads=t.all, writes=t.all)
                k.op("pool", lambda e: e.affine_select(t[:], t[:], pattern=[[1, 512]], compare_op=ALU.is_ge,
                                                       fill=NEGBIG, base=-koff, channel_multiplier=-1),
                     reads=t.all, writes=t.all)
                k.op("pool", lambda e: e.affine_select(t[:], t[:], pattern=[[-1, 512]], compare_op=ALU.is_ge,
                                                       fill=NEGBIG, base=511 + koff, channel_multiplier=1),
                     reads=t.all, writes=t.all)
            k.op("pool", lambda e: e.iota(Dgen[:], pattern=[[1, 512]], base=0, channel_multiplier=-1,
                                          allow_small_or_imprecise_dtypes=True), writes=Dgen.all)
            k.op("pool", lambda e: e.tensor_scalar(Dgen[:], Dgen[:], -8.0, None, op0=ALU.mult),
                 reads=Dgen.all, writes=Dgen.all)
            for (qt, nt), t in Dcm.items():
                base = 512 * qt - 2048 * nt - 31
                k.op("pool", lambda e: e.iota(t[:], pattern=[[1, 512]], base=base, channel_multiplier=-16,
                                              allow_small_or_imprecise_dtypes=True), writes=t.all)
                k.op("pool", lambda e: e.tensor_scalar(t[:], t[:], -8.0, None, op0=ALU.mult),
                     reads=t.all, writes=t.all)
                k.op("pool", lambda e: e.affine_select(t[:], t[:], pattern=[[1, 512]], compare_op=ALU.is_ge,
                                                       fill=NEGBIG, base=base, channel_multiplier=-16),
                     reads=t.all, writes=t.all)
            k.op("pool", lambda e: e.memset(ovl[:], 1.0), writes=ovl.all)
            k.op("pool", lambda e: e.affine_select(ovl[:], ovl[:], pattern=[[128, NNT], [-4, 64]], compare_op=ALU.is_ge,
                                                   fill=0.0, base=1, channel_multiplier=1), reads=ovl.all, writes=ovl.all)
            k.op("pool", lambda e: e.affine_select(ovl[:], ovl[:], pattern=[[-128, NNT], [4, 64]], compare_op=ALU.is_ge,
                                                   fill=0.0, base=3, channel_multiplier=-1), reads=ovl.all, writes=ovl.all)
            k.op("pool", lambda e: e.memset(keepG[:], 1.0), writes=keepG.all)
            k.op("pool", lambda e: e.memset(addG[:], 1.0e4), writes=addG.all)
            for hp in range(2):
                rows = slice(hp * 64, hp * 64 + 64)
                k.op("pool", lambda e: e.affine_select(keepG[rows, :], keepG[rows, :], pattern=[[-1, 128]],
                                                       compare_op=ALU.is_ge, fill=0.0, base=62 + hp,
                                                       channel_multiplier=0), reads=keepG.all, writes=keepG.all)
                k.op("pool", lambda e: e.affine_select(addG[rows, :], addG[rows, :], pattern=[[1, 128]],
                                                       compare_op=ALU.is_ge, fill=0.0, base=-63 - hp,
                                                       channel_multiplier=0), reads=addG.all, writes=addG.all)
                k.op("pool", lambda e: e.affine_select(addG[rows, :], addG[rows, :], pattern=[[-1, 128]],
                                                       compare_op=ALU.is_ge, fill=0.0, base=64 + hp,
                                                       channel_multiplier=0), reads=addG.all, writes=addG.all)
            tmpst = contextlib.ExitStack()
            Exf = k.sb(tmpst, [64, NKT * 128], F32, "Exf")
            k.op("pool", lambda e: e.memset(Exf[:], 1.0), writes=Exf.all)
            k.op("pool", lambda e: e.affine_select(Exf[:].rearrange("p (a b c) -> p a b c", b=2, c=64),
                                                   Exf[:].rearrange("p (a b c) -> p a b c", b=2, c=64),
                                                   pattern=[[-2, NKT], [-1, 2], [0, 64]], compare_op=ALU.is_equal,
                                                   fill=0.0, base=0, channel_multiplier=1),
                 reads=Exf.all, writes=Exf.all)
            k.op("dve", lambda e: e.tensor_copy(Ex[:], Exf[:].rearrange("p (a c) -> p a c", c=128)),
                 reads=Exf.all, writes=Ex.all)
            k.barrier()
            tmpst.close()
            for j in range(8):
                cstore("Dwin%d" % j, Dwin[j], [128, 512], F32)
            cstore("Dgen", Dgen, [128, 512], F32)
            for (qt, nt), t in Dcm.items():
                cstore("Dcm%d_%d" % (qt, nt), t, [128, 512], F32)
            cstore("ovl", ovl, [128, NNT, 64], F32)
            cstore("keepG", keepG, [128, 128], F32)
            cstore("addG", addG, [128, 128], F32)
            cstore("Ex", Ex, [64, NKT, 128], BF16)
            cache["done"] = True
        gates = k.sb(ph, [128, NTT, 36], F32, "gates")
        gb = load_vec_bcast(k, ph, gate_b_ap, 36, "gateb")
        k.dma("sp", gates[:], GL.rearrange("(t p) c -> p t c", p=128), writes=gates.all)
        k.op("dve", lambda e: e.tensor_tensor(gates[:], gates[:], gb[:].rearrange("p (o c) -> p o c", o=1)
                                              .to_broadcast([128, NTT, 36]), op=ALU.add),
             reads=gates.all + gb.all, writes=gates.all)
        k.op("act", lambda e: e.activation(gates[:], gates[:], AF.Exp, scale=-1.0), reads=gates.all, writes=gates.all)
        k.op("dve", lambda e: e.tensor_scalar(gates[:], gates[:], 1.0, None, op0=ALU.add),
             reads=gates.all, writes=gates.all)
        k.op("dve", lambda e: e.reciprocal(gates[:], gates[:]), reads=gates.all, writes=gates.all)
        w1 = k.sb(ph, [64, 2, 32, 128], BF16, "cw1")
        w2 = k.sb(ph, [128, 2, 64], BF16, "cw2")
        posT = k.sb(ph, [64, 2, 32], BF16, "cposT")
        for kv in range(2):
            k.dma("pool", w1[:, kv, :, :], w1_ap[kv].rearrange("l d e -> d l e"), writes=w1.all)
            k.dma("pool", w2[:, kv, :], w2_ap[kv], writes=w2.all)
        with nc.allow_non_contiguous_dma(reason="tiny pos table"):
            k.dma("pool", posT[:], pos_ap.rearrange("v l d -> d v l"), writes=posT.all)
        cbias = k.sb(ph, [128, 2], F32, "cbias")
        for kv in range(2):
            ps = k.ps[6]
            k.mm(ps.p(), [(ps[:, 0:1], w1[:, kv, l, :], posT[:, kv, l:l + 1]) for l in range(32)],
                 reads=w1.all + posT.all)
            k.op("dve", lambda e: e.tensor_copy(cbias[:, kv:kv + 1], ps[:, 0:1]), reads=ps.all, writes=cbias.all)
        qT = k.sb(ph, [64, 3, S], BF16, "qT")
        ksT = k.sb(ph, [64, S], BF16, "ksT")
        kwT = k.sb(ph, [64, S], BF16, "kwT")
        cT = [k.sb(ph, [64, S], BF16, "cT%d" % i) for i in range(2)]
        vs = k.sb(ph, [128, NKT, 65], BF16, "vsaug")
        vw = k.sb(ph, [128, NKT, 65], BF16, "vwaug")
        k.op("pool", lambda e: e.memset(vs[:], 1.0), writes=vs.all)
        k.op("pool", lambda e: e.memset(vw[:], 1.0), writes=vw.all)
        kcbT = k.sb(ph, [64, NNT * 128], BF16, "kcbT")
        vcb = k.sb(ph, [128, NNT, 129], BF16, "vcbaug")
        k.op("pool", lambda e: e.memset(kcbT[:], 0.0), writes=kcbT.all)
        k.op("pool", lambda e: e.memset(vcb[:], 0.0), writes=vcb.all)
        k.op("pool", lambda e: e.memset(vcb[:, :, 128:129], 1.0), writes=vcb.all)
        k.op("dve", lambda e: e.tensor_copy(vcb[:, :, 64:128], ovl[:]), reads=ovl.all, writes=vcb.all)
        hx = k.sb(ph, [128, 256], F32, "hx")
        hy = k.sb(ph, [128, 256], F32, "hy")
        hact = k.sb(ph, [128, 256], BF16, "hact")
        k.op("pool", lambda e: e.memset(hact[:], 0.0), writes=hact.all)
        ab = AttnBufs(k, ph)
        oc = k.sb(ph, [128, 4, 3, 64], F32, "oc")
        imp = k.sb(ph, [128, 4, 64], F32, "imp")
        rec = k.sb(ph, [128, 4, 1], F32, "nrec")
        gs = k.sb(ph, [128, 4, 1], F32, "ngs")
        scq = k.sb(ph, [128, 4, 64], F32, "scq")
        wk = k.sb(ph, [128, 4, 64], F32, "wk")
        m8 = k.sb(ph, [128, 8], F32, "m8")
        nmT = k.sb(ph, [64, 512], BF16, "nmT")
        t1 = k.sb(ph, [128, 4, 64], F32, "t1")
        mixt = [k.sb(ph, [128, 4, 192], F32, "mixt%d" % i) for i in range(2)]
        mi = 0
        NC1 = NCMP
        for g in range(4):
            for r in range(3):
                k.dma("sp", qT[:, r, :], QT[(g * 3 + r) * 64:(g * 3 + r + 1) * 64, :], writes=qT.all)
            k.dma("sp", ksT[:], KST[g * 64:(g + 1) * 64, :], writes=ksT.all)
            k.dma("sp", kwT[:], KWT[g * 64:(g + 1) * 64, :], writes=kwT.all)
            k.dma("sp", cT[0][:], KCT[g * 64:(g + 1) * 64, :], writes=cT[0].all)
            k.dma("sp", cT[1][:], VCT[g * 64:(g + 1) * 64, :], writes=cT[1].all)
            with nc.allow_non_contiguous_dma(reason="v head slice"):
                k.dma("sp", vs[:, :, 0:64], VS[:, g * 64:(g + 1) * 64].rearrange("(t p) d -> p t d", p=128),
                      writes=vs.all)
                k.dma("sp", vw[:, :, 0:64], VW[:, g * 64:(g + 1) * 64].rearrange("(t p) d -> p t d", p=128),
                      writes=vw.all)
            for kv in range(2):
                src3 = cT[kv][:].rearrange("p (n s) -> p n s", s=16)
                ps = k.ps[6]
                k.mm(ps.p(), [(ps[:, 0:NC1], w1[:, kv, l, :], src3[:, (l // 16):(l // 16) + NC1, l % 16])
                              for l in range(32)], reads=w1.all + cT[kv].all)
                k.op("dve", lambda e: e.tensor_scalar(hx[:, 0:NC1], ps[:, 0:NC1], cbias[:, kv:kv + 1], None,
                                                      op0=ALU.add), reads=ps.all + cbias.all, writes=hx.all)
                k.op("dve", lambda e: e.tensor_tensor(hy[:, 0:NC1], hx[:, 0:NC1], hx[:, 0:NC1], op=ALU.mult),
                     reads=hx.all, writes=hy.all)
                k.op("dve", lambda e: e.tensor_scalar(hy[:, 0:NC1], hy[:, 0:NC1], 0.044715, 1.0,
                                                      op0=ALU.mult, op1=ALU.add), reads=hy.all, writes=hy.all)
                k.op("dve", lambda e: e.tensor_tensor(hy[:, 0:NC1], hy[:, 0:NC1], hx[:, 0:NC1], op=ALU.mult),
                     reads=hy.all + hx.all, writes=hy.all)
                k.op("act", lambda e: e.activation(hy[:, 0:NC1], hy[:, 0:NC1], AF.Exp, scale=-1.5957691216057308),
                     reads=hy.all, writes=hy.all)
                k.op("dve", lambda e: e.tensor_scalar(hy[:, 0:NC1], hy[:, 0:NC1], 1.0, None, op0=ALU.add),
                     reads=hy.all, writes=hy.all)
                k.op("dve", lambda e: e.reciprocal(hy[:, 0:NC1], hy[:, 0:NC1]), reads=hy.all, writes=hy.all)
                k.op("dve", lambda e: e.tensor_tensor(hact[:, 0:NC1], hx[:, 0:NC1], hy[:, 0:NC1], op=ALU.mult),
                     reads=hy.all + hx.all, writes=hact.all)
                if kv == 0:
                    ps2 = k.ps[7]
                    k.mm(ps2.p(), [(ps2[0:64, 0:NC1], w2[:, 0, :], hact[:, 0:NC1])], reads=w2.all + hact.all)
                    k.op("dve", lambda e: e.tensor_copy(kcbT[:, 0:NC1], ps2[0:64, 0:NC1]),
                         reads=ps2.all, writes=kcbT.all)
                else:
                    for nt in range(NNT):
                        nn = min(128, NC1 - nt * 128)
                        ps2 = k.ps[7]
                        k.mm(ps2.p(), [(ps2[0:nn, 0:64], hact[:, nt * 128:nt * 128 + nn], w2[:, 1, :])],
                             reads=w2.all + hact.all)
                        k.op("dve", lambda e: e.tensor_copy(vcb[0:nn, nt, 0:64], ps2[0:nn, 0:64]),
                             reads=ps2.all, writes=vcb.all)
            for qt in range(NQT):
                q0 = qt * 512
                accA, accB = k.ps[3], k.ps[4]
                vA = accA[:, 0:258].rearrange("p (s w) -> p s w", s=2)
                vB = accB[:, 0:258].rearrange("p (s w) -> p s w", s=2)
                nts = [nt for nt in range(NNT) if (qt, nt) in Dcm]

                def cmp_post(r):
                    for half, (acc, av) in enumerate(((accA, vA), (accB, vB))):
                        sl = slice(half * 2, half * 2 + 2)
                        k.op("dve", lambda e: e.tensor_scalar(rec[:, sl, :], av[:, :, 128:129], 1e-30, None,
                                                              op0=ALU.add), reads=acc.all, writes=rec.all)
                        k.op("dve", lambda e: e.reciprocal(rec[:, sl, :], rec[:, sl, :]), reads=rec.all, writes=rec.all)
                        k.op("dve", lambda e: e.tensor_tensor(oc[:, sl, r, :], av[:, :, 0:64],
                                                              rec[:, sl, :].to_broadcast([128, 2, 64]), op=ALU.mult),
                             reads=acc.all + rec.all, writes=oc.all)
                        if r == 0:
                            k.op("dve", lambda e: e.tensor_tensor(imp[:, sl, :], av[:, :, 64:128],
                                                                  rec[:, sl, :].to_broadcast([128, 2, 64]),
                                                                  op=ALU.mult),
                                 reads=acc.all + rec.all, writes=imp.all)
                        else:
                            k.op("dve", lambda e: e.tensor_tensor(wk[:, sl, :], av[:, :, 64:128],
                                                                  rec[:, sl, :].to_broadcast([128, 2, 64]),
                                                                  op=ALU.mult),
                                 reads=acc.all + rec.all, writes=wk.all)
                            k.op("dve", lambda e: e.tensor_tensor(imp[:, sl, :], imp[:, sl, :], wk[:, sl, :],
                                                                  op=ALU.add),
                                 reads=imp.all + wk.all, writes=imp.all)

                units = []
                for r in range(3):
                    for ii, nt in enumerate(nts):
                        units.append(dict(r=r, nt=nt, first=(ii == 0), last=(ii == len(nts) - 1)))

                def cmp_score(u):
                    Dt = Dcm[(qt, u["nt"])]
                    return score_unit(k, ab, [(kcbT[:, u["nt"] * 128:(u["nt"] + 1) * 128], qT[:, u["r"], q0:q0 + 512])],
                                      reads=kcbT.all + qT.all, Dt=Dt[:], Dres=Dt.all, slope=slopes[g * 3 + u["r"]],
                                      bias=0.0)

                def cmp_pv(u, E):
                    nt, first, last = u["nt"], u["first"], u["last"]
                    k.mmx(accA.p(), [(vA[:, qs, :], E[:, qs * 128:(qs + 1) * 128], vcb[:, nt, :],
                                      bool(first and qs == 0), bool(last and qs == 1)) for qs in range(2)],
                          reads=E.all + vcb.all)
                    k.mmx(accB.p(), [(vB[:, qs, :], E[:, (qs + 2) * 128:(qs + 3) * 128], vcb[:, nt, :],
                                      bool(first and qs == 0), bool(last and qs == 1)) for qs in range(2)],
                          reads=E.all + vcb.all)
                    if last:
                        cmp_post(u["r"])
                run_pipeline(units, cmp_score, cmp_pv)
                pst = k.ps[5]
                for qs in range(4):
                    tt = qt * 4 + qs
                    lo = 64 - 2 * tt
                    k.op("dve", lambda e: e.scalar_tensor_tensor(scq[:, qs, :], imp[:, qs, :], 1.0,
                                                                 keepG[:, lo:lo + 64], op0=ALU.add, op1=ALU.mult),
                         reads=imp.all + keepG.all, writes=scq.all)
                    k.op("dve", lambda e: e.tensor_tensor(scq[:, qs, :], scq[:, qs, :], addG[:, lo:lo + 64],
                                                          op=ALU.add), reads=scq.all + addG.all, writes=scq.all)
                    k.op("dve", lambda e: e.memset(scq[:, qs, 0:1], 1.0e4), reads=scq.all, writes=scq.all)
                    k.op("dve", lambda e: e.max(m8[:], scq[:, qs, :]), reads=scq.all, writes=m8.all)
                    k.op("dve", lambda e: e.match_replace(wk[:, qs, :], m8[:], scq[:, qs, :], 0.0),
                         reads=scq.all + m8.all, writes=wk.all)
                    k.op("dve", lambda e: e.max(m8[:], wk[:, qs, :]), reads=wk.all, writes=m8.all)
                    k.op("dve", lambda e: e.match_replace(wk[:, qs, :], m8[:], wk[:, qs, :], 0.0),
                         reads=wk.all + m8.all, writes=wk.all)
                    k.op("dve", lambda e: e.tensor_tensor(wk[:, qs, :], scq[:, qs, :], wk[:, qs, :], op=ALU.subtract),
                         reads=wk.all + scq.all, writes=wk.all)
                    k.op("dve", lambda e: e.tensor_scalar(wk[:, qs, :], wk[:, qs, :], 1.0, 1.0,
                                                          op0=ALU.min, op1=ALU.subtract), reads=wk.all, writes=wk.all)
                    k.op("pe", lambda e: e.transpose(pst[0:64, qs * 128:(qs + 1) * 128], wk[:, qs, :], c["idf"][:]),
                         reads=wk.all + c["idf"].all, writes=pst.all)
                k.op("act", lambda e: e.activation(nmT[:], pst[0:64, :], AF.Copy, scale=30000.0),
                     reads=pst.all, writes=nmT.all)
                mx = mixt[mi % 2]
                mi += 1
                accS, accW = k.ps[6], k.ps[7]
                vS = accS[:, 0:260].rearrange("p (s w) -> p s w", s=4)
                vW = accW[:, 0:260].rearrange("p (s w) -> p s w", s=4)
                gsl = slice(qt * 4, qt * 4 + 4)

                def combine(r):
                    h = g * 3 + r
                    k.op("dve", lambda e: e.reciprocal(rec[:], vS[:, :, 64:65]), reads=accS.all, writes=rec.all)
                    k.op("dve", lambda e: e.tensor_tensor(gs[:], rec[:], gates[:, gsl, 12 + h:13 + h], op=ALU.mult),
                         reads=rec.all + gates.all, writes=gs.all)
                    k.op("dve", lambda e: e.tensor_tensor(t1[:], vS[:, :, 0:64], gs[:].to_broadcast([128, 4, 64]),
                                                          op=ALU.mult), reads=accS.all + gs.all, writes=t1.all)
                    k.op("dve", lambda e: e.reciprocal(rec[:], vW[:, :, 64:65]), reads=accW.all, writes=rec.all)
                    k.op("dve", lambda e: e.tensor_tensor(gs[:], rec[:], gates[:, gsl, 24 + h:25 + h], op=ALU.mult),
                         reads=rec.all + gates.all, writes=gs.all)
                    k.op("dve", lambda e: e.tensor_tensor(wk[:], vW[:, :, 0:64], gs[:].to_broadcast([128, 4, 64]),
                                                          op=ALU.mult), reads=accW.all + gs.all, writes=wk.all)
                    k.op("pool", lambda e: e.tensor_tensor(t1[:], t1[:], wk[:], op=ALU.add),
                         reads=t1.all + wk.all, writes=t1.all)
                    k.op("pool", lambda e: e.tensor_tensor(wk[:], oc[:, :, r, :],
                                                           gates[:, gsl, h:h + 1].to_broadcast([128, 4, 64]),
                                                           op=ALU.mult), reads=oc.all + gates.all, writes=wk.all)
                    k.op("pool", lambda e: e.tensor_tensor(mx[:, :, r * 64:(r + 1) * 64], t1[:], wk[:], op=ALU.add),
                         reads=t1.all + wk.all, writes=mx.all)

                units = []
                for r in range(3):
                    sl_h = slopes[g * 3 + r]
                    skts = [kt for kt in range(4 * qt + 4)
                            if kt >= 4 * qt or sl_h * (q0 - (kt * 128 + 127)) < SKIP_EXP]
                    for ii, kt in enumerate(skts):
                        units.append(dict(br="s", r=r, kt=kt, first=(ii == 0), last=(ii == len(skts) - 1), post=False))
                    wkts = [kt for kt in range(4 * qt - 4, 4 * qt + 4) if kt >= 0]
                    for ii, kt in enumerate(wkts):
                        units.append(dict(br="w", r=r, kt=kt, first=(ii == 0), last=(ii == len(wkts) - 1),
                                          post=(ii == len(wkts) - 1)))

                def sw_score(u):
                    r, kt = u["r"], u["kt"]
                    sl_h = slopes[g * 3 + r]
                    if u["br"] == "s":
                        if kt >= 4 * qt:
                            Dt, bias = Dwin[4 + kt - 4 * qt], 0.0
                        else:
                            Dt, bias = Dgen, -sl_h * (q0 - kt * 128)
                        return score_unit(k, ab, [(ksT[:, kt * 128:(kt + 1) * 128], qT[:, r, q0:q0 + 512]),
                                                  (Ex[:, kt, :], nmT[:, :])],
                                          reads=ksT.all + qT.all + Ex.all + nmT.all, Dt=Dt[:], Dres=Dt.all,
                                          slope=sl_h, bias=bias)
                    Dt = Dwin[kt - 4 * qt + 4]
                    return score_unit(k, ab, [(kwT[:, kt * 128:(kt + 1) * 128], qT[:, r, q0:q0 + 512])],
                                      reads=kwT.all + qT.all, Dt=Dt[:], Dres=Dt.all, slope=sl_h, bias=0.0)

                def sw_pv(u, E):
                    if u["br"] == "s":
                        pv_accum(k, accS, vS, E, vs[:, u["kt"], :], vs.all, first=u["first"], last=u["last"])
                    else:
                        pv_accum(k, accW, vW, E, vw[:, u["kt"], :], vw.all, first=u["first"], last=u["last"])
                    if u["post"]:
                        combine(u["r"])
                run_pipeline(units, sw_score, sw_pv)
                with nc.allow_non_contiguous_dma(reason="mix slice"):
                    k.dma("sp", MIX[q0:q0 + 512, g * 192:(g + 1) * 192].rearrange("(s p) c -> p s c", p=128),
                          mx[:], reads=mx.all)
        k.barrier()


GN_EPS = 64e-5
TBK = 256


def phase_rwkv(k, c, ZT, prm, MIX, S, dbg=99):
    nc = k.nc
    NB = S // TBK if dbg >= 99 else 1
    NCH = TBK // 64
    NPR = TBK // 128
    W6 = [128, 6, TBK]

    def V3(t):
        return t[:].rearrange("p (c t) -> p c t", c=6)

    with contextlib.ExitStack() as ph:
        def colvec(name, ap, n=6):
            return load_vec_col(k, ph, ap, n, name)
        mu_c = k.sb(ph, [128, 21], F32, "mu_c")
        with nc.allow_non_contiguous_dma(reason="small vector load"):
            k.dma("sp", mu_c[:, 0:20], prm["mu"][0:2560].rearrange("(c p) -> p c", p=128), writes=mu_c.all)
            k.dma("sp", mu_c[0:32, 20:21], prm["mu"][2560:2592].rearrange("(c p) -> p c", p=32), writes=mu_c.all)
        w0n = colvec("w0n", prm["w0"])
        a0n = colvec("a0n", prm["a0"])
        kk_c = colvec("kk_c", prm["k_k"])
        ka_c = colvec("ka_c", prm["k_a"])
        rk_c = colvec("rk_c", prm["r_k"].rearrange("h n -> (h n)"))
        omka = k.sb(ph, [128, 6], F32, "omka")
        k.op("dve", lambda e: e.tensor_scalar(omka[:], ka_c[:], -1.0, 1.0, op0=ALU.mult, op1=ALU.add),
             reads=ka_c.all, writes=omka.all)
        k.op("dve", lambda e: e.tensor_scalar(w0n[:], w0n[:], -1.0, None, op0=ALU.mult), reads=w0n.all, writes=w0n.all)
        k.op("dve", lambda e: e.tensor_scalar(a0n[:], a0n[:], -1.0, None, op0=ALU.mult), reads=a0n.all, writes=a0n.all)
        lnw = load_vec_bcast(k, ph, prm["lnx_w"], 768, "lnw")
        lnb = load_vec_bcast(k, ph, prm["lnx_b"], 768, "lnb")
        w2a2 = k.sb(ph, [128, 768], BF16, "w2a2")
        k.dma("pool", w2a2[0:64, :], prm["w2"], writes=w2a2.all)
        k.dma("pool", w2a2[64:128, :], prm["a2"], writes=w2a2.all)
        g2a = k.sb(ph, [128, 768], BF16, "g2a")
        g2b = k.sb(ph, [32, 768], BF16, "g2b")
        k.dma("pool", g2a[:], prm["g2"][0:128, :], writes=g2a.all)
        k.dma("pool", g2b[:], prm["g2"][128:160, :], writes=g2b.all)
        tiny = k.sb(ph, [128, 1], F32, "tiny")
        k.op("pool", lambda e: e.memset(tiny[:], 1e-24), writes=tiny.all)
        gneps = k.sb(ph, [128, 1], F32, "gneps")
        k.op("pool", lambda e: e.memset(gneps[:], GN_EPS), writes=gneps.all)
        msT = k.sb(ph, [128, 128], F32, "msT")
        miT = k.sb(ph, [128, 128], F32, "miT")
        msL = k.sb(ph, [128, 128], F32, "msL")
        for (t, pat, base, cm) in ((msT, [[1, 128]], -1, -1), (miT, [[1, 128]], 0, -1), (msL, [[-1, 128]], -1, 1)):
            k.op("pool", lambda e: e.memset(t[:], 1.0), writes=t.all)
            k.op("pool", lambda e: e.affine_select(t[:], t[:], pattern=pat, compare_op=ALU.is_ge, fill=0.0,
                                                   base=base, channel_multiplier=cm), reads=t.all, writes=t.all)
            k.op("pool", lambda e: e.memset(t[0:64, 64:128], 0.0), reads=t.all, writes=t.all)
            k.op("pool", lambda e: e.memset(t[64:128, 0:64], 0.0), reads=t.all, writes=t.all)
        MASK1 = k.sb(ph, [128, 2, 256], F32, "MASK1")
        MASK3 = k.sb(ph, [128, 4, 128], F32, "MASK3")
        for hh in range(2):
            k.op("pool", lambda e: e.tensor_copy(MASK1[:, hh, 0:128], msT[:]), reads=msT.all, writes=MASK1.all)
            k.op("pool", lambda e: e.tensor_copy(MASK1[:, hh, 128:256], miT[:]), reads=miT.all, writes=MASK1.all)
        for hh in range(4):
            k.op("pool", lambda e: e.tensor_copy(MASK3[:, hh, :], msL[:]), reads=msL.all, writes=MASK3.all)
        bones = k.sb(ph, [128, 128], F32, "bones")
        k.op("pool", lambda e: e.memset(bones[:], 1.0), writes=bones.all)
        k.op("pool", lambda e: e.memset(bones[0:64, 64:128], 0.0), reads=bones.all, writes=bones.all)
        k.op("pool", lambda e: e.memset(bones[64:128, 0:64], 0.0), reads=bones.all, writes=bones.all)
        bsel = k.sb(ph, [128, 2], F32, "bsel")
        k.op("pool", lambda e: e.memset(bsel[:], 0.0), writes=bsel.all)
        k.op("pool", lambda e: e.memset(bsel[0:64, 0:1], 1.0), reads=bsel.all, writes=bsel.all)
        k.op("pool", lambda e: e.memset(bsel[64:128, 1:2], 1.0), reads=bsel.all, writes=bsel.all)
        Mscan = k.sb(ph, [128, TBK], F32, "Mscan")
        k.op("pool", lambda e: e.memset(Mscan[:], 1.0), writes=Mscan.all)
        k.op("pool", lambda e: e.memset(Mscan[:].rearrange("p (n s) -> p n s", s=64)[:, :, 0:1], 0.0),
             reads=Mscan.all, writes=Mscan.all)
        ZL = k.sb(ph, [128, 6, TBK + 1], F32, "ZL")
        ZS = k.sb(ph, [128, 3, TBK + 1], F32, "ZS")
        bufs = [k.sb(ph, [128, 6 * TBK], F32, "rb%d" % i) for i in range(7)]
        b1, b2, b3, b4, b5, b6, b7 = bufs
        sm = [k.sb(ph, [128, TBK], F32, "rsm%d" % i) for i in range(3)]
        twz = k.sb(ph, [128, TBK], BF16, "twz")
        sgA = k.sb(ph, [128, TBK], BF16, "sgA")
        sgB = k.sb(ph, [32, TBK], BF16, "sgB")
        CL = k.sb(ph, [128, 6, NCH], F32, "CL")
        PC = k.sb(ph, [128, 6, NCH], F32, "PC")
        AR = k.sb(ph, [128, 6, 2, TBK], BF16, "AR")
        BK = k.sb(ph, [128, 6, 2, TBK], BF16, "BK")
        Vt = k.sb(ph, [128, NPR, 768], BF16, "Vt")
        BBt = k.sb(ph, [128, NPR, 768], BF16, "BBt")
        KBt = k.sb(ph, [128, NPR, 768], BF16, "KBt")
        Gt = k.sb(ph, [128, NPR, 768], F32, "Gt")
        bon = k.sb(ph, [128, NPR, 12], F32, "bon")
        T1 = k.sb(ph, [128, 6, 2, 256], BF16, "T1")
        T2 = k.sb(ph, [128, 6, 2, 256], BF16, "T2")
        X = [k.sb(ph, [128, 3, 128], F32, "nX%d" % i) for i in range(2)]
        XT = [k.sb(ph, [128, 3, 128], F32, "nXT%d" % i) for i in range(2)]
        TT = [k.sb(ph, [128, 3, 128], F32, "nTT%d" % i) for i in range(2)]
        TTb = k.sb(ph, [128, 6, 2, 128], BF16, "TTb")
        U0 = k.sb(ph, [128, 768], F32, "U0")
        ST = k.sb(ph, [128, 6, 64], F32, "ST")
        STb = k.sb(ph, [128, 6, 64], BF16, "STb")
        k.op("pool", lambda e: e.memset(ST[:], 0.0), writes=ST.all)
        k.op("pool", lambda e: e.memset(STb[:], 0.0), writes=STb.all)
        Wsb = k.sb(ph, [128, 768], BF16, "Wsb")
        Usb = k.sb(ph, [128, 768], BF16, "Usb")
        Ysb = k.sb(ph, [128, 768], F32, "Ysb")
        Yw = [k.sb(ph, [128, 768], F32, "Yw%d" % i) for i in range(2)]
        st12 = [k.sb(ph, [128, 12], F32, "st12_%d" % i) for i in range(4)]
        Stmp = k.sb(ph, [128, 6, 64], F32, "Stmp")
        Yo = [k.sb(ph, [128, 768], F32, "Yo%d" % i) for i in range(2)]

        def load_group(dst, row0, nchunks, t0, rows=128):
            src = ZT[row0:row0 + nchunks * 128, :].rearrange("(c p) t -> p c t", p=128) if rows == 128 else None
            if t0 == 0:
                k.op("pool", lambda e: e.memset(dst[0:rows, 0:nchunks, 0:1], 0.0), writes=dst.all)
                k.dma("sp", dst[0:rows, 0:nchunks, 1:TBK + 1], src[:, :, 0:TBK], writes=dst.all)
            else:
                k.dma("sp", dst[0:rows, 0:nchunks, :], src[:, :, t0 - 1:t0 + TBK], writes=dst.all)

        def shift_mix(dst3, src, mu0, nchunks, tmp3, rows=128):
            k.op("dve", lambda e: e.tensor_tensor(tmp3[0:rows, 0:nchunks, :], src[0:rows, 0:nchunks, 0:TBK],
                                                  src[0:rows, 0:nchunks, 1:TBK + 1], op=ALU.subtract),
                 reads=src.all, writes=[tmp3_res[0]])
            for cc in range(nchunks):
                k.op("dve",
                     lambda e: e.scalar_tensor_tensor(dst3[0:rows, cc, :], tmp3[0:rows, cc, :],
                                                      mu_c[0:rows, mu0 + cc:mu0 + cc + 1],
                                                      src[0:rows, cc, 1:TBK + 1], op0=ALU.mult, op1=ALU.add),
                     reads=[tmp3_res[0]] + src.all + mu_c.all, writes=[dst3_res[0]])

        mixq = 0
        for blk in range(NB):
            t0 = blk * TBK
            if t0 == 0:
                k.op("pool", lambda e: e.memset(ZS[:, :, 0:1], 0.0), writes=ZS.all)
            lo = 1 if t0 == 0 else 0
            k.dma("sp", ZS[:, 0:2, lo:TBK + 1],
                  ZT[2304:2560, :].rearrange("(c p) t -> p c t", p=128)[:, :, t0 - 1 + lo:t0 + TBK], writes=ZS.all)
            k.dma("sp", ZS[0:32, 2, lo:TBK + 1], ZT[2560:2592, t0 - 1 + lo:t0 + TBK], writes=ZS.all)
            zsm = sm[0], sm[1], sm[2]
            for cc, rows in ((0, 128), (1, 128), (2, 32)):
                d = zsm[cc]
                k.op("dve", lambda e: e.tensor_tensor(d[0:rows, :], ZS[0:rows, cc, 0:TBK], ZS[0:rows, cc, 1:TBK + 1],
                                                      op=ALU.subtract), reads=ZS.all, writes=d.all)
                k.op("dve", lambda e: e.scalar_tensor_tensor(d[0:rows, :], d[0:rows, :],
                                                             mu_c[0:rows, 18 + cc:19 + cc], ZS[0:rows, cc, 1:TBK + 1],
                                                             op0=ALU.mult, op1=ALU.add),
                     reads=d.all + ZS.all + mu_c.all, writes=d.all)
            zwza = sm[0]
            k.op("act", lambda e: e.activation(zwza[0:64, :], zwza[0:64, :], AF.Exp, scale=2.0),
                 reads=zwza.all, writes=zwza.all)
            k.op("dve", lambda e: e.tensor_scalar(zwza[0:64, :], zwza[0:64, :], 1.0, None, op0=ALU.add),
                 reads=zwza.all, writes=zwza.all)
            k.op("dve", lambda e: e.reciprocal(zwza[0:64, :], zwza[0:64, :]), reads=zwza.all, writes=zwza.all)
            k.op("dve", lambda e: e.tensor_scalar(twz[0:64, :], zwza[0:64, :], -2.0, 1.0, op0=ALU.mult, op1=ALU.add),
                 reads=zwza.all, writes=twz.all)
            k.op("dve", lambda e: e.tensor_copy(twz[64:128, :], zwza[64:128, :]), reads=zwza.all, writes=twz.all)
            for (src_, dst_, rows) in ((sm[1], sgA, 128), (sm[2], sgB, 32)):
                k.op("act", lambda e: e.activation(src_[0:rows, :], src_[0:rows, :], AF.Exp, scale=-1.0),
                     reads=src_.all, writes=src_.all)
                k.op("dve", lambda e: e.tensor_scalar(src_[0:rows, :], src_[0:rows, :], 1.0, None, op0=ALU.add),
                     reads=src_.all, writes=src_.all)
                k.op("dve", lambda e: e.reciprocal(src_[0:rows, :], src_[0:rows, :]), reads=src_.all, writes=src_.all)
                k.op("dve", lambda e: e.tensor_copy(dst_[0:rows, :], src_[0:rows, :]), reads=src_.all, writes=dst_.all)
            if dbg <= 1:
                break
            LW, CUM, A3 = V3(b1), V3(b2), V3(b3)
            for (dst3, dres, rows, bcol) in ((LW, b1, slice(0, 64), w0n), (A3, b3, slice(64, 128), a0n)):
                for fc in range(6):
                    ps = k.ps[fc % 4]
                    k.mm(ps.p(), [(ps[:, 0:TBK], w2a2[rows, fc * 128:(fc + 1) * 128], twz[rows, :])],
                         reads=w2a2.all + twz.all)
                    k.op("act", lambda e: e.activation(dst3[:, fc, :], ps[:, 0:TBK], AF.Exp, scale=-1.0,
                                                       bias=bcol[:, fc:fc + 1]),
                         reads=ps.all + bcol.all, writes=dres.all)
                k.op("dve", lambda e: e.tensor_scalar(dres[:], dres[:], 1.0, None, op0=ALU.add),
                     reads=dres.all, writes=dres.all)
                k.op("dve", lambda e: e.reciprocal(dres[:], dres[:]), reads=dres.all, writes=dres.all)
            k.op("dve", lambda e: e.tensor_scalar(b1[:], b1[:], -0.6065306597126334, None, op0=ALU.mult),
                 reads=b1.all, writes=b1.all)
            for fc in range(6):
                k.op("dve", lambda e: e.tensor_tensor_scan(CUM[:, fc, :], Mscan[:], LW[:, fc, :], 0.0,
                                                           op0=ALU.mult, op1=ALU.add),
                     reads=Mscan.all + b1.all, writes=b2.all)
            CUM4 = b2[:].rearrange("p (c n s) -> p c n s", c=6, s=64)
            k.op("dve", lambda e: e.tensor_copy(CL[:], CUM4[:, :, :, 63]), reads=b2.all, writes=CL.all)
            k.op("act", lambda e: e.activation(PC[:], CL[:], AF.Exp), reads=CL.all, writes=PC.all)
            if dbg <= 2:
                break
            tmp3_res, dst3_res = [b7.p()], [b4.p()]
            load_group(ZL, 1536, 6, t0)
            shift_mix(V3(b4), ZL, 12, 6, V3(b7))
            VM = V3(b4)

            def to_token_major(src3, sres, dstT):
                for pr in range(NPR):
                    for half in range(2):
                        ps = k.ps[4 + (pr * 2 + half) % 4]
                        for j in range(3):
                            fc = half * 3 + j
                            k.op("pe", lambda e: e.transpose(ps[:, j * 128:(j + 1) * 128],
                                                             src3[:, fc, pr * 128:(pr + 1) * 128], c["idf"][:]),
                                 reads=sres.all + c["idf"].all, writes=ps.all)
                        if half == 0:
                            k.op("dve", lambda e: e.tensor_copy(dstT[:, pr, 0:384], ps[:, 0:384]),
                                 reads=ps.all, writes=dstT.all)
                        else:
                            k.op("act", lambda e: e.activation(dstT[:, pr, 384:768], ps[:, 0:384], AF.Copy),
                                 reads=ps.all, writes=dstT.all)
            to_token_major(VM, b4, Vt)
            if dbg <= 3:
                break
            dst3_res = [b4.p()]
            load_group(ZL, 768, 6, t0)
            shift_mix(V3(b4), ZL, 6, 6, V3(b7))
            KM, KK, B3, T7 = V3(b4), V3(b5), V3(b6), V3(b7)
            k.op("dve", lambda e: e.tensor_tensor(KK, KM, kk_c[:].rearrange("p (c o) -> p c o", o=1)
                                                  .to_broadcast(W6), op=ALU.mult),
                 reads=b4.all + kk_c.all, writes=b5.all)
            k.op("pool", lambda e: e.tensor_tensor(B3, KK, KK, op=ALU.mult), reads=b5.all, writes=b6.all)
            for fc in range(6):
                ps = k.ps[fc % 4]
                k.mm(ps.p(), [(ps[:, 0:TBK], bones[:], B3[:, fc, :])], reads=bones.all + b6.all)
                k.op("act", lambda e: e.activation(T7[:, fc, :], ps[:, 0:TBK], AF.Ln, bias=tiny[:, 0:1]),
                     reads=ps.all + tiny.all, writes=b7.all)
            k.op("act", lambda e: e.activation(b7[:], b7[:], AF.Exp, scale=-0.5), reads=b7.all, writes=b7.all)
            k.op("dve", lambda e: e.tensor_tensor(b5[:], b5[:], b7[:], op=ALU.mult),
                 reads=b5.all + b7.all, writes=b5.all)
            for fc in range(6):
                k.op("dve",
                     lambda e: e.tensor_scalar(T7[:, fc, :], A3[:, fc, :], ka_c[:, fc:fc + 1], omka[:, fc:fc + 1],
                                               op0=ALU.mult, op1=ALU.add),
                     reads=b3.all + ka_c.all + omka.all, writes=b7.all)
            k.op("dve", lambda e: e.tensor_tensor(b4[:], b4[:], b7[:], op=ALU.mult),
                 reads=b4.all + b7.all, writes=b4.all)
            k.op("pool", lambda e: e.tensor_tensor(b6[:], b5[:], b3[:], op=ALU.mult),
                 reads=b5.all + b3.all, writes=b6.all)
            if dbg <= 4:
                break
            BK4, AR4 = BK, AR
            k.op("act", lambda e: e.activation(b3[:], b2[:], AF.Exp, scale=-1.0), reads=b2.all, writes=b3.all)
            k.op("dve", lambda e: e.tensor_tensor(BK4[:, :, 0, :], B3, V3(b3), op=ALU.mult),
                 reads=b6.all + b3.all, writes=BK.all)
            k.op("pool", lambda e: e.tensor_tensor(BK4[:, :, 1, :], KM, V3(b3), op=ALU.mult),
                 reads=b4.all + b3.all, writes=BK.all)
            k.op("dve", lambda e: e.scalar_tensor_tensor(
                b3[:].rearrange("p (c n s) -> p c n s", c=6, s=64), CUM4, -1.0,
                CL[:].rearrange("p c (n o) -> p c n o", o=1).to_broadcast([128, 6, NCH, 64]),
                op0=ALU.mult, op1=ALU.add), reads=b2.all + CL.all, writes=b3.all)
            k.op("act", lambda e: e.activation(b3[:], b3[:], AF.Exp), reads=b3.all, writes=b3.all)
            k.op("dve", lambda e: e.tensor_tensor(b6[:], b6[:], b3[:], op=ALU.mult),
                 reads=b6.all + b3.all, writes=b6.all)
            k.op("pool", lambda e: e.tensor_tensor(b7[:], b4[:], b3[:], op=ALU.mult),
                 reads=b4.all + b3.all, writes=b7.all)
            to_token_major(V3(b6), b6, BBt)
            to_token_major(V3(b7), b7, KBt)
            k.op("dve", lambda e: e.tensor_tensor(b3[:], b2[:], b1[:], op=ALU.subtract),
                 reads=b2.all + b1.all, writes=b3.all)
            k.op("act", lambda e: e.activation(b3[:], b3[:], AF.Exp), reads=b3.all, writes=b3.all)
            k.op("dve", lambda e: e.scalar_tensor_tensor(AR4[:, :, 0, :], KK, -1.0, V3(b3), op0=ALU.mult, op1=ALU.mult),
                 reads=b5.all + b3.all, writes=AR.all)
            if dbg <= 5:
                break
            tmp3_res, dst3_res = [b7.p()], [b1.p()]
            load_group(ZL, 0, 6, t0)
            shift_mix(V3(b1), ZL, 0, 6, V3(b7))
            RM = V3(b1)
            k.op("act", lambda e: e.activation(b3[:], b2[:], AF.Exp), reads=b2.all, writes=b3.all)
            k.op("dve", lambda e: e.tensor_tensor(AR4[:, :, 1, :], RM, V3(b3), op=ALU.mult),
                 reads=b1.all + b3.all, writes=AR.all)
            k.op("pool", lambda e: e.tensor_tensor(b7[:], b1[:], b4[:], op=ALU.mult),
                 reads=b1.all + b4.all, writes=b7.all)
            k.op("dve", lambda e: e.tensor_tensor(V3(b7), V3(b7), rk_c[:].rearrange("p (c o) -> p c o", o=1)
                                                  .to_broadcast(W6), op=ALU.mult),
                 reads=b7.all + rk_c.all, writes=b7.all)
            for pr in range(NPR):
                ps = k.ps[pr % 4]
                k.mmx(ps.p(), [(ps[:, fc * 2:fc * 2 + 2], V3(b7)[:, fc, pr * 128:(pr + 1) * 128], bsel[:],
                                True, True) for fc in range(6)], reads=b7.all + bsel.all)
                k.op("dve", lambda e: e.tensor_copy(bon[:, pr, :], ps[:, 0:12]), reads=ps.all, writes=bon.all)
            if dbg <= 6:
                break
            for pr in range(NPR):
                for half in range(2):
                    ps = k.ps[4 + (pr * 2 + half) % 4]
                    k.mm(ps.p(), [(ps[:, 0:384], sgA[:, pr * 128:(pr + 1) * 128], g2a[:, half * 384:(half + 1) * 384]),
                                  (ps[:, 0:384], sgB[:, pr * 128:(pr + 1) * 128], g2b[:, half * 384:(half + 1) * 384])],
                         reads=sgA.all + sgB.all + g2a.all + g2b.all)
                    k.op("act", lambda e: e.activation(Gt[:, pr, half * 384:(half + 1) * 384], ps[:, 0:384], AF.Copy),
                         reads=ps.all, writes=Gt.all)
            if dbg <= 7:
                break
            for pr in range(NPR):
                tc = slice(pr * 128, (pr + 1) * 128)
                bi = 0
                for which, dstT in ((0, T1), (1, T2)):
                    for hp in range(2):
                        rows = slice(hp * 64, hp * 64 + 64)
                        for f0 in range(0, 6, 2):
                            ps = k.ps[bi % 4]
                            bi += 1
                            mms = []
                            for ff in range(2):
                                fc = f0 + ff
                                mms.append((ps[:, ff * 256:(ff + 1) * 256].rearrange("p (a t) -> p a t", a=2),
                                            BK[rows, fc, which, tc], AR[rows, fc, :, tc], True, True))
                            k.mmx(ps.p(), mms, reads=BK.all + AR.all)
                            k.op("dve", lambda e: e.tensor_tensor(dstT[:, f0:f0 + 2, hp, :],
                                                                  ps[:, :].rearrange("p (h c) -> p h c", h=2), MASK1[:],
                                                                  op=ALU.mult),
                                 reads=ps.all + MASK1.all, writes=dstT.all)
                if dbg <= 8:
                    break
                for hp in range(2):
                    rows = slice(hp * 64, hp * 64 + 64)
                    for f0 in (0, 3):
                        cur = 0
                        psL, psT = k.ps[4], k.ps[5]
                        k.mmx(psL.p(), [(psL[:, ff * 128:(ff + 1) * 128], AR[rows, f0 + ff, 0, tc], BK[rows, f0 + ff, 0, tc],
                                         True, True) for ff in range(3)], reads=BK.all + AR.all)
                        k.mmx(psT.p(), [(psT[:, ff * 128:(ff + 1) * 128], BK[rows, f0 + ff, 0, tc], AR[rows, f0 + ff, 0, tc],
                                         True, True) for ff in range(3)], reads=BK.all + AR.all)
                        k.op("dve", lambda e: e.tensor_tensor(X[0][:], psL[:, 0:384].rearrange("p (h c) -> p h c", h=3),
                                                              MASK3[:, 0:3, :], op=ALU.mult),
                             reads=psL.all + MASK3.all, writes=X[0].all)
                        k.op("dve", lambda e: e.tensor_tensor(XT[0][:], psT[:, 0:384].rearrange("p (h c) -> p h c", h=3),
                                                              msT[:].rearrange("p (o c) -> p o c", o=1)
                                                              .to_broadcast([128, 3, 128]), op=ALU.mult),
                             reads=psT.all + msT.all, writes=XT[0].all)
                        k.op("pool", lambda e: e.tensor_tensor(TT[0][:], XT[0][:],
                                                               c["idf"][:].rearrange("p (o c) -> p o c", o=1)
                                                               .to_broadcast([128, 3, 128]), op=ALU.add),
                             reads=XT[0].all + c["idf"].all, writes=TT[0].all)
                        for lvl in range(5):
                            nxt = 1 - cur
                            pa, pb, pc2 = k.ps[4 + (lvl % 2) * 2], k.ps[5 + (lvl % 2) * 2], k.ps[6 - (lvl % 2) * 2]
                            k.mmx(pa.p(), [(pa[:, hh * 128:(hh + 1) * 128], XT[cur][:, hh, :], X[cur][:, hh, :], True, True)
                                           for hh in range(3)], reads=X[cur].all + XT[cur].all)
                            k.op("dve", lambda e: e.tensor_copy(X[nxt][:], pa[:, 0:384].rearrange("p (h c) -> p h c", h=3)),
                                 reads=pa.all, writes=X[nxt].all)
                            if lvl < 4:
                                k.mmx(pb.p(), [(pb[:, hh * 128:(hh + 1) * 128], X[cur][:, hh, :], XT[cur][:, hh, :],
                                                True, True) for hh in range(3)], reads=X[cur].all + XT[cur].all)
                                k.op("act", lambda e: e.activation(XT[nxt][:],
                                                                   pb[:, 0:384].rearrange("p (h c) -> p h c", h=3),
                                                                   AF.Copy), reads=pb.all, writes=XT[nxt].all)
                            k.mmx(pc2.p(), [(pc2[:, hh * 128:(hh + 1) * 128], X[nxt][:, hh, :], TT[cur][:, hh, :],
                                             True, True) for hh in range(3)], reads=X[nxt].all + TT[cur].all)
                            k.op("dve", lambda e: e.tensor_tensor(TT[nxt][:], TT[cur][:],
                                                                  pc2[:, 0:384].rearrange("p (h c) -> p h c", h=3),
                                                                  op=ALU.add),
                                 reads=pc2.all + TT[cur].all, writes=TT[nxt].all)
                            cur = nxt
                        k.op("act", lambda e: e.activation(TTb[:, f0:f0 + 3, hp, :], TT[cur][:], AF.Copy),
                             reads=TT[cur].all, writes=TTb.all)
                if dbg <= 9:
                    break
                for ci in range(2):
                    prs = slice(ci * 64, ci * 64 + 64)
                    tl = slice(ci * 64, ci * 64 + 64)
                    pw = [k.ps[4], k.ps[5]]
                    for half in range(2):
                        k.mmx(pw[half].p(), [(pw[half][prs, hh * 64:(hh + 1) * 64],
                                              T2[prs, (half * 6 + hh) // 2, (half * 6 + hh) % 2, tl],
                                              Vt[prs, pr, (half * 6 + hh) * 64:(half * 6 + hh + 1) * 64],
                                              hh == 0, hh == 5) for hh in range(6)], reads=T2.all + Vt.all)
                        k.op("act", lambda e: e.activation(Wsb[prs, half * 384:(half + 1) * 384], pw[half][prs, 0:384],
                                                           AF.Copy), reads=pw[half].all, writes=Wsb.all)
                    pu = [k.ps[6], k.ps[7]]
                    for half in range(2):
                        k.mmx(pu[half].p(), [(pu[half][prs, hh * 64:(hh + 1) * 64],
                                              TTb[prs, (half * 6 + hh) // 2, (half * 6 + hh) % 2, tl],
                                              Wsb[prs, (half * 6 + hh) * 64:(half * 6 + hh + 1) * 64],
                                              hh == 0, hh == 5) for hh in range(6)], reads=TTb.all + Wsb.all)
                        k.op("dve", lambda e: e.tensor_copy(U0[prs, half * 384:(half + 1) * 384], pu[half][prs, 0:384]),
                             reads=pu[half].all, writes=U0.all)
                for ci in range(2):
                    ch = pr * 2 + ci
                    prs = slice(ci * 64, ci * 64 + 64)
                    tcc = slice(pr * 128 + ci * 64, pr * 128 + ci * 64 + 64)
                    tl = slice(ci * 64, ci * 64 + 64)
                    pw = [k.ps[0], k.ps[1]]
                    py = [k.ps[2], k.ps[3]]
                    for (pbk, which) in ((pw, 0), (py, 1)):
                        for hp in range(2):
                            rows = slice(hp * 64, hp * 64 + 64)
                            k.mmx(pbk[hp].p(), [(pbk[hp][prs, fc * 64:(fc + 1) * 64], AR[rows, fc, which, tcc],
                                                 STb[rows, fc, :], fc == 0, fc == 5) for fc in range(6)],
                                  reads=AR.all + STb.all)
                    Wv = Wsb[:].rearrange("p (f h v) -> p f h v", f=6, h=2)
                    for hp in range(2):
                        k.op("dve" if hp == 0 else "act",
                             (lambda e: e.tensor_copy(Wv[prs, :, hp, :], pw[hp][prs, 0:384].rearrange("p (f v) -> p f v", f=6)))
                             if hp == 0 else
                             (lambda e: e.activation(Wv[prs, :, hp, :], pw[hp][prs, 0:384].rearrange("p (f v) -> p f v", f=6),
                                                     AF.Copy)),
                             reads=pw[hp].all, writes=Wsb.all)
                    pu = [k.ps[4], k.ps[5]]
                    for half in range(2):
                        k.mmx(pu[half].p(), [(pu[half][prs, hh * 64:(hh + 1) * 64],
                                              TTb[prs, (half * 6 + hh) // 2, (half * 6 + hh) % 2, tl],
                                              Wsb[prs, (half * 6 + hh) * 64:(half * 6 + hh + 1) * 64],
                                              hh == 0, hh == 5) for hh in range(6)], reads=TTb.all + Wsb.all)
                        k.op("dve", lambda e: e.tensor_tensor(Usb[prs, half * 384:(half + 1) * 384],
                                                              U0[prs, half * 384:(half + 1) * 384], pu[half][prs, 0:384],
                                                              op=ALU.add),
                             reads=pu[half].all + U0.all, writes=Usb.all)
                    pya = [k.ps[6], k.ps[7]]
                    for half in range(2):
                        mms = []
                        for hh in range(6):
                            h = half * 6 + hh
                            o = pya[half][prs, hh * 64:(hh + 1) * 64]
                            mms.append((o, T1[prs, h // 2, h % 2, 128 + ci * 64:128 + ci * 64 + 64],
                                        Usb[prs, h * 64:(h + 1) * 64], hh == 0, False))
                            mms.append((o, T2[prs, h // 2, h % 2, 128 + ci * 64:128 + ci * 64 + 64],
                                        Vt[prs, pr, h * 64:(h + 1) * 64], False, hh == 5))
                        k.mmx(pya[half].p(), mms, reads=T1.all + T2.all + Vt.all + Usb.all)
                    Yv = Ysb[:].rearrange("p (f h v) -> p f h v", f=6, h=2)
                    for hp in range(2):
                        k.op("act", lambda e: e.activation(Yv[prs, :, hp, :],
                                                           py[hp][prs, 0:384].rearrange("p (f v) -> p f v", f=6), AF.Copy),
                             reads=py[hp].all, writes=Ysb.all)
                    for half in range(2):
                        k.op("dve", lambda e: e.tensor_tensor(Ysb[prs, half * 384:(half + 1) * 384],
                                                              Ysb[prs, half * 384:(half + 1) * 384], pya[half][prs, 0:384],
                                                              op=ALU.add),
                             reads=pya[half].all + Ysb.all, writes=Ysb.all)
                    pss = k.ps[0]
                    mms = []
                    for h in range(12):
                        fc, hp = h // 2, h % 2
                        rows = slice(hp * 64, hp * 64 + 64)
                        o = pss[rows, fc * 64:(fc + 1) * 64]
                        mms.append((o, BBt[prs, pr, h * 64:(h + 1) * 64], Usb[prs, h * 64:(h + 1) * 64], h < 2, False))
                        mms.append((o, KBt[prs, pr, h * 64:(h + 1) * 64], Vt[prs, pr, h * 64:(h + 1) * 64], False, h == 11))
                    k.mmx(pss.p(), mms, reads=BBt.all + KBt.all + Vt.all + Usb.all)
                    k.op("dve", lambda e: e.tensor_tensor(Stmp[:], ST[:], PC[:, :, ch:ch + 1].to_broadcast([128, 6, 64]),
                                                          op=ALU.mult), reads=ST.all + PC.all, writes=Stmp.all)
                    k.op("dve", lambda e: e.tensor_tensor(ST[:], Stmp[:], pss[:, 0:384].rearrange("p (c v) -> p c v", c=6),
                                                          op=ALU.add), reads=Stmp.all + pss.all, writes=ST.all)
                    k.op("act", lambda e: e.activation(STb[:], ST[:], AF.Copy), reads=ST.all, writes=STb.all)
                if dbg <= 10:
                    break
                yw, yo = Yw[pr % 2], Yo[pr % 2]
                s1, s2, mean, rstd = st12
                Y3 = Ysb[:].rearrange("p (h v) -> p h v", h=12)
                k.op("dve", lambda e: e.reduce_sum(s1[:], Y3, axis=AX.X), reads=Ysb.all, writes=s1.all)
                k.op("pool", lambda e: e.tensor_tensor(yw[:], Ysb[:], Ysb[:], op=ALU.mult), reads=Ysb.all, writes=yw.all)
                k.op("dve", lambda e: e.reduce_sum(s2[:], yw[:].rearrange("p (h v) -> p h v", h=12), axis=AX.X),
                     reads=yw.all, writes=s2.all)
                k.op("dve", lambda e: e.tensor_scalar(mean[:], s1[:], 1.0 / 64, None, op0=ALU.mult),
                     reads=s1.all, writes=mean.all)
                k.op("dve", lambda e: e.tensor_tensor(s1[:], mean[:], mean[:], op=ALU.mult), reads=mean.all, writes=s1.all)
                k.op("dve", lambda e: e.scalar_tensor_tensor(s2[:], s2[:], 1.0 / 64, s1[:], op0=ALU.mult, op1=ALU.subtract),
                     reads=s2.all + s1.all, writes=s2.all)
                k.op("act", lambda e: e.activation(rstd[:], s2[:], AF.Ln, bias=gneps[:, 0:1]),
                     reads=s2.all + gneps.all, writes=rstd.all)
                k.op("act", lambda e: e.activation(rstd[:], rstd[:], AF.Exp, scale=-0.5), reads=rstd.all, writes=rstd.all)
                yw3 = yw[:].rearrange("p (h v) -> p h v", h=12)
                k.op("dve", lambda e: e.tensor_tensor(yw3, Y3, mean[:].rearrange("p (h o) -> p h o", o=1)
                                                      .to_broadcast([128, 12, 64]), op=ALU.subtract),
                     reads=Ysb.all + mean.all, writes=yw.all)
                k.op("dve", lambda e: e.tensor_tensor(yw3, yw3, rstd[:].rearrange("p (h o) -> p h o", o=1)
                                                      .to_broadcast([128, 12, 64]), op=ALU.mult),
                     reads=yw.all + rstd.all, writes=yw.all)
                k.op("pool", lambda e: e.tensor_tensor(yw[:], yw[:], lnw[:], op=ALU.mult),
                     reads=yw.all + lnw.all, writes=yw.all)
                k.op("pool", lambda e: e.tensor_tensor(yw[:], yw[:], lnb[:], op=ALU.add),
                     reads=yw.all + lnb.all, writes=yw.all)
                yo3 = yo[:].rearrange("p (h v) -> p h v", h=12)
                k.op("dve", lambda e: e.tensor_tensor(yo3, Vt[:, pr, :].rearrange("p (h v) -> p h v", h=12),
                                                      bon[:, pr, :].rearrange("p (h o) -> p h o", o=1)
                                                      .to_broadcast([128, 12, 64]), op=ALU.mult),
                     reads=Vt.all + bon.all, writes=yo.all)
                k.op("dve", lambda e: e.tensor_tensor(yo[:], yo[:], yw[:], op=ALU.add),
                     reads=yo.all + yw.all, writes=yo.all)
                k.op("dve", lambda e: e.tensor_tensor(yo[:], yo[:], Gt[:, pr, :], op=ALU.mult),
                     reads=yo.all + Gt.all, writes=yo.all)
                with nc.allow_non_contiguous_dma(reason="mix slice"):
                    k.dma("sp", MIX[t0 + pr * 128:t0 + (pr + 1) * 128, 0:768], yo[:], reads=yo.all)
        k.barrier()


SEQ = 4096
DEPTH = 4
PARAM_SHAPES = {
    "norm1": (4, 1024), "norm_mem": (4, 1024), "w_mem_kv": (4, 1024, 512), "w_o": (4, 1024, 1024),
    "norm2": (4, 1024), "w_ffn_in": (4, 1024, 5632), "w_ffn_out": (4, 2816, 1024),
    "nsa_w_in": (2, 1024, 2596), "nsa_gate_b": (2, 36), "nsa_cmp_pos": (2, 2, 32, 64),
    "nsa_cmp_w1": (2, 2, 32, 64, 128), "nsa_cmp_w2": (2, 2, 128, 64),
    "rw_w_in": (2, 1024, 2848), "rw_mu": (2, 2592), "rw_w0": (2, 768), "rw_w2": (2, 64, 768),
    "rw_a0": (2, 768), "rw_a2": (2, 64, 768), "rw_g2": (2, 160, 768), "rw_k_k": (2, 768), "rw_k_a": (2, 768),
    "rw_r_k": (2, 12, 64), "rw_lnx_w": (2, 768), "rw_lnx_b": (2, 768), "final_norm": (1024,),
}


def build_program(S=SEQ, depth=DEPTH):
    nc = bass.Bass("TRN2", target_bir_lowering=False)
    X0 = nc.dram_tensor("x", [S, 1024], F32, kind="ExternalInput").ap()
    MEM = nc.dram_tensor("mem", [256, 1024], F32, kind="ExternalInput").ap()
    P = {n: nc.dram_tensor(n, list(sh), F32, kind="ExternalInput").ap() for n, sh in PARAM_SHAPES.items()}
    OUT = nc.dram_tensor("out", [S, 1024], F32, kind="ExternalOutput").ap()

    def SC(name, shape, dt):
        return nc.dram_tensor(name, list(shape), dt, kind="Internal").ap()
    XA, XB = SC("XA", (S, 1024), F32), SC("XB", (S, 1024), F32)
    MIX = SC("MIX", (S, 1024), F32)
    sc = dict(QT=SC("QT", (768, S), BF16), KCT=SC("KCT", (256, S), BF16), VCT=SC("VCT", (256, S), BF16),
              KST=SC("KST", (256, S), BF16), KWT=SC("KWT", (256, S), BF16), VS=SC("VS", (S, 256), BF16),
              VW=SC("VW", (S, 256), BF16), GL=SC("GL", (S, 36), F32), QMT=SC("QMT", (256, S), BF16))
    ZT = SC("ZT", (2592, S), F32)
    with contextlib.ExitStack() as st:
        k = K(nc, st)
        with contextlib.ExitStack() as ph0:
            c = make_consts(k, ph0)
            Xcur = X0
            nsa_cache = {}
            for i in range(depth):
                j = i // 2
                if i % 2 == 0:
                    outs = [dict(c0=0, n=768, mode="fm", dst=sc["QT"], dt=BF16),
                            dict(c0=768, n=256, mode="fm", dst=sc["KCT"], dt=BF16),
                            dict(c0=1024, n=256, mode="fm", dst=sc["VCT"], dt=BF16),
                            dict(c0=1280, n=256, mode="fm", dst=sc["KST"], dt=BF16),
                            dict(c0=1536, n=256, mode="tm", dst=sc["VS"], dt=BF16),
                            dict(c0=1792, n=256, mode="fm", dst=sc["KWT"], dt=BF16),
                            dict(c0=2048, n=256, mode="tm", dst=sc["VW"], dt=BF16),
                            dict(c0=2304, n=36, mode="tm", dst=sc["GL"], dt=F32),
                            dict(c0=2340, n=256, mode="fm", dst=sc["QMT"], dt=BF16)]
                    phase_proj(k, c, Xcur, P["norm1"][i], P["nsa_w_in"][j], outs, S)
                    phase_nsa(k, c, sc, P["nsa_gate_b"][j], P["nsa_cmp_pos"][j], P["nsa_cmp_w1"][j],
                              P["nsa_cmp_w2"][j], MIX, S, cache=nsa_cache)
                else:
                    outs = [dict(c0=0, n=2592, mode="fm", dst=ZT, dt=F32),
                            dict(c0=2592, n=256, mode="fm", dst=sc["QMT"], dt=BF16)]
                    phase_proj(k, c, Xcur, P["norm1"][i], P["rw_w_in"][j], outs, S)
                    prm = dict(mu=P["rw_mu"][j], w0=P["rw_w0"][j], w2=P["rw_w2"][j], a0=P["rw_a0"][j],
                               a2=P["rw_a2"][j], g2=P["rw_g2"][j], k_k=P["rw_k_k"][j], k_a=P["rw_k_a"][j],
                               r_k=P["rw_r_k"][j], lnx_w=P["rw_lnx_w"][j], lnx_b=P["rw_lnx_b"][j])
                    phase_rwkv(k, c, ZT, prm, MIX, S)
                phase_memattn(k, c, MEM, P["norm_mem"][i], P["w_mem_kv"][i], sc["QMT"], MIX, S)
                phase_oproj(k, c, Xcur, XA, MIX, P["w_o"][i], S)
                phase_ffn(k, c, XA, XB, P["norm2"][i], P["w_ffn_in"][i], P["w_ffn_out"][i], S)
                Xcur = XB
            phase_final_norm(k, c, Xcur, OUT, P["final_norm"], S)
    return nc


def kernel(**inputs):
    x = np.ascontiguousarray(inputs["x"], dtype=np.float32)
    mem = np.ascontiguousarray(inputs["mem"], dtype=np.float32)
    B = x.shape[0]
    nc = build_program()
    params = {n: np.ascontiguousarray(inputs[n], dtype=np.float32) for n in PARAM_SHAPES}
    in_maps = []
    for b in range(B):
        d = dict(params)
        d["x"] = x[b]
        d["mem"] = mem[b]
        in_maps.append(d)
    res = run_bass_kernel_spmd(nc, in_maps, core_ids=list(range(B)))
    return np.stack([np.asarray(r["out"], dtype=np.float32) for r in res.results], axis=0)
```

```python
import contextlib
import numpy as np
import concourse.bass as bass
import concourse.mybir as mybir
from concourse.bass_utils import run_bass_kernel_spmd

F32 = mybir.dt.float32
BF16 = mybir.dt.bfloat16
AF = mybir.ActivationFunctionType
ALU = mybir.AluOpType
AX = mybir.AxisListType

D = 1024
FFN_H = 2816
RMS_EPS = 1e-6
NDMA = 24


class Res:
    __slots__ = ("w", "r")

    def __init__(self):
        self.w = None
        self.r = {}


class Eng:
    def __init__(self, name, eng, sem):
        self.name, self.eng, self.sem = name, eng, sem
        self.count = 0
        self.seen = {}


class Tile:
    def __init__(self, t, parts=1):
        self.t = t
        self.parts = [Res() for _ in range(parts)]

    def __getitem__(self, idx):
        return self.t[idx]

    def p(self, i=0):
        return self.parts[i]

    @property
    def all(self):
        return list(self.parts)


class K:
    def __init__(self, nc, stack):
        self.nc = nc
        self.st = stack
        self.E = {}
        for nm, eng in (("pe", nc.tensor), ("dve", nc.vector), ("act", nc.scalar),
                        ("pool", nc.gpsimd), ("sp", nc.sync)):
            sem = stack.enter_context(nc.semaphore("sem_" + nm))
            self.E[nm] = Eng(nm, eng, sem)
        self.slots = []
        for i in range(NDMA):
            sem = stack.enter_context(nc.semaphore("dsem%d" % i))
            self.slots.append([("dma%d" % i), sem, 0])
        self.rr = 0
        self.ps = [Tile(stack.enter_context(nc.psum_tensor("ps%d" % i, [128, 512], F32)))
                   for i in range(8)]
        self.uid = 0

    def sb(self, ph, shape, dtype, name, parts=1):
        self.uid += 1
        return Tile(ph.enter_context(self.nc.sbuf_tensor("%s_%d" % (name, self.uid), shape, dtype)), parts)

    def dram(self, name, shape, dtype, kind="Internal", parts=1):
        return Tile(self.nc.dram_tensor(name, shape, dtype, kind=kind), parts)

    def _wait(self, E, tok):
        key, sem, val = tok
        if E.seen.get(key, 0) < val:
            E.eng.wait_ge(sem, val)
            E.seen[key] = val

    def _deps(self, E, reads, writes):
        pe = E.name == "pe"
        for r in reads:
            if r.w is not None and not (pe and r.w[0] == "pe"):
                self._wait(E, r.w)
        for w in writes:
            if w.w is not None and not (pe and w.w[0] == "pe"):
                self._wait(E, w.w)
            for key, tok in w.r.items():
                if key != E.name:
                    self._wait(E, tok)

    def _mark(self, tok, reads, writes):
        for r in reads:
            r.r[tok[0]] = tok
        for w in writes:
            w.w = tok
            w.r = {}

    def op(self, en, fn, reads=(), writes=()):
        E = self.E[en]
        self._deps(E, reads, writes)
        inst = fn(E.eng)
        E.count += 1
        inst.then_inc(E.sem, 1)
        tok = (E.name, E.sem, E.count)
        E.seen[E.name] = E.seen.get(E.name, 0)
        self._mark(tok, reads, writes)
        return tok

    def mm(self, out_res, mms, reads=()):
        E = self.E["pe"]
        self._deps(E, reads, [out_res])
        n = len(mms)
        inst = None
        for i, (o, l, r) in enumerate(mms):
            inst = E.eng.matmul(o, l, r, start=(i == 0), stop=(i == n - 1))
        E.count += 1
        inst.then_inc(E.sem, 1)
        tok = (E.name, E.sem, E.count)
        self._mark(tok, reads, [out_res])
        return tok

    def mmx(self, out_res, mms, reads=()):
        E = self.E["pe"]
        self._deps(E, reads, [out_res])
        inst = None
        for (o, l, r, st, sp) in mms:
            inst = E.eng.matmul(o, l, r, start=st, stop=sp, skip_group_check=True)
        E.count += 1
        inst.then_inc(E.sem, 1)
        tok = (E.name, E.sem, E.count)
        self._mark(tok, reads, [out_res])
        return tok

    def dma(self, qn, out_ap, in_ap, reads=(), writes=(), **kw):
        E = self.E[qn]
        self._deps(E, reads, writes)
        slot = self.slots[self.rr]
        self.rr = (self.rr + 1) % NDMA
        if slot[2] > 0:
            self._wait(E, (slot[0], slot[1], slot[2]))
        inst = E.eng.dma_start(out=out_ap, in_=in_ap, **kw)
        slot[2] += 16
        inst.then_inc(slot[1], 16)
        tok = (slot[0], slot[1], slot[2])
        self._mark(tok, reads, writes)
        return tok

    def barrier(self):
        toks = [(e.name, e.sem, e.count) for e in self.E.values() if e.count > 0]
        toks += [(s[0], s[1], s[2]) for s in self.slots if s[2] > 0]
        for E in self.E.values():
            for t in toks:
                if t[0] != E.name:
                    self._wait(E, t)


def make_consts(k, ph):
    nc = k.nc
    c = {}
    idf = k.sb(ph, [128, 128], F32, "identf")
    k.op("pool", lambda e: e.memset(idf[:], 1.0), writes=idf.all)
    k.op("pool", lambda e: e.affine_select(idf[:], idf[:], pattern=[[-1, 128]], compare_op=ALU.is_equal,
                                           fill=0.0, base=0, channel_multiplier=1),
         reads=idf.all, writes=idf.all)
    idb = k.sb(ph, [128, 128], BF16, "identb")
    k.op("dve", lambda e: e.tensor_copy(idb[:], idf[:]), reads=idf.all, writes=idb.all)
    c["idf"], c["idb"] = idf, idb
    return c


def load_vec_bcast(k, ph, dram_ap_1d, n, name, q="sp"):
    t = k.sb(ph, [128, n], F32, name)
    k.dma(q, t[:], dram_ap_1d.partition_broadcast(128), writes=t.all)
    return t


def load_vec_col(k, ph, dram_ap_1d, nch, name, q="sp"):
    t = k.sb(ph, [128, nch], F32, name)
    with k.nc.allow_non_contiguous_dma(reason="small vector load"):
        k.dma(q, t[:], dram_ap_1d.rearrange("(c p) -> p c", p=128), writes=t.all)
    return t


def norm_block(k, c, xts, gcol, hT, hparts, scr, psbase=0, col0=0, eps=RMS_EPS):
    n = len(xts)
    sq, ss, rstd = scr["sq"], scr["ss"], scr["rstd"]
    for i, x in enumerate(xts):
        k.op("act", lambda e: e.activation(sq[:], x[:], AF.Square, accum_out=ss[:, i:i + 1]),
             reads=x.all, writes=sq.all + ss.all)
    k.op("act", lambda e: e.activation(rstd[:, 0:n], ss[:, 0:n], AF.Ln, scale=1.0 / D, bias=scr["eps"][:, 0:1]),
         reads=ss.all + scr["eps"].all, writes=rstd.all)
    k.op("act", lambda e: e.activation(rstd[:, 0:n], rstd[:, 0:n], AF.Exp, scale=-0.5),
         reads=rstd.all, writes=rstd.all)
    for i, x in enumerate(xts):
        xn = scr["xn"][i % 2]
        k.op("dve", lambda e: e.tensor_scalar(xn[:], x[:], rstd[:, i:i + 1], None, op0=ALU.mult),
             reads=x.all + rstd.all, writes=xn.all)
        for g4 in range(2):
            ps = k.ps[psbase + (i % 2) * 2 + g4]
            for j in range(4):
                ch = g4 * 4 + j
                k.op("pe", lambda e: e.transpose(ps[:, j * 128:(j + 1) * 128],
                                                 xn[:, ch * 128:(ch + 1) * 128], c["idf"][:]),
                     reads=xn.all + c["idf"].all, writes=ps.all)
            for j in range(4):
                ch = g4 * 4 + j
                if j % 2 == 0:
                    k.op("dve", lambda e: e.tensor_scalar(hT[:, ch, col0 + i * 128:col0 + (i + 1) * 128],
                                                          ps[:, j * 128:(j + 1) * 128],
                                                          gcol[:, ch:ch + 1], None, op0=ALU.mult),
                         reads=ps.all + gcol.all, writes=[hparts[i]])
                else:
                    k.op("act", lambda e: e.activation(hT[:, ch, col0 + i * 128:col0 + (i + 1) * 128],
                                                       ps[:, j * 128:(j + 1) * 128],
                                                       AF.Copy, scale=gcol[:, ch:ch + 1]),
                         reads=ps.all + gcol.all, writes=[hparts[i]])


def norm_scratch(k, ph, n):
    scr = dict(sq=k.sb(ph, [128, 1024], F32, "sq"), ss=k.sb(ph, [128, n], F32, "ss"),
               rstd=k.sb(ph, [128, n], F32, "rstd"),
               xn=[k.sb(ph, [128, 1024], F32, "xn%d" % i) for i in range(2)],
               eps=k.sb(ph, [128, 1], F32, "epsc"))
    k.op("pool", lambda e: e.memset(scr["eps"][:], RMS_EPS), writes=scr["eps"].all)
    return scr


def phase_ffn(k, c, X, Xout, g2_ap, w_in_ap, w_out_ap, S):
    nc = k.nc
    TB = 1024
    nblk = S // TB
    with contextlib.ExitStack() as ph:
        gcol = load_vec_col(k, ph, g2_ap, 8, "g2col")
        wout = k.sb(ph, [128, 22, 1024], BF16, "wout")
        k.dma("pool", wout[:], w_out_ap.rearrange("(c p) m -> p c m", p=128), writes=wout.all)
        hT = k.sb(ph, [128, 8, TB], BF16, "hT", parts=8)
        actT = k.sb(ph, [128, 22, TB], BF16, "actT", parts=44)
        xt = [k.sb(ph, [128, 1024], F32, "xt%d" % i) for i in range(8)]
        scr = norm_scratch(k, ph, 8)
        wg = [k.sb(ph, [128, 8, 256], BF16, "wg%d" % i) for i in range(2)]
        wu = [k.sb(ph, [128, 8, 256], BF16, "wu%d" % i) for i in range(2)]
        sg = [k.sb(ph, [128, 512], F32, "sg%d" % i) for i in range(2)]
        xo = [k.sb(ph, [128, 1024], F32, "xo%d" % i) for i in range(2)]
        w_in_v = w_in_ap.rearrange("(c p) m -> p c m", p=128)
        for blk in range(nblk):
            t0 = blk * TB
            for tt in range(TB // 128):
                k.dma("sp", xt[tt][:], X[t0 + tt * 128:t0 + (tt + 1) * 128, :], writes=xt[tt].all)
            norm_block(k, c, xt, gcol, hT, hT.parts, scr, psbase=0)
            it = 0
            for j in range(11):
                b = j % 2
                k.dma("pool", wg[b][:], w_in_v[:, :, j * 256:(j + 1) * 256], writes=wg[b].all)
                k.dma("pool", wu[b][:], w_in_v[:, :, FFN_H + j * 256:FFN_H + (j + 1) * 256], writes=wu[b].all)
                for half in range(2):
                    mch = j * 2 + half
                    for th in range(TB // 512):
                        pg = k.ps[4 + (it % 2) * 2]
                        pu = k.ps[5 + (it % 2) * 2]
                        s = sg[it % 2]
                        it += 1
                        hreads = [hT.p(th * 4 + i) for i in range(4)]
                        k.mm(pg.p(), [(pg[:, :], wg[b][:, kc, half * 128:(half + 1) * 128],
                                       hT[:, kc, th * 512:(th + 1) * 512]) for kc in range(8)],
                             reads=hreads + wg[b].all)
                        k.mm(pu.p(), [(pu[:, :], wu[b][:, kc, half * 128:(half + 1) * 128],
                                       hT[:, kc, th * 512:(th + 1) * 512]) for kc in range(8)],
                             reads=hreads + wu[b].all)
                        k.op("act", lambda e: e.activation(s[:], pg[:, :], AF.Silu),
                             reads=pg.all, writes=s.all)
                        ar = actT.p(mch * 2 + th)
                        k.op("dve", lambda e: e.tensor_tensor(actT[:, mch, th * 512:(th + 1) * 512], s[:], pu[:, :],
                                                              op=ALU.mult),
                             reads=s.all + pu.all, writes=[ar])
            for tt in range(TB // 128):
                x = xt[tt]
                o = xo[tt % 2]
                for chh in range(2):
                    po = k.ps[(tt % 2) * 2 + chh]
                    th = tt // 4
                    k.mm(po.p(), [(po[:, :], actT[:, kc, tt * 128:(tt + 1) * 128],
                                   wout[:, kc, chh * 512:(chh + 1) * 512]) for kc in range(22)],
                         reads=[actT.p(kc * 2 + th) for kc in range(22)] + wout.all)
                    k.op("dve", lambda e: e.tensor_tensor(o[:, chh * 512:(chh + 1) * 512],
                                                          x[:, chh * 512:(chh + 1) * 512], po[:, :], op=ALU.add),
                         reads=x.all + po.all, writes=o.all)
                k.dma("pool", Xout[t0 + tt * 128:t0 + (tt + 1) * 128, :], o[:], reads=o.all)
        k.barrier()


def phase_final_norm(k, c, X, OUT, g_ap, S):
    with contextlib.ExitStack() as ph:
        gb = load_vec_bcast(k, ph, g_ap, 1024, "gfin")
        epsc = k.sb(ph, [128, 1], F32, "epsf")
        k.op("pool", lambda e: e.memset(epsc[:], RMS_EPS), writes=epsc.all)
        xt = [k.sb(ph, [128, 1024], F32, "fx%d" % i) for i in range(2)]
        sq = [k.sb(ph, [128, 1024], F32, "fsq%d" % i) for i in range(2)]
        ss = [k.sb(ph, [128, 1], F32, "fss%d" % i) for i in range(2)]
        xo = [k.sb(ph, [128, 1024], F32, "fo%d" % i) for i in range(2)]
        for tt in range(S // 128):
            b = tt % 2
            x, q, s, o = xt[b], sq[b], ss[b], xo[b]
            k.dma("sp", x[:], X[tt * 128:(tt + 1) * 128, :], writes=x.all)
            k.op("act", lambda e: e.activation(q[:], x[:], AF.Square, accum_out=s[:]),
                 reads=x.all, writes=q.all + s.all)
            k.op("act", lambda e: e.activation(s[:], s[:], AF.Ln, scale=1.0 / D, bias=epsc[:, 0:1]),
                 reads=s.all + epsc.all, writes=s.all)
            k.op("act", lambda e: e.activation(s[:], s[:], AF.Exp, scale=-0.5),
                 reads=s.all, writes=s.all)
            k.op("dve", lambda e: e.scalar_tensor_tensor(o[:], x[:], s[:, 0:1], gb[:], op0=ALU.mult, op1=ALU.mult),
                 reads=x.all + s.all + gb.all, writes=o.all)
            k.dma("pool", OUT[tt * 128:(tt + 1) * 128, :], o[:], reads=o.all)
        k.barrier()


def phase_proj(k, c, X, g1_ap, W_ap, outs, S):
    with contextlib.ExitStack() as ph:
        gcol = load_vec_col(k, ph, g1_ap, 8, "g1col")
        hT = k.sb(ph, [128, 8, S], BF16, "hnT", parts=S // 128)
        xt = [k.sb(ph, [128, 1024], F32, "pxt%d" % i) for i in range(8)]
        scr = norm_scratch(k, ph, 8)
        for blk in range(S // 1024):
            t0 = blk * 1024
            for tt in range(8):
                k.dma("sp", xt[tt][:], X[t0 + tt * 128:t0 + (tt + 1) * 128, :], writes=xt[tt].all)
            norm_block(k, c, xt, gcol, hT, hT.parts[blk * 8:(blk + 1) * 8], scr, psbase=0, col0=t0)
        wsl = [k.sb(ph, [128, 8, 512], BF16, "wsl%d" % i) for i in range(2)]
        stg = {}
        W_v = W_ap.rearrange("(c p) m -> p c m", p=128)
        si = 0
        ei = 0
        for o in outs:
            c0, n, mode, dst, dt = o["c0"], o["n"], o["mode"], o["dst"], o["dt"]
            for s0 in range(0, n, 512):
                sn = min(512, n - s0)
                w = wsl[si % 2]
                si += 1
                k.dma("pool", w[:, :, 0:sn], W_v[:, :, c0 + s0:c0 + s0 + sn], writes=w.all)
                if mode == "fm":
                    for m0 in range(0, sn, 128):
                        mn = min(128, sn - m0)
                        key = ("fm", dt)
                        if key not in stg:
                            stg[key] = [[k.sb(ph, [128, S], dt, "stgfm%d" % i) for i in range(2)], 0]
                        st = stg[key][0][stg[key][1] % 2]
                        stg[key][1] += 1
                        for tb in range(S // 512):
                            ps = k.ps[4 + ei % 4]
                            k.mm(ps.p(), [(ps[0:mn, :], w[:, kc, m0:m0 + mn], hT[:, kc, tb * 512:(tb + 1) * 512])
                                          for kc in range(8)],
                                 reads=hT.parts[tb * 4:(tb + 1) * 4] + w.all)
                            if ei % 2 == 0:
                                k.op("dve", lambda e: e.tensor_copy(st[0:mn, tb * 512:(tb + 1) * 512], ps[0:mn, :]),
                                     reads=ps.all, writes=st.all)
                            else:
                                k.op("act", lambda e: e.activation(st[0:mn, tb * 512:(tb + 1) * 512], ps[0:mn, :],
                                                                   AF.Copy),
                                     reads=ps.all, writes=st.all)
                            ei += 1
                        k.dma("sp", dst[s0 + m0:s0 + m0 + mn, :], st[0:mn, :], reads=st.all)
                else:
                    key = ("tm", dt)
                    if key not in stg:
                        stg[key] = [[k.sb(ph, [128, 512], dt, "stgtm%d" % i) for i in range(4)], 0]
                    for tt in range(S // 128):
                        st = stg[key][0][stg[key][1] % 4]
                        stg[key][1] += 1
                        ps = k.ps[4 + ei % 4]
                        k.mm(ps.p(), [(ps[:, 0:sn], hT[:, kc, tt * 128:(tt + 1) * 128], w[:, kc, 0:sn])
                                      for kc in range(8)],
                             reads=[hT.p(tt)] + w.all)
                        if ei % 2 == 0:
                            k.op("dve", lambda e: e.tensor_copy(st[:, 0:sn], ps[:, 0:sn]), reads=ps.all, writes=st.all)
                        else:
                            k.op("act", lambda e: e.activation(st[:, 0:sn], ps[:, 0:sn], AF.Copy),
                                 reads=ps.all, writes=st.all)
                        ei += 1
                        k.dma("sp", dst[tt * 128:(tt + 1) * 128, s0:s0 + sn], st[:, 0:sn], reads=st.all)
        k.barrier()


def phase_oproj(k, c, X, Xout, MIX, wo_ap, S):
    with contextlib.ExitStack() as ph:
        wo = k.sb(ph, [128, 8, 1024], BF16, "wo")
        k.dma("pool", wo[:], wo_ap.rearrange("(c p) m -> p c m", p=128), writes=wo.all)
        mt = [k.sb(ph, [128, 1024], F32, "mixt%d" % i) for i in range(2)]
        mT = [k.sb(ph, [128, 8, 128], BF16, "mixT%d" % i) for i in range(2)]
        xt = [k.sb(ph, [128, 1024], F32, "oxt%d" % i) for i in range(2)]
        xo = [k.sb(ph, [128, 1024], F32, "oxo%d" % i) for i in range(2)]
        for tt in range(S // 128):
            b = tt % 2
            k.dma("sp", mt[b][:], MIX[tt * 128:(tt + 1) * 128, :], writes=mt[b].all)
            k.dma("sp", xt[b][:], X[tt * 128:(tt + 1) * 128, :], writes=xt[b].all)
            for g4 in range(2):
                ps = k.ps[b * 2 + g4]
                for j in range(4):
                    ch = g4 * 4 + j
                    k.op("pe", lambda e: e.transpose(ps[:, j * 128:(j + 1) * 128],
                                                     mt[b][:, ch * 128:(ch + 1) * 128], c["idf"][:]),
                         reads=mt[b].all + c["idf"].all, writes=ps.all)
                if g4 == 0:
                    k.op("dve", lambda e: e.tensor_copy(mT[b][:, 0:4, :],
                                                        ps[:, :].rearrange("p (c t) -> p c t", c=4)),
                         reads=ps.all, writes=mT[b].all)
                else:
                    k.op("act", lambda e: e.activation(mT[b][:, 4:8, :],
                                                       ps[:, :].rearrange("p (c t) -> p c t", c=4), AF.Copy),
                         reads=ps.all, writes=mT[b].all)
            for chh in range(2):
                po = k.ps[4 + b * 2 + chh]
                k.mm(po.p(), [(po[:, :], mT[b][:, kc, :], wo[:, kc, chh * 512:(chh + 1) * 512]) for kc in range(8)],
                     reads=mT[b].all + wo.all)
                k.op("dve", lambda e: e.tensor_tensor(xo[b][:, chh * 512:(chh + 1) * 512],
                                                      xt[b][:, chh * 512:(chh + 1) * 512], po[:, :], op=ALU.add),
                     reads=xt[b].all + po.all, writes=xo[b].all)
            k.dma("pool", Xout[tt * 128:(tt + 1) * 128, :], xo[b][:], reads=xo[b].all)
        k.barrier()


def alibi_slopes12():
    def pow2(m):
        start = 2.0 ** (-8.0 / m)
        return [start ** (i + 1) for i in range(m)]
    s = pow2(8)
    s = s + pow2(16)[0::2][:4]
    return [float(np.float32(v)) for v in s]


class AttnBufs:
    def __init__(self, k, ph, n=3):
        self.n = n
        self.u = [k.sb(ph, [128, 512], F32, "au%d" % i) for i in range(n)]
        self.E = [k.sb(ph, [128, 512], BF16, "aE%d" % i) for i in range(n)]
        self.i = 0


def score_unit(k, ab, mms, reads, Dt=None, Dres=(), slope=0.0, bias=0.0):
    i = ab.i % ab.n
    ab.i += 1
    ps, u, E = k.ps[i], ab.u[i], ab.E[i]
    k.mm(ps.p(), [(ps[:, :], l, r) for (l, r) in mms], reads=reads)
    if Dt is not None:
        k.op("dve", lambda e: e.scalar_tensor_tensor(u[:], Dt, float(slope), ps[:, :], op0=ALU.mult, op1=ALU.add),
             reads=ps.all + list(Dres), writes=u.all)
        k.op("act", lambda e: e.activation(E[:], u[:], AF.Exp, scale=0.125, bias=float(bias)),
             reads=u.all, writes=E.all)
    else:
        k.op("act", lambda e: e.activation(E[:], ps[:, :], AF.Exp, scale=0.125), reads=ps.all, writes=E.all)
    return E


SKIP_EXP = 150.0


def run_pipeline(units, score_fn, pv_fn, la=2):
    Es = [None] * len(units)
    for i in range(len(units) + la):
        if i < len(units):
            Es[i] = score_fn(units[i])
        j = i - la
        if j >= 0:
            pv_fn(units[j], Es[j])
            Es[j] = None


def pv_accum(k, acc, accv, E, rhs, rhs_res, first, last, nsub=4):
    mms = []
    for qs in range(nsub):
        mms.append((accv[:, qs, :], E[:, qs * 128:(qs + 1) * 128], rhs,
                    bool(first and qs == 0), bool(last and qs == nsub - 1)))
    k.mmx(acc.p(), mms, reads=E.all + list(rhs_res))


def phase_memattn(k, c, MEM, gm_ap, wkv_ap, QMT, MIX, S):
    with contextlib.ExitStack() as ph:
        gcol = load_vec_col(k, ph, gm_ap, 8, "gmcol")
        mt = [k.sb(ph, [128, 1024], F32, "memt%d" % i) for i in range(2)]
        scr = norm_scratch(k, ph, 2)
        mnT = k.sb(ph, [128, 8, 256], BF16, "mnT", parts=2)
        for i in range(2):
            k.dma("sp", mt[i][:], MEM[i * 128:(i + 1) * 128, :], writes=mt[i].all)
        norm_block(k, c, mt, gcol, mnT, mnT.parts, scr, psbase=4)
        wkv = k.sb(ph, [128, 8, 512], BF16, "wkv")
        k.dma("pool", wkv[:], wkv_ap.rearrange("(c p) m -> p c m", p=128), writes=wkv.all)
        kmT = k.sb(ph, [64, 4, 256], BF16, "kmT")
        vm = k.sb(ph, [128, 2, 4, 65], BF16, "vmaug")
        k.op("pool", lambda e: e.memset(vm[:], 1.0), writes=vm.all)
        for h in range(4):
            ps = k.ps[4 + h % 2]
            k.mm(ps.p(), [(ps[0:64, 0:256], wkv[:, kc, h * 64:(h + 1) * 64], mnT[:, kc, :]) for kc in range(8)],
                 reads=mnT.all + wkv.all)
            k.op("dve", lambda e: e.tensor_copy(kmT[:, h, :], ps[0:64, 0:256]), reads=ps.all, writes=kmT.all)
        for m2 in range(2):
            ps = k.ps[6 + m2]
            k.mm(ps.p(), [(ps[:, 0:256], mnT[:, kc, m2 * 128:(m2 + 1) * 128], wkv[:, kc, 256:512])
                          for kc in range(8)], reads=mnT.all + wkv.all)
            k.op("dve", lambda e: e.tensor_copy(vm[:, m2, :, 0:64],
                                                ps[:, 0:256].rearrange("p (h d) -> p h d", h=4)),
                 reads=ps.all, writes=vm.all)
        ab = AttnBufs(k, ph)
        qT = [k.sb(ph, [64, S], BF16, "qmT%d" % i) for i in range(2)]
        cross = k.sb(ph, [128, S // 128, 256], F32, "crossall")
        rec = k.sb(ph, [128, 4, 1], F32, "mrec")
        ai = 0
        for h in range(4):
            q = qT[h % 2]
            k.dma("sp", q[:], QMT[h * 64:(h + 1) * 64, :], writes=q.all)
            for qt in range(S // 512):
                acc = k.ps[3 + ai % 2]
                ai += 1
                accv = acc[:, 0:260].rearrange("p (s w) -> p s w", s=4)
                for m2 in range(2):
                    E = score_unit(k, ab, [(kmT[:, h, m2 * 128:(m2 + 1) * 128], q[:, qt * 512:(qt + 1) * 512])],
                                   reads=kmT.all + q.all)
                    pv_accum(k, acc, accv, E, vm[:, m2, h, :], vm.all, first=(m2 == 0), last=(m2 == 1))
                k.op("dve", lambda e: e.reciprocal(rec[:], accv[:, :, 64:65]), reads=acc.all, writes=rec.all)
                k.op("dve", lambda e: e.tensor_tensor(cross[:, qt * 4:(qt + 1) * 4, h * 64:(h + 1) * 64],
                                                      accv[:, :, 0:64], rec[:].to_broadcast([128, 4, 64]),
                                                      op=ALU.mult),
                     reads=acc.all + rec.all, writes=cross.all)
        k.dma("sp", MIX[:, 768:1024].rearrange("(t p) c -> p t c", p=128), cross[:], reads=cross.all)
        k.barrier()


NEGBIG = -1.0e9


def phase_nsa(k, c, sc, gate_b_ap, pos_ap, w1_ap, w2_ap, MIX, S, cache=None):
    QT, KCT, VCT, KST, KWT, VS, VW, GL = (sc[n] for n in ("QT", "KCT", "VCT", "KST", "KWT", "VS", "VW", "GL"))
    slopes = alibi_slopes12()
    NQT = S // 512
    NKT = S // 128
    NTT = S // 128
    NCMP = (S - 32) // 16 + 1
    NNT = (NCMP + 127) // 128
    nc = k.nc
    with contextlib.ExitStack() as ph:
        cache = {} if cache is None else cache
        build = "done" not in cache

        def ctile(name, shape, dt):
            t = k.sb(ph, shape, dt, name)
            if not build:
                k.dma("sp", t[:], cache[name], writes=t.all)
            return t

        def cstore(name, t, shape, dt):
            cache[name] = nc.dram_tensor("nsac_" + name, list(shape), dt, kind="Internal").ap()
            k.dma("sp", cache[name], t[:], reads=t.all)

        Dwin = [ctile("Dwin%d" % j, [128, 512], F32) for j in range(8)]
        Dgen = ctile("Dgen", [128, 512], F32)
        Dcm = {}
        for qt in range(NQT):
            for nt in range(NNT):
                base = 512 * qt - 2048 * nt - 31
                if base + 511 < 0:
                    continue
                Dcm[(qt, nt)] = ctile("Dcm%d_%d" % (qt, nt), [128, 512], F32)
        ovl = ctile("ovl", [128, NNT, 64], F32)
        keepG = ctile("keepG", [128, 128], F32)
        addG = ctile("addG", [128, 128], F32)
        Ex = ctile("Ex", [64, NKT, 128], BF16)
        if build:
            for j in range(8):
                koff = 128 * j - 512
                t = Dwin[j]
                k.op("pool", lambda e: e.iota(t[:], pattern=[[1, 512]], base=-koff, channel_multiplier=-1,
                                              allow_small_or_imprecise_dtypes=True), writes=t.all)
                k.op("pool", lambda e: e.tensor_scalar(t[:], t[:], -8.0, None, op0=ALU.mult), reads=t.all, writes=t.all)
                k.op("pool", lambda e: e.affine_select(t[:], t[:], pattern=[[1, 512]], compare_op=ALU.is_ge,
                                                       fill=NEGBIG, base=-koff, channel_multiplier=-1),
                     reads=t.all, writes=t.all)
                k.op("pool", lambda e: e.affine_select(t[:], t[:], pattern=[[-1, 512]], compare_op=ALU.is_ge,
                                                       fill=NEGBIG, base=511 + koff, channel_multiplier=1),
                     reads=t.all, writes=t.all)
            k.op("pool", lambda e: e.iota(Dgen[:], pattern=[[1, 512]], base=0, channel_multiplier=-1,
                                          allow_small_or_imprecise_dtypes=True), writes=Dgen.all)
            k.op("pool", lambda e: e.tensor_scalar(Dgen[:], Dgen[:], -8.0, None, op0=ALU.mult),
                 reads=Dgen.all, writes=Dgen.all)
            for (qt, nt), t in Dcm.items():
                base = 512 * qt - 2048 * nt - 31
                k.op("pool", lambda e: e.iota(t[:], pattern=[[1, 512]], base=base, channel_multiplier=-16,
                                              allow_small_or_imprecise_dtypes=True), writes=t.all)
                k.op("pool", lambda e: e.tensor_scalar(t[:], t[:], -8.0, None, op0=ALU.mult),
                     reads=t.all, writes=t.all)
                k.op("pool", lambda e: e.affine_select(t[:], t[:], pattern=[[1, 512]], compare_op=ALU.is_ge,
                                                       fill=NEGBIG, base=base, channel_multiplier=-16),
                     reads=t.all, writes=t.all)
            k.op("pool", lambda e: e.memset(ovl[:], 1.0), writes=ovl.all)
            k.op("pool", lambda e: e.affine_select(ovl[:], ovl[:], pattern=[[128, NNT], [-4, 64]], compare_op=ALU.is_ge,
                                                   fill=0.0, base=1, channel_multiplier=1), reads=ovl.all, writes=ovl.all)
            k.op("pool", lambda e: e.affine_select(ovl[:], ovl[:], pattern=[[-128, NNT], [4, 64]], compare_op=ALU.is_ge,
                                                   fill=0.0, base=3, channel_multiplier=-1), reads=ovl.all, writes=ovl.all)
            k.op("pool", lambda e: e.memset(keepG[:], 1.0), writes=keepG.all)
            k.op("pool", lambda e: e.memset(addG[:], 1.0e4), writes=addG.all)
            for hp in range(2):
                rows = slice(hp * 64, hp * 64 + 64)
                k.op("pool", lambda e: e.affine_select(keepG[rows, :], keepG[rows, :], pattern=[[-1, 128]],
                                                       compare_op=ALU.is_ge, fill=0.0, base=62 + hp,
                                                       channel_multiplier=0), reads=keepG.all, writes=keepG.all)
                k.op("pool", lambda e: e.affine_select(addG[rows, :], addG[rows, :], pattern=[[1, 128]],
                                                       compare_op=ALU.is_ge, fill=0.0, base=-63 - hp,
                                                       channel_multiplier=0), reads=addG.all, writes=addG.all)
                k.op("pool", lambda e: e.affine_select(addG[rows, :], addG[rows, :], pattern=[[-1, 128]],
                                                       compare_op=ALU.is_ge, fill=0.0, base=64 + hp,
                                                       channel_multiplier=0), reads=addG.all, writes=addG.all)
            tmpst = contextlib.ExitStack()
            Exf = k.sb(tmpst, [64, NKT * 128], F32, "Exf")
            k.op("pool", lambda e: e.memset(Exf[:], 1.0), writes=Exf.all)
            k.op("pool", lambda e: e.affine_select(Exf[:].rearrange("p (a b c) -> p a b c", b=2, c=64),
                                                   Exf[:].rearrange("p (a b c) -> p a b c", b=2, c=64),
                                                   pattern=[[-2, NKT], [-1, 2], [0, 64]], compare_op=ALU.is_equal,
                                                   fill=0.0, base=0, channel_multiplier=1),
                 reads=Exf.all, writes=Exf.all)
            k.op("dve", lambda e: e.tensor_copy(Ex[:], Exf[:].rearrange("p (a c) -> p a c", c=128)),
                 reads=Exf.all, writes=Ex.all)
            k.barrier()
            tmpst.close()
            for j in range(8):
                cstore("Dwin%d" % j, Dwin[j], [128, 512], F32)
            cstore("Dgen", Dgen, [128, 512], F32)
            for (qt, nt), t in Dcm.items():
                cstore("Dcm%d_%d" % (qt, nt), t, [128, 512], F32)
            cstore("ovl", ovl, [128, NNT, 64], F32)
            cstore("keepG", keepG, [128, 128], F32)
            cstore("addG", addG, [128, 128], F32)
            cstore("Ex", Ex, [64, NKT, 128], BF16)
            cache["done"] = True
        gates = k.sb(ph, [128, NTT, 36], F32, "gates")
        gb = load_vec_bcast(k, ph, gate_b_ap, 36, "gateb")
        k.dma("sp", gates[:], GL.rearrange("(t p) c -> p t c", p=128), writes=gates.all)
        k.op("dve", lambda e: e.tensor_tensor(gates[:], gates[:], gb[:].rearrange("p (o c) -> p o c", o=1)
                                              .to_broadcast([128, NTT, 36]), op=ALU.add),
             reads=gates.all + gb.all, writes=gates.all)
        k.op("act", lambda e: e.activation(gates[:], gates[:], AF.Exp, scale=-1.0), reads=gates.all, writes=gates.all)
        k.op("dve", lambda e: e.tensor_scalar(gates[:], gates[:], 1.0, None, op0=ALU.add),
             reads=gates.all, writes=gates.all)
        k.op("dve", lambda e: e.reciprocal(gates[:], gates[:]), reads=gates.all, writes=gates.all)
        w1 = k.sb(ph, [64, 2, 32, 128], BF16, "cw1")
        w2 = k.sb(ph, [128, 2, 64], BF16, "cw2")
        posT = k.sb(ph, [64, 2, 32], BF16, "cposT")
        for kv in range(2):
            k.dma("pool", w1[:, kv, :, :], w1_ap[kv].rearrange("l d e -> d l e"), writes=w1.all)
            k.dma("pool", w2[:, kv, :], w2_ap[kv], writes=w2.all)
        with nc.allow_non_contiguous_dma(reason="tiny pos table"):
            k.dma("pool", posT[:], pos_ap.rearrange("v l d -> d v l"), writes=posT.all)
        cbias = k.sb(ph, [128, 2], F32, "cbias")
        for kv in range(2):
            ps = k.ps[6]
            k.mm(ps.p(), [(ps[:, 0:1], w1[:, kv, l, :], posT[:, kv, l:l + 1]) for l in range(32)],
                 reads=w1.all + posT.all)
            k.op("dve", lambda e: e.tensor_copy(cbias[:, kv:kv + 1], ps[:, 0:1]), reads=ps.all, writes=cbias.all)
        qT = k.sb(ph, [64, 3, S], BF16, "qT")
        ksT = k.sb(ph, [64, S], BF16, "ksT")
        kwT = k.sb(ph, [64, S], BF16, "kwT")
        cT = [k.sb(ph, [64, S], BF16, "cT%d" % i) for i in range(2)]
        vs = k.sb(ph, [128, NKT, 65], BF16, "vsaug")
        vw = k.sb(ph, [128, NKT, 65], BF16, "vwaug")
        k.op("pool", lambda e: e.memset(vs[:], 1.0), writes=vs.all)
        k.op("pool", lambda e: e.memset(vw[:], 1.0), writes=vw.all)
        kcbT = k.sb(ph, [64, NNT * 128], BF16, "kcbT")
        vcb = k.sb(ph, [128, NNT, 129], BF16, "vcbaug")
        k.op("pool", lambda e: e.memset(kcbT[:], 0.0), writes=kcbT.all)
        k.op("pool", lambda e: e.memset(vcb[:], 0.0), writes=vcb.all)
        k.op("pool", lambda e: e.memset(vcb[:, :, 128:129], 1.0), writes=vcb.all)
        k.op("dve", lambda e: e.tensor_copy(vcb[:, :, 64:128], ovl[:]), reads=ovl.all, writes=vcb.all)
        hx = k.sb(ph, [128, 256], F32, "hx")
        hy = k.sb(ph, [128, 256], F32, "hy")
        hact = k.sb(ph, [128, 256], BF16, "hact")
        k.op("pool", lambda e: e.memset(hact[:], 0.0), writes=hact.all)
        ab = AttnBufs(k, ph)
        oc = k.sb(ph, [128, 4, 3, 64], F32, "oc")
        imp = k.sb(ph, [128, 4, 64], F32, "imp")
        rec = k.sb(ph, [128, 4, 1], F32, "nrec")
        gs = k.sb(ph, [128, 4, 1], F32, "ngs")
        scq = k.sb(ph, [128, 4, 64], F32, "scq")
        wk = k.sb(ph, [128, 4, 64], F32, "wk")
        m8 = k.sb(ph, [128, 8], F32, "m8")
        nmT = k.sb(ph, [64, 512], BF16, "nmT")
        t1 = k.sb(ph, [128, 4, 64], F32, "t1")
        mixt = [k.sb(ph, [128, 4, 192], F32, "mixt%d" % i) for i in range(2)]
        mi = 0
        NC1 = NCMP
        for g in range(4):
            for r in range(3):
                k.dma("sp", qT[:, r, :], QT[(g * 3 + r) * 64:(g * 3 + r + 1) * 64, :], writes=qT.all)
            k.dma("sp", ksT[:], KST[g * 64:(g + 1) * 64, :], writes=ksT.all)
            k.dma("sp", kwT[:], KWT[g * 64:(g + 1) * 64, :], writes=kwT.all)
            k.dma("sp", cT[0][:], KCT[g * 64:(g + 1) * 64, :], writes=cT[0].all)
            k.dma("sp", cT[1][:], VCT[g * 64:(g + 1) * 64, :], writes=cT[1].all)
            with nc.allow_non_contiguous_dma(reason="v head slice"):
                k.dma("sp", vs[:, :, 0:64], VS[:, g * 64:(g + 1) * 64].rearrange("(t p) d -> p t d", p=128),
                      writes=vs.all)
                k.dma("sp", vw[:, :, 0:64], VW[:, g * 64:(g + 1) * 64].rearrange("(t p) d -> p t d", p=128),
                      writes=vw.all)
            for kv in range(2):
                src3 = cT[kv][:].rearrange("p (n s) -> p n s", s=16)
                ps = k.ps[6]
                k.mm(ps.p(), [(ps[:, 0:NC1], w1[:, kv, l, :], src3[:, (l // 16):(l // 16) + NC1, l % 16])
                              for l in range(32)], reads=w1.all + cT[kv].all)
                k.op("dve", lambda e: e.tensor_scalar(hx[:, 0:NC1], ps[:, 0:NC1], cbias[:, kv:kv + 1], None,
                                                      op0=ALU.add), reads=ps.all + cbias.all, writes=hx.all)
                k.op("dve", lambda e: e.tensor_tensor(hy[:, 0:NC1], hx[:, 0:NC1], hx[:, 0:NC1], op=ALU.mult),
                     reads=hx.all, writes=hy.all)
                k.op("dve", lambda e: e.tensor_scalar(hy[:, 0:NC1], hy[:, 0:NC1], 0.044715, 1.0,
                                                      op0=ALU.mult, op1=ALU.add), reads=hy.all, writes=hy.all)
                k.op("dve", lambda e: e.tensor_tensor(hy[:, 0:NC1], hy[:, 0:NC1], hx[:, 0:NC1], op=ALU.mult),
                     reads=hy.all + hx.all, writes=hy.all)
                k.op("act", lambda e: e.activation(hy[:, 0:NC1], hy[:, 0:NC1], AF.Exp, scale=-1.5957691216057308),
                     reads=hy.all, writes=hy.all)
                k.op("dve", lambda e: e.tensor_scalar(hy[:, 0:NC1], hy[:, 0:NC1], 1.0, None, op0=ALU.add),
                     reads=hy.all, writes=hy.all)
                k.op("dve", lambda e: e.reciprocal(hy[:, 0:NC1], hy[:, 0:NC1]), reads=hy.all, writes=hy.all)
                k.op("dve", lambda e: e.tensor_tensor(hact[:, 0:NC1], hx[:, 0:NC1], hy[:, 0:NC1], op=ALU.mult),
                     reads=hy.all + hx.all, writes=hact.all)
                if kv == 0:
                    ps2 = k.ps[7]
                    k.mm(ps2.p(), [(ps2[0:64, 0:NC1], w2[:, 0, :], hact[:, 0:NC1])], reads=w2.all + hact.all)
                    k.op("dve", lambda e: e.tensor_copy(kcbT[:, 0:NC1], ps2[0:64, 0:NC1]),
                         reads=ps2.all, writes=kcbT.all)
                else:
                    for nt in range(NNT):
                        nn = min(128, NC1 - nt * 128)
                        ps2 = k.ps[7]
                        k.mm(ps2.p(), [(ps2[0:nn, 0:64], hact[:, nt * 128:nt * 128 + nn], w2[:, 1, :])],
                             reads=w2.all + hact.all)
                        k.op("dve", lambda e: e.tensor_copy(vcb[0:nn, nt, 0:64], ps2[0:nn, 0:64]),
                             reads=ps2.all, writes=vcb.all)
            for qt in range(NQT):
                q0 = qt * 512
                accA, accB = k.ps[3], k.ps[4]
                vA = accA[:, 0:258].rearrange("p (s w) -> p s w", s=2)
                vB = accB[:, 0:258].rearrange("p (s w) -> p s w", s=2)
                nts = [nt for nt in range(NNT) if (qt, nt) in Dcm]

                def cmp_post(r):
                    for half, (acc, av) in enumerate(((accA, vA), (accB, vB))):
                        sl = slice(half * 2, half * 2 + 2)
                        k.op("dve", lambda e: e.tensor_scalar(rec[:, sl, :], av[:, :, 128:129], 1e-30, None,
                                                              op0=ALU.add), reads=acc.all, writes=rec.all)
                        k.op("dve", lambda e: e.reciprocal(rec[:, sl, :], rec[:, sl, :]), reads=rec.all, writes=rec.all)
                        k.op("dve", lambda e: e.tensor_tensor(oc[:, sl, r, :], av[:, :, 0:64],
                                                              rec[:, sl, :].to_broadcast([128, 2, 64]), op=ALU.mult),
                             reads=acc.all + rec.all, writes=oc.all)
                        if r == 0:
                            k.op("dve", lambda e: e.tensor_tensor(imp[:, sl, :], av[:, :, 64:128],
                                                                  rec[:, sl, :].to_broadcast([128, 2, 64]),
                                                                  op=ALU.mult),
                                 reads=acc.all + rec.all, writes=imp.all)
                        else:
                            k.op("dve", lambda e: e.tensor_tensor(wk[:, sl, :], av[:, :, 64:128],
                                                                  rec[:, sl, :].to_broadcast([128, 2, 64]),
                                                                  op=ALU.mult),
                                 reads=acc.all + rec.all, writes=wk.all)
                            k.op("dve", lambda e: e.tensor_tensor(imp[:, sl, :], imp[:, sl, :], wk[:, sl, :],
                                                                  op=ALU.add),
                                 reads=imp.all + wk.all, writes=imp.all)

                units = []
                for r in range(3):
                    for ii, nt in enumerate(nts):
                        units.append(dict(r=r, nt=nt, first=(ii == 0), last=(ii == len(nts) - 1)))

                def cmp_score(u):
                    Dt = Dcm[(qt, u["nt"])]
                    return score_unit(k, ab, [(kcbT[:, u["nt"] * 128:(u["nt"] + 1) * 128], qT[:, u["r"], q0:q0 + 512])],
                                      reads=kcbT.all + qT.all, Dt=Dt[:], Dres=Dt.all, slope=slopes[g * 3 + u["r"]],
                                      bias=0.0)

                def cmp_pv(u, E):
                    nt, first, last = u["nt"], u["first"], u["last"]
                    k.mmx(accA.p(), [(vA[:, qs, :], E[:, qs * 128:(qs + 1) * 128], vcb[:, nt, :],
                                      bool(first and qs == 0), bool(last and qs == 1)) for qs in range(2)],
                          reads=E.all + vcb.all)
                    k.mmx(accB.p(), [(vB[:, qs, :], E[:, (qs + 2) * 128:(qs + 3) * 128], vcb[:, nt, :],
                                      bool(first and qs == 0), bool(last and qs == 1)) for qs in range(2)],
                          reads=E.all + vcb.all)
                    if last:
                        cmp_post(u["r"])
                run_pipeline(units, cmp_score, cmp_pv)
                pst = k.ps[5]
                for qs in range(4):
                    tt = qt * 4 + qs
                    lo = 64 - 2 * tt
                    k.op("dve", lambda e: e.scalar_tensor_tensor(scq[:, qs, :], imp[:, qs, :], 1.0,
                                                                 keepG[:, lo:lo + 64], op0=ALU.add, op1=ALU.mult),
                         reads=imp.all + keepG.all, writes=scq.all)
                    k.op("dve", lambda e: e.tensor_tensor(scq[:, qs, :], scq[:, qs, :], addG[:, lo:lo + 64],
                                                          op=ALU.add), reads=scq.all + addG.all, writes=scq.all)
                    k.op("dve", lambda e: e.memset(scq[:, qs, 0:1], 1.0e4), reads=scq.all, writes=scq.all)
                    k.op("dve", lambda e: e.max(m8[:], scq[:, qs, :]), reads=scq.all, writes=m8.all)
                    k.op("dve", lambda e: e.match_replace(wk[:, qs, :], m8[:], scq[:, qs, :], 0.0),
                         reads=scq.all + m8.all, writes=wk.all)
                    k.op("dve", lambda e: e.max(m8[:], wk[:, qs, :]), reads=wk.all, writes=m8.all)
                    k.op("dve", lambda e: e.match_replace(wk[:, qs, :], m8[:], wk[:, qs, :], 0.0),
                         reads=wk.all + m8.all, writes=wk.all)
                    k.op("dve", lambda e: e.tensor_tensor(wk[:, qs, :], scq[:, qs, :], wk[:, qs, :], op=ALU.subtract),
                         reads=wk.all + scq.all, writes=wk.all)
                    k.op("dve", lambda e: e.tensor_scalar(wk[:, qs, :], wk[:, qs, :], 1.0, 1.0,
                                                          op0=ALU.min, op1=ALU.subtract), reads=wk.all, writes=wk.all)
                    k.op("pe", lambda e: e.transpose(pst[0:64, qs * 128:(qs + 1) * 128], wk[:, qs, :], c["idf"][:]),
                         reads=wk.all + c["idf"].all, writes=pst.all)
                k.op("act", lambda e: e.activation(nmT[:], pst[0:64, :], AF.Copy, scale=30000.0),
                     reads=pst.all, writes=nmT.all)
                mx = mixt[mi % 2]
                mi += 1
                accS, accW = k.ps[6], k.ps[7]
                vS = accS[:, 0:260].rearrange("p (s w) -> p s w", s=4)
                vW = accW[:, 0:260].rearrange("p (s w) -> p s w", s=4)
                gsl = slice(qt * 4, qt * 4 + 4)

                def combine(r):
                    h = g * 3 + r
                    k.op("dve", lambda e: e.reciprocal(rec[:], vS[:, :, 64:65]), reads=accS.all, writes=rec.all)
                    k.op("dve", lambda e: e.tensor_tensor(gs[:], rec[:], gates[:, gsl, 12 + h:13 + h], op=ALU.mult),
                         reads=rec.all + gates.all, writes=gs.all)
                    k.op("dve", lambda e: e.tensor_tensor(t1[:], vS[:, :, 0:64], gs[:].to_broadcast([128, 4, 64]),
                                                          op=ALU.mult), reads=accS.all + gs.all, writes=t1.all)
                    k.op("dve", lambda e: e.reciprocal(rec[:], vW[:, :, 64:65]), reads=accW.all, writes=rec.all)
                    k.op("dve", lambda e: e.tensor_tensor(gs[:], rec[:], gates[:, gsl, 24 + h:25 + h], op=ALU.mult),
                         reads=rec.all + gates.all, writes=gs.all)
                    k.op("dve", lambda e: e.tensor_tensor(wk[:], vW[:, :, 0:64], gs[:].to_broadcast([128, 4, 64]),
                                                          op=ALU.mult), reads=accW.all + gs.all, writes=wk.all)
                    k.op("pool", lambda e: e.tensor_tensor(t1[:], t1[:], wk[:], op=ALU.add),
                         reads=t1.all + wk.all, writes=t1.all)
                    k.op("pool", lambda e: e.tensor_tensor(wk[:], oc[:, :, r, :],
                                                           gates[:, gsl, h:h + 1].to_broadcast([128, 4, 64]),
                                                           op=ALU.mult), reads=oc.all + gates.all, writes=wk.all)
                    k.op("pool", lambda e: e.tensor_tensor(mx[:, :, r * 64:(r + 1) * 64], t1[:], wk[:], op=ALU.add),
                         reads=t1.all + wk.all, writes=mx.all)

                units = []
                for r in range(3):
                    sl_h = slopes[g * 3 + r]
                    skts = [kt for kt in range(4 * qt + 4)
                            if kt >= 4 * qt or sl_h * (q0 - (kt * 128 + 127)) < SKIP_EXP]
                    for ii, kt in enumerate(skts):
                        units.append(dict(br="s", r=r, kt=kt, first=(ii == 0), last=(ii == len(skts) - 1), post=False))
                    wkts = [kt for kt in range(4 * qt - 4, 4 * qt + 4) if kt >= 0]
                    for ii, kt in enumerate(wkts):
                        units.append(dict(br="w", r=r, kt=kt, first=(ii == 0), last=(ii == len(wkts) - 1),
                                          post=(ii == len(wkts) - 1)))

                def sw_score(u):
                    r, kt = u["r"], u["kt"]
                    sl_h = slopes[g * 3 + r]
                    if u["br"] == "s":
                        if kt >= 4 * qt:
                            Dt, bias = Dwin[4 + kt - 4 * qt], 0.0
                        else:
                            Dt, bias = Dgen, -sl_h * (q0 - kt * 128)
                        return score_unit(k, ab, [(ksT[:, kt * 128:(kt + 1) * 128], qT[:, r, q0:q0 + 512]),
                                                  (Ex[:, kt, :], nmT[:, :])],
                                          reads=ksT.all + qT.all + Ex.all + nmT.all, Dt=Dt[:], Dres=Dt.all,
                                          slope=sl_h, bias=bias)
                    Dt = Dwin[kt - 4 * qt + 4]
                    return score_unit(k, ab, [(kwT[:, kt * 128:(kt + 1) * 128], qT[:, r, q0:q0 + 512])],
                                      reads=kwT.all + qT.all, Dt=Dt[:], Dres=Dt.all, slope=sl_h, bias=0.0)

                def sw_pv(u, E):
                    if u["br"] == "s":
                        pv_accum(k, accS, vS, E, vs[:, u["kt"], :], vs.all, first=u["first"], last=u["last"])
                    else:
                        pv_accum(k, accW, vW, E, vw[:, u["kt"], :], vw.all, first=u["first"], last=u["last"])
                    if u["post"]:
                        combine(u["r"])
                run_pipeline(units, sw_score, sw_pv)
                with nc.allow_non_contiguous_dma(reason="mix slice"):
                    k.dma("sp", MIX[q0:q0 + 512, g * 192:(g + 1) * 192].rearrange("(s p) c -> p s c", p=128),
                          mx[:], reads=mx.all)
        k.barrier()


GN_EPS = 64e-5
TBK = 256


def phase_rwkv(k, c, ZT, prm, MIX, S, dbg=99):
    nc = k.nc
    NB = S // TBK if dbg >= 99 else 1
    NCH = TBK // 64
    NPR = TBK // 128
    W6 = [128, 6, TBK]

    def V3(t):
        return t[:].rearrange("p (c t) -> p c t", c=6)

    with contextlib.ExitStack() as ph:
        def colvec(name, ap, n=6):
            return load_vec_col(k, ph, ap, n, name)
        mu_c = k.sb(ph, [128, 21], F32, "mu_c")
        with nc.allow_non_contiguous_dma(reason="small vector load"):
            k.dma("sp", mu_c[:, 0:20], prm["mu"][0:2560].rearrange("(c p) -> p c", p=128), writes=mu_c.all)
            k.dma("sp", mu_c[0:32, 20:21], prm["mu"][2560:2592].rearrange("(c p) -> p c", p=32), writes=mu_c.all)
        w0n = colvec("w0n", prm["w0"])
        a0n = colvec("a0n", prm["a0"])
        kk_c = colvec("kk_c", prm["k_k"])
        ka_c = colvec("ka_c", prm["k_a"])
        rk_c = colvec("rk_c", prm["r_k"].rearrange("h n -> (h n)"))
        omka = k.sb(ph, [128, 6], F32, "omka")
        k.op("dve", lambda e: e.tensor_scalar(omka[:], ka_c[:], -1.0, 1.0, op0=ALU.mult, op1=ALU.add),
             reads=ka_c.all, writes=omka.all)
        k.op("dve", lambda e: e.tensor_scalar(w0n[:], w0n[:], -1.0, None, op0=ALU.mult), reads=w0n.all, writes=w0n.all)
        k.op("dve", lambda e: e.tensor_scalar(a0n[:], a0n[:], -1.0, None, op0=ALU.mult), reads=a0n.all, writes=a0n.all)
        lnw = load_vec_bcast(k, ph, prm["lnx_w"], 768, "lnw")
        lnb = load_vec_bcast(k, ph, prm["lnx_b"], 768, "lnb")
        w2a2 = k.sb(ph, [128, 768], BF16, "w2a2")
        k.dma("pool", w2a2[0:64, :], prm["w2"], writes=w2a2.all)
        k.dma("pool", w2a2[64:128, :], prm["a2"], writes=w2a2.all)
        g2a = k.sb(ph, [128, 768], BF16, "g2a")
        g2b = k.sb(ph, [32, 768], BF16, "g2b")
        k.dma("pool", g2a[:], prm["g2"][0:128, :], writes=g2a.all)
        k.dma("pool", g2b[:], prm["g2"][128:160, :], writes=g2b.all)
        tiny = k.sb(ph, [128, 1], F32, "tiny")
        k.op("pool", lambda e: e.memset(tiny[:], 1e-24), writes=tiny.all)
        gneps = k.sb(ph, [128, 1], F32, "gneps")
        k.op("pool", lambda e: e.memset(gneps[:], GN_EPS), writes=gneps.all)
        msT = k.sb(ph, [128, 128], F32, "msT")
        miT = k.sb(ph, [128, 128], F32, "miT")
        msL = k.sb(ph, [128, 128], F32, "msL")
        for (t, pat, base, cm) in ((msT, [[1, 128]], -1, -1), (miT, [[1, 128]], 0, -1), (msL, [[-1, 128]], -1, 1)):
            k.op("pool", lambda e: e.memset(t[:], 1.0), writes=t.all)
            k.op("pool", lambda e: e.affine_select(t[:], t[:], pattern=pat, compare_op=ALU.is_ge, fill=0.0,
                                                   base=base, channel_multiplier=cm), reads=t.all, writes=t.all)
            k.op("pool", lambda e: e.memset(t[0:64, 64:128], 0.0), reads=t.all, writes=t.all)
            k.op("pool", lambda e: e.memset(t[64:128, 0:64], 0.0), reads=t.all, writes=t.all)
        MASK1 = k.sb(ph, [128, 2, 256], F32, "MASK1")
        MASK3 = k.sb(ph, [128, 4, 128], F32, "MASK3")
        for hh in range(2):
            k.op("pool", lambda e: e.tensor_copy(MASK1[:, hh, 0:128], msT[:]), reads=msT.all, writes=MASK1.all)
            k.op("pool", lambda e: e.tensor_copy(MASK1[:, hh, 128:256], miT[:]), reads=miT.all, writes=MASK1.all)
        for hh in range(4):
            k.op("pool", lambda e: e.tensor_copy(MASK3[:, hh, :], msL[:]), reads=msL.all, writes=MASK3.all)
        bones = k.sb(ph, [128, 128], F32, "bones")
        k.op("pool", lambda e: e.memset(bones[:], 1.0), writes=bones.all)
        k.op("pool", lambda e: e.memset(bones[0:64, 64:128], 0.0), reads=bones.all, writes=bones.all)
        k.op("pool", lambda e: e.memset(bones[64:128, 0:64], 0.0), reads=bones.all, writes=bones.all)
        bsel = k.sb(ph, [128, 2], F32, "bsel")
        k.op("pool", lambda e: e.memset(bsel[:], 0.0), writes=bsel.all)
        k.op("pool", lambda e: e.memset(bsel[0:64, 0:1], 1.0), reads=bsel.all, writes=bsel.all)
        k.op("pool", lambda e: e.memset(bsel[64:128, 1:2], 1.0), reads=bsel.all, writes=bsel.all)
        Mscan = k.sb(ph, [128, TBK], F32, "Mscan")
        k.op("pool", lambda e: e.memset(Mscan[:], 1.0), writes=Mscan.all)
        k.op("pool", lambda e: e.memset(Mscan[:].rearrange("p (n s) -> p n s", s=64)[:, :, 0:1], 0.0),
             reads=Mscan.all, writes=Mscan.all)
        ZL = k.sb(ph, [128, 6, TBK + 1], F32, "ZL")
        ZS = k.sb(ph, [128, 3, TBK + 1], F32, "ZS")
        bufs = [k.sb(ph, [128, 6 * TBK], F32, "rb%d" % i) for i in range(7)]
        b1, b2, b3, b4, b5, b6, b7 = bufs
        sm = [k.sb(ph, [128, TBK], F32, "rsm%d" % i) for i in range(3)]
        twz = k.sb(ph, [128, TBK], BF16, "twz")
        sgA = k.sb(ph, [128, TBK], BF16, "sgA")
        sgB = k.sb(ph, [32, TBK], BF16, "sgB")
        CL = k.sb(ph, [128, 6, NCH], F32, "CL")
        PC = k.sb(ph, [128, 6, NCH], F32, "PC")
        AR = k.sb(ph, [128, 6, 2, TBK], BF16, "AR")
        BK = k.sb(ph, [128, 6, 2, TBK], BF16, "BK")
        Vt = k.sb(ph, [128, NPR, 768], BF16, "Vt")
        BBt = k.sb(ph, [128, NPR, 768], BF16, "BBt")
        KBt = k.sb(ph, [128, NPR, 768], BF16, "KBt")
        Gt = k.sb(ph, [128, NPR, 768], F32, "Gt")
        bon = k.sb(ph, [128, NPR, 12], F32, "bon")
        T1 = k.sb(ph, [128, 6, 2, 256], BF16, "T1")
        T2 = k.sb(ph, [128, 6, 2, 256], BF16, "T2")
        X = [[k.sb(ph, [128, 3, 128], F32, "nX%d_%d" % (g_, i)) for i in range(2)] for g_ in range(4)]
        XT = [[k.sb(ph, [128, 3, 128], F32, "nXT%d_%d" % (g_, i)) for i in range(2)] for g_ in range(4)]
        TT = [[k.sb(ph, [128, 3, 128], F32, "nTT%d_%d" % (g_, i)) for i in range(2)] for g_ in range(4)]
        TTb = k.sb(ph, [128, 6, 2, 128], BF16, "TTb")
        U0 = k.sb(ph, [128, 768], F32, "U0")
        ST = k.sb(ph, [128, 6, 64], F32, "ST")
        STb = k.sb(ph, [128, 6, 64], BF16, "STb")
        k.op("pool", lambda e: e.memset(ST[:], 0.0), writes=ST.all)
        k.op("pool", lambda e: e.memset(STb[:], 0.0), writes=STb.all)
        Wsb = k.sb(ph, [128, 768], BF16, "Wsb")
        Usb = k.sb(ph, [128, 768], BF16, "Usb")
        Ysb = k.sb(ph, [128, 768], F32, "Ysb")
        Yw = [k.sb(ph, [128, 768], F32, "Yw%d" % i) for i in range(2)]
        st12 = [k.sb(ph, [128, 12], F32, "st12_%d" % i) for i in range(4)]
        Stmp = k.sb(ph, [128, 6, 64], F32, "Stmp")
        Yo = [k.sb(ph, [128, 768], F32, "Yo%d" % i) for i in range(2)]

        def load_group(dst, row0, nchunks, t0, rows=128):
            src = ZT[row0:row0 + nchunks * 128, :].rearrange("(c p) t -> p c t", p=128) if rows == 128 else None
            if t0 == 0:
                k.op("pool", lambda e: e.memset(dst[0:rows, 0:nchunks, 0:1], 0.0), writes=dst.all)
                k.dma("sp", dst[0:rows, 0:nchunks, 1:TBK + 1], src[:, :, 0:TBK], writes=dst.all)
            else:
                k.dma("sp", dst[0:rows, 0:nchunks, :], src[:, :, t0 - 1:t0 + TBK], writes=dst.all)

        def shift_mix(dst3, src, mu0, nchunks, tmp3, rows=128):
            k.op("dve", lambda e: e.tensor_tensor(tmp3[0:rows, 0:nchunks, :], src[0:rows, 0:nchunks, 0:TBK],
                                                  src[0:rows, 0:nchunks, 1:TBK + 1], op=ALU.subtract),
                 reads=src.all, writes=[tmp3_res[0]])
            for cc in range(nchunks):
                k.op("dve",
                     lambda e: e.scalar_tensor_tensor(dst3[0:rows, cc, :], tmp3[0:rows, cc, :],
                                                      mu_c[0:rows, mu0 + cc:mu0 + cc + 1],
                                                      src[0:rows, cc, 1:TBK + 1], op0=ALU.mult, op1=ALU.add),
                     reads=[tmp3_res[0]] + src.all + mu_c.all, writes=[dst3_res[0]])

        mixq = 0
        for blk in range(NB):
            t0 = blk * TBK
            if t0 == 0:
                k.op("pool", lambda e: e.memset(ZS[:, :, 0:1], 0.0), writes=ZS.all)
            lo = 1 if t0 == 0 else 0
            k.dma("sp", ZS[:, 0:2, lo:TBK + 1],
                  ZT[2304:2560, :].rearrange("(c p) t -> p c t", p=128)[:, :, t0 - 1 + lo:t0 + TBK], writes=ZS.all)
            k.dma("sp", ZS[0:32, 2, lo:TBK + 1], ZT[2560:2592, t0 - 1 + lo:t0 + TBK], writes=ZS.all)
            zsm = sm[0], sm[1], sm[2]
            for cc, rows in ((0, 128), (1, 128), (2, 32)):
                d = zsm[cc]
                k.op("dve", lambda e: e.tensor_tensor(d[0:rows, :], ZS[0:rows, cc, 0:TBK], ZS[0:rows, cc, 1:TBK + 1],
                                                      op=ALU.subtract), reads=ZS.all, writes=d.all)
                k.op("dve", lambda e: e.scalar_tensor_tensor(d[0:rows, :], d[0:rows, :],
                                                             mu_c[0:rows, 18 + cc:19 + cc], ZS[0:rows, cc, 1:TBK + 1],
                                                             op0=ALU.mult, op1=ALU.add),
                     reads=d.all + ZS.all + mu_c.all, writes=d.all)
            zwza = sm[0]
            k.op("act", lambda e: e.activation(zwza[0:64, :], zwza[0:64, :], AF.Exp, scale=2.0),
                 reads=zwza.all, writes=zwza.all)
            k.op("dve", lambda e: e.tensor_scalar(zwza[0:64, :], zwza[0:64, :], 1.0, None, op0=ALU.add),
                 reads=zwza.all, writes=zwza.all)
            k.op("dve", lambda e: e.reciprocal(zwza[0:64, :], zwza[0:64, :]), reads=zwza.all, writes=zwza.all)
            k.op("dve", lambda e: e.tensor_scalar(twz[0:64, :], zwza[0:64, :], -2.0, 1.0, op0=ALU.mult, op1=ALU.add),
                 reads=zwza.all, writes=twz.all)
            k.op("dve", lambda e: e.tensor_copy(twz[64:128, :], zwza[64:128, :]), reads=zwza.all, writes=twz.all)
            for (src_, dst_, rows) in ((sm[1], sgA, 128), (sm[2], sgB, 32)):
                k.op("act", lambda e: e.activation(src_[0:rows, :], src_[0:rows, :], AF.Exp, scale=-1.0),
                     reads=src_.all, writes=src_.all)
                k.op("dve", lambda e: e.tensor_scalar(src_[0:rows, :], src_[0:rows, :], 1.0, None, op0=ALU.add),
                     reads=src_.all, writes=src_.all)
                k.op("dve", lambda e: e.reciprocal(src_[0:rows, :], src_[0:rows, :]), reads=src_.all, writes=src_.all)
                k.op("dve", lambda e: e.tensor_copy(dst_[0:rows, :], src_[0:rows, :]), reads=src_.all, writes=dst_.all)
            if dbg <= 1:
                break
            LW, CUM, A3 = V3(b1), V3(b2), V3(b3)
            for (dst3, dres, rows, bcol) in ((LW, b1, slice(0, 64), w0n), (A3, b3, slice(64, 128), a0n)):
                for fc in range(6):
                    ps = k.ps[fc % 4]
                    k.mm(ps.p(), [(ps[:, 0:TBK], w2a2[rows, fc * 128:(fc + 1) * 128], twz[rows, :])],
                         reads=w2a2.all + twz.all)
                    k.op("act", lambda e: e.activation(dst3[:, fc, :], ps[:, 0:TBK], AF.Exp, scale=-1.0,
                                                       bias=bcol[:, fc:fc + 1]),
                         reads=ps.all + bcol.all, writes=dres.all)
                k.op("dve", lambda e: e.tensor_scalar(dres[:], dres[:], 1.0, None, op0=ALU.add),
                     reads=dres.all, writes=dres.all)
                k.op("dve", lambda e: e.reciprocal(dres[:], dres[:]), reads=dres.all, writes=dres.all)
            k.op("dve", lambda e: e.tensor_scalar(b1[:], b1[:], -0.6065306597126334, None, op0=ALU.mult),
                 reads=b1.all, writes=b1.all)
            for fc in range(6):
                k.op("dve", lambda e: e.tensor_tensor_scan(CUM[:, fc, :], Mscan[:], LW[:, fc, :], 0.0,
                                                           op0=ALU.mult, op1=ALU.add),
                     reads=Mscan.all + b1.all, writes=b2.all)
            CUM4 = b2[:].rearrange("p (c n s) -> p c n s", c=6, s=64)
            k.op("dve", lambda e: e.tensor_copy(CL[:], CUM4[:, :, :, 63]), reads=b2.all, writes=CL.all)
            k.op("act", lambda e: e.activation(PC[:], CL[:], AF.Exp), reads=CL.all, writes=PC.all)
            if dbg <= 2:
                break
            tmp3_res, dst3_res = [b7.p()], [b4.p()]
            load_group(ZL, 1536, 6, t0)
            shift_mix(V3(b4), ZL, 12, 6, V3(b7))
            VM = V3(b4)

            def to_token_major(src3, sres, dstT):
                for pr in range(NPR):
                    for half in range(2):
                        ps = k.ps[4 + (pr * 2 + half) % 4]
                        for j in range(3):
                            fc = half * 3 + j
                            k.op("pe", lambda e: e.transpose(ps[:, j * 128:(j + 1) * 128],
                                                             src3[:, fc, pr * 128:(pr + 1) * 128], c["idf"][:]),
                                 reads=sres.all + c["idf"].all, writes=ps.all)
                        if half == 0:
                            k.op("dve", lambda e: e.tensor_copy(dstT[:, pr, 0:384], ps[:, 0:384]),
                                 reads=ps.all, writes=dstT.all)
                        else:
                            k.op("act", lambda e: e.activation(dstT[:, pr, 384:768], ps[:, 0:384], AF.Copy),
                                 reads=ps.all, writes=dstT.all)
            to_token_major(VM, b4, Vt)
            if dbg <= 3:
                break
            dst3_res = [b4.p()]
            load_group(ZL, 768, 6, t0)
            shift_mix(V3(b4), ZL, 6, 6, V3(b7))
            KM, KK, B3, T7 = V3(b4), V3(b5), V3(b6), V3(b7)
            k.op("dve", lambda e: e.tensor_tensor(KK, KM, kk_c[:].rearrange("p (c o) -> p c o", o=1)
                                                  .to_broadcast(W6), op=ALU.mult),
                 reads=b4.all + kk_c.all, writes=b5.all)
            k.op("pool", lambda e: e.tensor_tensor(B3, KK, KK, op=ALU.mult), reads=b5.all, writes=b6.all)
            for fc in range(6):
                ps = k.ps[fc % 4]
                k.mm(ps.p(), [(ps[:, 0:TBK], bones[:], B3[:, fc, :])], reads=bones.all + b6.all)
                k.op("act", lambda e: e.activation(T7[:, fc, :], ps[:, 0:TBK], AF.Ln, bias=tiny[:, 0:1]),
                     reads=ps.all + tiny.all, writes=b7.all)
            k.op("act", lambda e: e.activation(b7[:], b7[:], AF.Exp, scale=-0.5), reads=b7.all, writes=b7.all)
            k.op("dve", lambda e: e.tensor_tensor(b5[:], b5[:], b7[:], op=ALU.mult),
                 reads=b5.all + b7.all, writes=b5.all)
            for fc in range(6):
                k.op("dve",
                     lambda e: e.tensor_scalar(T7[:, fc, :], A3[:, fc, :], ka_c[:, fc:fc + 1], omka[:, fc:fc + 1],
                                               op0=ALU.mult, op1=ALU.add),
                     reads=b3.all + ka_c.all + omka.all, writes=b7.all)
            k.op("dve", lambda e: e.tensor_tensor(b4[:], b4[:], b7[:], op=ALU.mult),
                 reads=b4.all + b7.all, writes=b4.all)
            k.op("pool", lambda e: e.tensor_tensor(b6[:], b5[:], b3[:], op=ALU.mult),
                 reads=b5.all + b3.all, writes=b6.all)
            if dbg <= 4:
                break
            BK4, AR4 = BK, AR
            k.op("act", lambda e: e.activation(b3[:], b2[:], AF.Exp, scale=-1.0), reads=b2.all, writes=b3.all)
            k.op("dve", lambda e: e.tensor_tensor(BK4[:, :, 0, :], B3, V3(b3), op=ALU.mult),
                 reads=b6.all + b3.all, writes=BK.all)
            k.op("pool", lambda e: e.tensor_tensor(BK4[:, :, 1, :], KM, V3(b3), op=ALU.mult),
                 reads=b4.all + b3.all, writes=BK.all)
            k.op("dve", lambda e: e.scalar_tensor_tensor(
                b3[:].rearrange("p (c n s) -> p c n s", c=6, s=64), CUM4, -1.0,
                CL[:].rearrange("p c (n o) -> p c n o", o=1).to_broadcast([128, 6, NCH, 64]),
                op0=ALU.mult, op1=ALU.add), reads=b2.all + CL.all, writes=b3.all)
            k.op("act", lambda e: e.activation(b3[:], b3[:], AF.Exp), reads=b3.all, writes=b3.all)
            k.op("dve", lambda e: e.tensor_tensor(b6[:], b6[:], b3[:], op=ALU.mult),
                 reads=b6.all + b3.all, writes=b6.all)
            k.op("pool", lambda e: e.tensor_tensor(b7[:], b4[:], b3[:], op=ALU.mult),
                 reads=b4.all + b3.all, writes=b7.all)
            to_token_major(V3(b6), b6, BBt)
            to_token_major(V3(b7), b7, KBt)
            k.op("dve", lambda e: e.tensor_tensor(b3[:], b2[:], b1[:], op=ALU.subtract),
                 reads=b2.all + b1.all, writes=b3.all)
            k.op("act", lambda e: e.activation(b3[:], b3[:], AF.Exp), reads=b3.all, writes=b3.all)
            k.op("dve", lambda e: e.scalar_tensor_tensor(AR4[:, :, 0, :], KK, -1.0, V3(b3), op0=ALU.mult, op1=ALU.mult),
                 reads=b5.all + b3.all, writes=AR.all)
            if dbg <= 5:
                break
            tmp3_res, dst3_res = [b7.p()], [b1.p()]
            load_group(ZL, 0, 6, t0)
            shift_mix(V3(b1), ZL, 0, 6, V3(b7))
            RM = V3(b1)
            k.op("act", lambda e: e.activation(b3[:], b2[:], AF.Exp), reads=b2.all, writes=b3.all)
            k.op("dve", lambda e: e.tensor_tensor(AR4[:, :, 1, :], RM, V3(b3), op=ALU.mult),
                 reads=b1.all + b3.all, writes=AR.all)
            k.op("pool", lambda e: e.tensor_tensor(b7[:], b1[:], b4[:], op=ALU.mult),
                 reads=b1.all + b4.all, writes=b7.all)
            k.op("dve", lambda e: e.tensor_tensor(V3(b7), V3(b7), rk_c[:].rearrange("p (c o) -> p c o", o=1)
                                                  .to_broadcast(W6), op=ALU.mult),
                 reads=b7.all + rk_c.all, writes=b7.all)
            for pr in range(NPR):
                ps = k.ps[pr % 4]
                k.mmx(ps.p(), [(ps[:, fc * 2:fc * 2 + 2], V3(b7)[:, fc, pr * 128:(pr + 1) * 128], bsel[:],
                                True, True) for fc in range(6)], reads=b7.all + bsel.all)
                k.op("dve", lambda e: e.tensor_copy(bon[:, pr, :], ps[:, 0:12]), reads=ps.all, writes=bon.all)
            if dbg <= 6:
                break
            for pr in range(NPR):
                for half in range(2):
                    ps = k.ps[4 + (pr * 2 + half) % 4]
                    k.mm(ps.p(), [(ps[:, 0:384], sgA[:, pr * 128:(pr + 1) * 128], g2a[:, half * 384:(half + 1) * 384]),
                                  (ps[:, 0:384], sgB[:, pr * 128:(pr + 1) * 128], g2b[:, half * 384:(half + 1) * 384])],
                         reads=sgA.all + sgB.all + g2a.all + g2b.all)
                    k.op("act", lambda e: e.activation(Gt[:, pr, half * 384:(half + 1) * 384], ps[:, 0:384], AF.Copy),
                         reads=ps.all, writes=Gt.all)
            if dbg <= 7:
                break
            for pr in range(NPR):
                tc = slice(pr * 128, (pr + 1) * 128)
                bi = 0
                for which, dstT in ((0, T1), (1, T2)):
                    for hp in range(2):
                        rows = slice(hp * 64, hp * 64 + 64)
                        for f0 in range(0, 6, 2):
                            ps = k.ps[bi % 4]
                            bi += 1
                            mms = []
                            for ff in range(2):
                                fc = f0 + ff
                                mms.append((ps[:, ff * 256:(ff + 1) * 256].rearrange("p (a t) -> p a t", a=2),
                                            BK[rows, fc, which, tc], AR[rows, fc, :, tc], True, True))
                            k.mmx(ps.p(), mms, reads=BK.all + AR.all)
                            k.op("dve", lambda e: e.tensor_tensor(dstT[:, f0:f0 + 2, hp, :],
                                                                  ps[:, :].rearrange("p (h c) -> p h c", h=2), MASK1[:],
                                                                  op=ALU.mult),
                                 reads=ps.all + MASK1.all, writes=dstT.all)
                if dbg <= 8:
                    break
                groups = [(hp, f0) for hp in range(2) for f0 in (0, 3)]
                cur = 0
                for gi, (hp, f0) in enumerate(groups):
                    rows = slice(hp * 64, hp * 64 + 64)
                    psL, psT = k.ps[2 * gi], k.ps[2 * gi + 1]
                    k.mmx(psL.p(), [(psL[:, ff * 128:(ff + 1) * 128], AR[rows, f0 + ff, 0, tc], BK[rows, f0 + ff, 0, tc],
                                     True, True) for ff in range(3)], reads=BK.all + AR.all)
                    k.mmx(psT.p(), [(psT[:, ff * 128:(ff + 1) * 128], BK[rows, f0 + ff, 0, tc], AR[rows, f0 + ff, 0, tc],
                                     True, True) for ff in range(3)], reads=BK.all + AR.all)
                for gi, (hp, f0) in enumerate(groups):
                    psL, psT = k.ps[2 * gi], k.ps[2 * gi + 1]
                    Xg, XTg, TTg = X[gi], XT[gi], TT[gi]
                    k.op("dve", lambda e: e.tensor_tensor(Xg[0][:], psL[:, 0:384].rearrange("p (h c) -> p h c", h=3),
                                                          MASK3[:, 0:3, :], op=ALU.mult),
                         reads=psL.all + MASK3.all, writes=Xg[0].all)
                    k.op("dve", lambda e: e.tensor_tensor(XTg[0][:], psT[:, 0:384].rearrange("p (h c) -> p h c", h=3),
                                                          msT[:].rearrange("p (o c) -> p o c", o=1)
                                                          .to_broadcast([128, 3, 128]), op=ALU.mult),
                         reads=psT.all + msT.all, writes=XTg[0].all)
                    k.op("pool", lambda e: e.tensor_tensor(TTg[0][:], XTg[0][:],
                                                           c["idf"][:].rearrange("p (o c) -> p o c", o=1)
                                                           .to_broadcast([128, 3, 128]), op=ALU.add),
                         reads=XTg[0].all + c["idf"].all, writes=TTg[0].all)
                for lvl in range(5):
                    nxt = 1 - cur
                    for gi in range(4):
                        pa, pb = k.ps[2 * gi], k.ps[2 * gi + 1]
                        Xg, XTg = X[gi], XT[gi]
                        k.mmx(pa.p(), [(pa[:, hh * 128:(hh + 1) * 128], XTg[cur][:, hh, :], Xg[cur][:, hh, :], True, True)
                                       for hh in range(3)], reads=Xg[cur].all + XTg[cur].all)
                        if lvl < 4:
                            k.mmx(pb.p(), [(pb[:, hh * 128:(hh + 1) * 128], Xg[cur][:, hh, :], XTg[cur][:, hh, :],
                                            True, True) for hh in range(3)], reads=Xg[cur].all + XTg[cur].all)
                    for gi in range(4):
                        pa, pb = k.ps[2 * gi], k.ps[2 * gi + 1]
                        Xg, XTg = X[gi], XT[gi]
                        if gi % 2 == 0:
                            k.op("dve", lambda e: e.tensor_copy(Xg[nxt][:], pa[:, 0:384].rearrange("p (h c) -> p h c", h=3)),
                                 reads=pa.all, writes=Xg[nxt].all)
                        else:
                            k.op("act", lambda e: e.activation(Xg[nxt][:], pa[:, 0:384].rearrange("p (h c) -> p h c", h=3),
                                                               AF.Copy), reads=pa.all, writes=Xg[nxt].all)
                        if lvl < 4:
                            if gi % 2 == 1:
                                k.op("dve", lambda e: e.tensor_copy(XTg[nxt][:],
                                                                    pb[:, 0:384].rearrange("p (h c) -> p h c", h=3)),
                                     reads=pb.all, writes=XTg[nxt].all)
                            else:
                                k.op("act", lambda e: e.activation(XTg[nxt][:],
                                                                   pb[:, 0:384].rearrange("p (h c) -> p h c", h=3),
                                                                   AF.Copy), reads=pb.all, writes=XTg[nxt].all)
                    for gi in range(4):
                        pc2 = k.ps[2 * gi]
                        Xg, TTg = X[gi], TT[gi]
                        k.mmx(pc2.p(), [(pc2[:, hh * 128:(hh + 1) * 128], Xg[nxt][:, hh, :], TTg[cur][:, hh, :],
                                         True, True) for hh in range(3)], reads=Xg[nxt].all + TTg[cur].all)
                    for gi in range(4):
                        pc2 = k.ps[2 * gi]
                        TTg = TT[gi]
                        k.op("dve", lambda e: e.tensor_tensor(TTg[nxt][:], TTg[cur][:],
                                                              pc2[:, 0:384].rearrange("p (h c) -> p h c", h=3),
                                                              op=ALU.add),
                             reads=pc2.all + TTg[cur].all, writes=TTg[nxt].all)
                    cur = nxt
                for gi, (hp, f0) in enumerate(groups):
                    TTg = TT[gi]
                    k.op("act" if gi % 2 == 0 else "pool",
                         (lambda e: e.activation(TTb[:, f0:f0 + 3, hp, :], TTg[cur][:], AF.Copy))
                         if gi % 2 == 0 else
                         (lambda e: e.tensor_copy(TTb[:, f0:f0 + 3, hp, :], TTg[cur][:])),
                         reads=TTg[cur].all, writes=TTb.all)
                if dbg <= 9:
                    break
                for ci in range(2):
                    prs = slice(ci * 64, ci * 64 + 64)
                    tl = slice(ci * 64, ci * 64 + 64)
                    pw = [k.ps[4], k.ps[5]]
                    for half in range(2):
                        k.mmx(pw[half].p(), [(pw[half][prs, hh * 64:(hh + 1) * 64],
                                              T2[prs, (half * 6 + hh) // 2, (half * 6 + hh) % 2, tl],
                                              Vt[prs, pr, (half * 6 + hh) * 64:(half * 6 + hh + 1) * 64],
                                              hh == 0, hh == 5) for hh in range(6)], reads=T2.all + Vt.all)
                        k.op("act", lambda e: e.activation(Wsb[prs, half * 384:(half + 1) * 384], pw[half][prs, 0:384],
                                                           AF.Copy), reads=pw[half].all, writes=Wsb.all)
                    pu = [k.ps[6], k.ps[7]]
                    for half in range(2):
                        k.mmx(pu[half].p(), [(pu[half][prs, hh * 64:(hh + 1) * 64],
                                              TTb[prs, (half * 6 + hh) // 2, (half * 6 + hh) % 2, tl],
                                              Wsb[prs, (half * 6 + hh) * 64:(half * 6 + hh + 1) * 64],
                                              hh == 0, hh == 5) for hh in range(6)], reads=TTb.all + Wsb.all)
                        k.op("dve", lambda e: e.tensor_copy(U0[prs, half * 384:(half + 1) * 384], pu[half][prs, 0:384]),
                             reads=pu[half].all, writes=U0.all)
                for ci in range(2):
                    ch = pr * 2 + ci
                    prs = slice(ci * 64, ci * 64 + 64)
                    tcc = slice(pr * 128 + ci * 64, pr * 128 + ci * 64 + 64)
                    tl = slice(ci * 64, ci * 64 + 64)
                    pw = [k.ps[0], k.ps[1]]
                    py = [k.ps[2], k.ps[3]]
                    for (pbk, which) in ((pw, 0), (py, 1)):
                        for hp in range(2):
                            rows = slice(hp * 64, hp * 64 + 64)
                            k.mmx(pbk[hp].p(), [(pbk[hp][prs, fc * 64:(fc + 1) * 64], AR[rows, fc, which, tcc],
                                                 STb[rows, fc, :], fc == 0, fc == 5) for fc in range(6)],
                                  reads=AR.all + STb.all)
                    Wv = Wsb[:].rearrange("p (f h v) -> p f h v", f=6, h=2)
                    for hp in range(2):
                        k.op("dve" if hp == 0 else "act",
                             (lambda e: e.tensor_copy(Wv[prs, :, hp, :], pw[hp][prs, 0:384].rearrange("p (f v) -> p f v", f=6)))
                             if hp == 0 else
                             (lambda e: e.activation(Wv[prs, :, hp, :], pw[hp][prs, 0:384].rearrange("p (f v) -> p f v", f=6),
                                                     AF.Copy)),
                             reads=pw[hp].all, writes=Wsb.all)
                    pu = [k.ps[4], k.ps[5]]
                    for half in range(2):
                        k.mmx(pu[half].p(), [(pu[half][prs, hh * 64:(hh + 1) * 64],
                                              TTb[prs, (half * 6 + hh) // 2, (half * 6 + hh) % 2, tl],
                                              Wsb[prs, (half * 6 + hh) * 64:(half * 6 + hh + 1) * 64],
                                              hh == 0, hh == 5) for hh in range(6)], reads=TTb.all + Wsb.all)
                        k.op("dve", lambda e: e.tensor_tensor(Usb[prs, half * 384:(half + 1) * 384],
                                                              U0[prs, half * 384:(half + 1) * 384], pu[half][prs, 0:384],
                                                              op=ALU.add),
                             reads=pu[half].all + U0.all, writes=Usb.all)
                    pya = [k.ps[6], k.ps[7]]
                    for half in range(2):
                        mms = []
                        for hh in range(6):
                            h = half * 6 + hh
                            o = pya[half][prs, hh * 64:(hh + 1) * 64]
                            mms.append((o, T1[prs, h // 2, h % 2, 128 + ci * 64:128 + ci * 64 + 64],
                                        Usb[prs, h * 64:(h + 1) * 64], hh == 0, False))
                            mms.append((o, T2[prs, h // 2, h % 2, 128 + ci * 64:128 + ci * 64 + 64],
                                        Vt[prs, pr, h * 64:(h + 1) * 64], False, hh == 5))
                        k.mmx(pya[half].p(), mms, reads=T1.all + T2.all + Vt.all + Usb.all)
                    Yv = Ysb[:].rearrange("p (f h v) -> p f h v", f=6, h=2)
                    for hp in range(2):
                        k.op("act", lambda e: e.activation(Yv[prs, :, hp, :],
                                                           py[hp][prs, 0:384].rearrange("p (f v) -> p f v", f=6), AF.Copy),
                             reads=py[hp].all, writes=Ysb.all)
                    for half in range(2):
                        k.op("dve", lambda e: e.tensor_tensor(Ysb[prs, half * 384:(half + 1) * 384],
                                                              Ysb[prs, half * 384:(half + 1) * 384], pya[half][prs, 0:384],
                                                              op=ALU.add),
                             reads=pya[half].all + Ysb.all, writes=Ysb.all)
                    pss = k.ps[0]
                    mms = []
                    for h in range(12):
                        fc, hp = h // 2, h % 2
                        rows = slice(hp * 64, hp * 64 + 64)
                        o = pss[rows, fc * 64:(fc + 1) * 64]
                        mms.append((o, BBt[prs, pr, h * 64:(h + 1) * 64], Usb[prs, h * 64:(h + 1) * 64], h < 2, False))
                        mms.append((o, KBt[prs, pr, h * 64:(h + 1) * 64], Vt[prs, pr, h * 64:(h + 1) * 64], False, h == 11))
                    k.mmx(pss.p(), mms, reads=BBt.all + KBt.all + Vt.all + Usb.all)
                    k.op("dve", lambda e: e.tensor_tensor(Stmp[:], ST[:], PC[:, :, ch:ch + 1].to_broadcast([128, 6, 64]),
                                                          op=ALU.mult), reads=ST.all + PC.all, writes=Stmp.all)
                    k.op("dve", lambda e: e.tensor_tensor(ST[:], Stmp[:], pss[:, 0:384].rearrange("p (c v) -> p c v", c=6),
                                                          op=ALU.add), reads=Stmp.all + pss.all, writes=ST.all)
                    k.op("act", lambda e: e.activation(STb[:], ST[:], AF.Copy), reads=ST.all, writes=STb.all)
                if dbg <= 10:
                    break
                yw, yo = Yw[pr % 2], Yo[pr % 2]
                s1, s2, mean, rstd = st12
                Y3 = Ysb[:].rearrange("p (h v) -> p h v", h=12)
                k.op("dve", lambda e: e.reduce_sum(s1[:], Y3, axis=AX.X), reads=Ysb.all, writes=s1.all)
                k.op("pool", lambda e: e.tensor_tensor(yw[:], Ysb[:], Ysb[:], op=ALU.mult), reads=Ysb.all, writes=yw.all)
                k.op("dve", lambda e: e.reduce_sum(s2[:], yw[:].rearrange("p (h v) -> p h v", h=12), axis=AX.X),
                     reads=yw.all, writes=s2.all)
                k.op("dve", lambda e: e.tensor_scalar(mean[:], s1[:], 1.0 / 64, None, op0=ALU.mult),
                     reads=s1.all, writes=mean.all)
                k.op("dve", lambda e: e.tensor_tensor(s1[:], mean[:], mean[:], op=ALU.mult), reads=mean.all, writes=s1.all)
                k.op("dve", lambda e: e.scalar_tensor_tensor(s2[:], s2[:], 1.0 / 64, s1[:], op0=ALU.mult, op1=ALU.subtract),
                     reads=s2.all + s1.all, writes=s2.all)
                k.op("act", lambda e: e.activation(rstd[:], s2[:], AF.Ln, bias=gneps[:, 0:1]),
                     reads=s2.all + gneps.all, writes=rstd.all)
                k.op("act", lambda e: e.activation(rstd[:], rstd[:], AF.Exp, scale=-0.5), reads=rstd.all, writes=rstd.all)
                yw3 = yw[:].rearrange("p (h v) -> p h v", h=12)
                k.op("dve", lambda e: e.tensor_tensor(yw3, Y3, mean[:].rearrange("p (h o) -> p h o", o=1)
                                                      .to_broadcast([128, 12, 64]), op=ALU.subtract),
                     reads=Ysb.all + mean.all, writes=yw.all)
                k.op("dve", lambda e: e.tensor_tensor(yw3, yw3, rstd[:].rearrange("p (h o) -> p h o", o=1)
                                                      .to_broadcast([128, 12, 64]), op=ALU.mult),
                     reads=yw.all + rstd.all, writes=yw.all)
                k.op("pool", lambda e: e.tensor_tensor(yw[:], yw[:], lnw[:], op=ALU.mult),
                     reads=yw.all + lnw.all, writes=yw.all)
                k.op("pool", lambda e: e.tensor_tensor(yw[:], yw[:], lnb[:], op=ALU.add),
                     reads=yw.all + lnb.all, writes=yw.all)
                yo3 = yo[:].rearrange("p (h v) -> p h v", h=12)
                k.op("dve", lambda e: e.tensor_tensor(yo3, Vt[:, pr, :].rearrange("p (h v) -> p h v", h=12),
                                                      bon[:, pr, :].rearrange("p (h o) -> p h o", o=1)
                                                      .to_broadcast([128, 12, 64]), op=ALU.mult),
                     reads=Vt.all + bon.all, writes=yo.all)
                k.op("dve", lambda e: e.tensor_tensor(yo[:], yo[:], yw[:], op=ALU.add),
                     reads=yo.all + yw.all, writes=yo.all)
                k.op("dve", lambda e: e.tensor_tensor(yo[:], yo[:], Gt[:, pr, :], op=ALU.mult),
                     reads=yo.all + Gt.all, writes=yo.all)
                with nc.allow_non_contiguous_dma(reason="mix slice"):
                    k.dma("sp", MIX[t0 + pr * 128:t0 + (pr + 1) * 128, 0:768], yo[:], reads=yo.all)
        k.barrier()


SEQ = 4096
DEPTH = 4
PARAM_SHAPES = {
    "norm1": (4, 1024), "norm_mem": (4, 1024), "w_mem_kv": (4, 1024, 512), "w_o": (4, 1024, 1024),
    "norm2": (4, 1024), "w_ffn_in": (4, 1024, 5632), "w_ffn_out": (4, 2816, 1024),
    "nsa_w_in": (2, 1024, 2596), "nsa_gate_b": (2, 36), "nsa_cmp_pos": (2, 2, 32, 64),
    "nsa_cmp_w1": (2, 2, 32, 64, 128), "nsa_cmp_w2": (2, 2, 128, 64),
    "rw_w_in": (2, 1024, 2848), "rw_mu": (2, 2592), "rw_w0": (2, 768), "rw_w2": (2, 64, 768),
    "rw_a0": (2, 768), "rw_a2": (2, 64, 768), "rw_g2": (2, 160, 768), "rw_k_k": (2, 768), "rw_k_a": (2, 768),
    "rw_r_k": (2, 12, 64), "rw_lnx_w": (2, 768), "rw_lnx_b": (2, 768), "final_norm": (1024,),
}


def build_program(S=SEQ, depth=DEPTH):
    nc = bass.Bass("TRN2", target_bir_lowering=False)
    X0 = nc.dram_tensor("x", [S, 1024], F32, kind="ExternalInput").ap()
    MEM = nc.dram_tensor("mem", [256, 1024], F32, kind="ExternalInput").ap()
    P = {n: nc.dram_tensor(n, list(sh), F32, kind="ExternalInput").ap() for n, sh in PARAM_SHAPES.items()}
    OUT = nc.dram_tensor("out", [S, 1024], F32, kind="ExternalOutput").ap()

    def SC(name, shape, dt):
        return nc.dram_tensor(name, list(shape), dt, kind="Internal").ap()
    XA, XB = SC("XA", (S, 1024), F32), SC("XB", (S, 1024), F32)
    MIX = SC("MIX", (S, 1024), F32)
    sc = dict(QT=SC("QT", (768, S), BF16), KCT=SC("KCT", (256, S), BF16), VCT=SC("VCT", (256, S), BF16),
              KST=SC("KST", (256, S), BF16), KWT=SC("KWT", (256, S), BF16), VS=SC("VS", (S, 256), BF16),
              VW=SC("VW", (S, 256), BF16), GL=SC("GL", (S, 36), F32), QMT=SC("QMT", (256, S), BF16))
    ZT = SC("ZT", (2592, S), F32)
    with contextlib.ExitStack() as st:
        k = K(nc, st)
        with contextlib.ExitStack() as ph0:
            c = make_consts(k, ph0)
            Xcur = X0
            nsa_cache = {}
            for i in range(depth):
                j = i // 2
                if i % 2 == 0:
                    outs = [dict(c0=0, n=768, mode="fm", dst=sc["QT"], dt=BF16),
                            dict(c0=768, n=256, mode="fm", dst=sc["KCT"], dt=BF16),
                            dict(c0=1024, n=256, mode="fm", dst=sc["VCT"], dt=BF16),
                            dict(c0=1280, n=256, mode="fm", dst=sc["KST"], dt=BF16),
                            dict(c0=1536, n=256, mode="tm", dst=sc["VS"], dt=BF16),
                            dict(c0=1792, n=256, mode="fm", dst=sc["KWT"], dt=BF16),
                            dict(c0=2048, n=256, mode="tm", dst=sc["VW"], dt=BF16),
                            dict(c0=2304, n=36, mode="tm", dst=sc["GL"], dt=F32),
                            dict(c0=2340, n=256, mode="fm", dst=sc["QMT"], dt=BF16)]
                    phase_proj(k, c, Xcur, P["norm1"][i], P["nsa_w_in"][j], outs, S)
                    phase_nsa(k, c, sc, P["nsa_gate_b"][j], P["nsa_cmp_pos"][j], P["nsa_cmp_w1"][j],
                              P["nsa_cmp_w2"][j], MIX, S, cache=nsa_cache)
                else:
                    outs = [dict(c0=0, n=2592, mode="fm", dst=ZT, dt=F32),
                            dict(c0=2592, n=256, mode="fm", dst=sc["QMT"], dt=BF16)]
                    phase_proj(k, c, Xcur, P["norm1"][i], P["rw_w_in"][j], outs, S)
                    prm = dict(mu=P["rw_mu"][j], w0=P["rw_w0"][j], w2=P["rw_w2"][j], a0=P["rw_a0"][j],
                               a2=P["rw_a2"][j], g2=P["rw_g2"][j], k_k=P["rw_k_k"][j], k_a=P["rw_k_a"][j],
                               r_k=P["rw_r_k"][j], lnx_w=P["rw_lnx_w"][j], lnx_b=P["rw_lnx_b"][j])
                    phase_rwkv(k, c, ZT, prm, MIX, S)
                phase_memattn(k, c, MEM, P["norm_mem"][i], P["w_mem_kv"][i], sc["QMT"], MIX, S)
                phase_oproj(k, c, Xcur, XA, MIX, P["w_o"][i], S)
                phase_ffn(k, c, XA, XB, P["norm2"][i], P["w_ffn_in"][i], P["w_ffn_out"][i], S)
                Xcur = XB
            phase_final_norm(k, c, Xcur, OUT, P["final_norm"], S)
    return nc


def kernel(**inputs):
    x = np.ascontiguousarray(inputs["x"], dtype=np.float32)
    mem = np.ascontiguousarray(inputs["mem"], dtype=np.float32)
    B = x.shape[0]
    nc = build_program()
    params = {n: np.ascontiguousarray(inputs[n], dtype=np.float32) for n in PARAM_SHAPES}
    in_maps = []
    for b in range(B):
        d = dict(params)
        d["x"] = x[b]
        d["mem"] = mem[b]
        in_maps.append(d)
    res = run_bass_kernel_spmd(nc, in_maps, core_ids=list(range(B)))
    return np.stack([np.asarray(r["out"], dtype=np.float32) for r in res.results], axis=0)
```

```python
import contextlib
import numpy as np
import concourse.bass as bass
import concourse.mybir as mybir
from concourse.bass_utils import run_bass_kernel_spmd

F32 = mybir.dt.float32
BF16 = mybir.dt.bfloat16
AF = mybir.ActivationFunctionType
ALU = mybir.AluOpType
AX = mybir.AxisListType

D = 1024
FFN_H = 2816
RMS_EPS = 1e-6
NDMA = 24


class Res:
    __slots__ = ("w", "r")

    def __init__(self):
        self.w = None
        self.r = {}


class Eng:
    def __init__(self, name, eng, sem):
        self.name, self.eng, self.sem = name, eng, sem
        self.count = 0
        self.seen = {}


class Tile:
    def __init__(self, t, parts=1):
        self.t = t
        self.parts = [Res() for _ in range(parts)]

    def __getitem__(self, idx):
        return self.t[idx]

    def p(self, i=0):
        return self.parts[i]

    @property
    def all(self):
        return list(self.parts)


class K:
    def __init__(self, nc, stack):
        self.nc = nc
        self.st = stack
        self.E = {}
        for nm, eng in (("pe", nc.tensor), ("dve", nc.vector), ("act", nc.scalar),
                        ("pool", nc.gpsimd), ("sp", nc.sync)):
            sem = stack.enter_context(nc.semaphore("sem_" + nm))
            self.E[nm] = Eng(nm, eng, sem)
        self.slots = []
        for i in range(NDMA):
            sem = stack.enter_context(nc.semaphore("dsem%d" % i))
            self.slots.append([("dma%d" % i), sem, 0])
        self.rr = 0
        self.ps = [Tile(stack.enter_context(nc.psum_tensor("ps%d" % i, [128, 512], F32)))
                   for i in range(8)]
        self.uid = 0

    def sb(self, ph, shape, dtype, name, parts=1):
        self.uid += 1
        return Tile(ph.enter_context(self.nc.sbuf_tensor("%s_%d" % (name, self.uid), shape, dtype)), parts)

    def dram(self, name, shape, dtype, kind="Internal", parts=1):
        return Tile(self.nc.dram_tensor(name, shape, dtype, kind=kind), parts)

    def _wait(self, E, tok):
        key, sem, val = tok
        if E.seen.get(key, 0) < val:
            E.eng.wait_ge(sem, val)
            E.seen[key] = val

    def _deps(self, E, reads, writes):
        pe = E.name == "pe"
        for r in reads:
            if r.w is not None and not (pe and r.w[0] == "pe"):
                self._wait(E, r.w)
        for w in writes:
            if w.w is not None and not (pe and w.w[0] == "pe"):
                self._wait(E, w.w)
            for key, tok in w.r.items():
                if key != E.name:
                    self._wait(E, tok)

    def _mark(self, tok, reads, writes):
        for r in reads:
            r.r[tok[0]] = tok
        for w in writes:
            w.w = tok
            w.r = {}

    def op(self, en, fn, reads=(), writes=()):
        E = self.E[en]
        self._deps(E, reads, writes)
        inst = fn(E.eng)
        E.count += 1
        inst.then_inc(E.sem, 1)
        tok = (E.name, E.sem, E.count)
        E.seen[E.name] = E.seen.get(E.name, 0)
        self._mark(tok, reads, writes)
        return tok

    def mm(self, out_res, mms, reads=()):
        E = self.E["pe"]
        self._deps(E, reads, [out_res])
        n = len(mms)
        inst = None
        for i, (o, l, r) in enumerate(mms):
            inst = E.eng.matmul(o, l, r, start=(i == 0), stop=(i == n - 1))
        E.count += 1
        inst.then_inc(E.sem, 1)
        tok = (E.name, E.sem, E.count)
        self._mark(tok, reads, [out_res])
        return tok

    def mmx(self, out_res, mms, reads=()):
        E = self.E["pe"]
        self._deps(E, reads, [out_res])
        inst = None
        for (o, l, r, st, sp) in mms:
            inst = E.eng.matmul(o, l, r, start=st, stop=sp, skip_group_check=True)
        E.count += 1
        inst.then_inc(E.sem, 1)
        tok = (E.name, E.sem, E.count)
        self._mark(tok, reads, [out_res])
        return tok

    def dma(self, qn, out_ap, in_ap, reads=(), writes=(), **kw):
        E = self.E[qn]
        self._deps(E, reads, writes)
        slot = self.slots[self.rr]
        self.rr = (self.rr + 1) % NDMA
        if slot[2] > 0:
            self._wait(E, (slot[0], slot[1], slot[2]))
        inst = E.eng.dma_start(out=out_ap, in_=in_ap, **kw)
        slot[2] += 16
        inst.then_inc(slot[1], 16)
        tok = (slot[0], slot[1], slot[2])
        self._mark(tok, reads, writes)
        return tok

    def barrier(self):
        toks = [(e.name, e.sem, e.count) for e in self.E.values() if e.count > 0]
        toks += [(s[0], s[1], s[2]) for s in self.slots if s[2] > 0]
        for E in self.E.values():
            for t in toks:
                if t[0] != E.name:
                    self._wait(E, t)


def make_consts(k, ph):
    nc = k.nc
    c = {}
    idf = k.sb(ph, [128, 128], F32, "identf")
    k.op("pool", lambda e: e.memset(idf[:], 1.0), writes=idf.all)
    k.op("pool", lambda e: e.affine_select(idf[:], idf[:], pattern=[[-1, 128]], compare_op=ALU.is_equal,
                                           fill=0.0, base=0, channel_multiplier=1),
         reads=idf.all, writes=idf.all)
    idb = k.sb(ph, [128, 128], BF16, "identb")
    k.op("dve", lambda e: e.tensor_copy(idb[:], idf[:]), reads=idf.all, writes=idb.all)
    c["idf"], c["idb"] = idf, idb
    return c


def load_vec_bcast(k, ph, dram_ap_1d, n, name, q="sp"):
    t = k.sb(ph, [128, n], F32, name)
    k.dma(q, t[:], dram_ap_1d.partition_broadcast(128), writes=t.all)
    return t


def load_vec_col(k, ph, dram_ap_1d, nch, name, q="sp"):
    t = k.sb(ph, [128, nch], F32, name)
    with k.nc.allow_non_contiguous_dma(reason="small vector load"):
        k.dma(q, t[:], dram_ap_1d.rearrange("(c p) -> p c", p=128), writes=t.all)
    return t


def norm_block(k, c, xts, gcol, hT, hparts, scr, psbase=0, col0=0, eps=RMS_EPS):
    n = len(xts)
    sq, ss, rstd = scr["sq"], scr["ss"], scr["rstd"]
    for i, x in enumerate(xts):
        k.op("act", lambda e: e.activation(sq[:], x[:], AF.Square, accum_out=ss[:, i:i + 1]),
             reads=x.all, writes=sq.all + ss.all)
    k.op("act", lambda e: e.activation(rstd[:, 0:n], ss[:, 0:n], AF.Ln, scale=1.0 / D, bias=scr["eps"][:, 0:1]),
         reads=ss.all + scr["eps"].all, writes=rstd.all)
    k.op("act", lambda e: e.activation(rstd[:, 0:n], rstd[:, 0:n], AF.Exp, scale=-0.5),
         reads=rstd.all, writes=rstd.all)
    for i, x in enumerate(xts):
        xn = scr["xn"][i % 2]
        k.op("dve", lambda e: e.tensor_scalar(xn[:], x[:], rstd[:, i:i + 1], None, op0=ALU.mult),
             reads=x.all + rstd.all, writes=xn.all)
        for g4 in range(2):
            ps = k.ps[psbase + (i % 2) * 2 + g4]
            for j in range(4):
                ch = g4 * 4 + j
                k.op("pe", lambda e: e.transpose(ps[:, j * 128:(j + 1) * 128],
                                                 xn[:, ch * 128:(ch + 1) * 128], c["idf"][:]),
                     reads=xn.all + c["idf"].all, writes=ps.all)
            for j in range(4):
                ch = g4 * 4 + j
                if j % 2 == 0:
                    k.op("dve", lambda e: e.tensor_scalar(hT[:, ch, col0 + i * 128:col0 + (i + 1) * 128],
                                                          ps[:, j * 128:(j + 1) * 128],
                                                          gcol[:, ch:ch + 1], None, op0=ALU.mult),
                         reads=ps.all + gcol.all, writes=[hparts[i]])
                else:
                    k.op("act", lambda e: e.activation(hT[:, ch, col0 + i * 128:col0 + (i + 1) * 128],
                                                       ps[:, j * 128:(j + 1) * 128],
                                                       AF.Copy, scale=gcol[:, ch:ch + 1]),
                         reads=ps.all + gcol.all, writes=[hparts[i]])


def norm_scratch(k, ph, n):
    scr = dict(sq=k.sb(ph, [128, 1024], F32, "sq"), ss=k.sb(ph, [128, n], F32, "ss"),
               rstd=k.sb(ph, [128, n], F32, "rstd"),
               xn=[k.sb(ph, [128, 1024], F32, "xn%d" % i) for i in range(2)],
               eps=k.sb(ph, [128, 1], F32, "epsc"))
    k.op("pool", lambda e: e.memset(scr["eps"][:], RMS_EPS), writes=scr["eps"].all)
    return scr


def phase_ffn(k, c, X, Xout, g2_ap, w_in_ap, w_out_ap, S):
    nc = k.nc
    TB = 1024
    nblk = S // TB
    with contextlib.ExitStack() as ph:
        gcol = load_vec_col(k, ph, g2_ap, 8, "g2col")
        wout = k.sb(ph, [128, 22, 1024], BF16, "wout")
        k.dma("pool", wout[:], w_out_ap.rearrange("(c p) m -> p c m", p=128), writes=wout.all)
        hT = k.sb(ph, [128, 8, TB], BF16, "hT", parts=8)
        actT = k.sb(ph, [128, 22, TB], BF16, "actT", parts=44)
        xt = [k.sb(ph, [128, 1024], F32, "xt%d" % i) for i in range(8)]
        scr = norm_scratch(k, ph, 8)
        wg = [k.sb(ph, [128, 8, 256], BF16, "wg%d" % i) for i in range(2)]
        wu = [k.sb(ph, [128, 8, 256], BF16, "wu%d" % i) for i in range(2)]
        sg = [k.sb(ph, [128, 512], F32, "sg%d" % i) for i in range(2)]
        xo = [k.sb(ph, [128, 1024], F32, "xo%d" % i) for i in range(2)]
        w_in_v = w_in_ap.rearrange("(c p) m -> p c m", p=128)
        for blk in range(nblk):
            t0 = blk * TB
            for tt in range(TB // 128):
                k.dma("sp", xt[tt][:], X[t0 + tt * 128:t0 + (tt + 1) * 128, :], writes=xt[tt].all)
            norm_block(k, c, xt, gcol, hT, hT.parts, scr, psbase=0)
            it = 0
            for j in range(11):
                b = j % 2
                k.dma("pool", wg[b][:], w_in_v[:, :, j * 256:(j + 1) * 256], writes=wg[b].all)
                k.dma("pool", wu[b][:], w_in_v[:, :, FFN_H + j * 256:FFN_H + (j + 1) * 256], writes=wu[b].all)
                for half in range(2):
                    mch = j * 2 + half
                    for th in range(TB // 512):
                        pg = k.ps[4 + (it % 2) * 2]
                        pu = k.ps[5 + (it % 2) * 2]
                        s = sg[it % 2]
                        it += 1
                        hreads = [hT.p(th * 4 + i) for i in range(4)]
                        k.mm(pg.p(), [(pg[:, :], wg[b][:, kc, half * 128:(half + 1) * 128],
                                       hT[:, kc, th * 512:(th + 1) * 512]) for kc in range(8)],
                             reads=hreads + wg[b].all)
                        k.mm(pu.p(), [(pu[:, :], wu[b][:, kc, half * 128:(half + 1) * 128],
                                       hT[:, kc, th * 512:(th + 1) * 512]) for kc in range(8)],
                             reads=hreads + wu[b].all)
                        k.op("act", lambda e: e.activation(s[:], pg[:, :], AF.Silu),
                             reads=pg.all, writes=s.all)
                        ar = actT.p(mch * 2 + th)
                        k.op("dve", lambda e: e.tensor_tensor(actT[:, mch, th * 512:(th + 1) * 512], s[:], pu[:, :],
                                                              op=ALU.mult),
                             reads=s.all + pu.all, writes=[ar])
            for tt in range(TB // 128):
                x = xt[tt]
                o = xo[tt % 2]
                for chh in range(2):
                    po = k.ps[(tt % 2) * 2 + chh]
                    th = tt // 4
                    k.mm(po.p(), [(po[:, :], actT[:, kc, tt * 128:(tt + 1) * 128],
                                   wout[:, kc, chh * 512:(chh + 1) * 512]) for kc in range(22)],
                         reads=[actT.p(kc * 2 + th) for kc in range(22)] + wout.all)
                    k.op("dve", lambda e: e.tensor_tensor(o[:, chh * 512:(chh + 1) * 512],
                                                          x[:, chh * 512:(chh + 1) * 512], po[:, :], op=ALU.add),
                         reads=x.all + po.all, writes=o.all)
                k.dma("pool", Xout[t0 + tt * 128:t0 + (tt + 1) * 128, :], o[:], reads=o.all)
        k.barrier()


def phase_final_norm(k, c, X, OUT, g_ap, S):
    with contextlib.ExitStack() as ph:
        gb = load_vec_bcast(k, ph, g_ap, 1024, "gfin")
        epsc = k.sb(ph, [128, 1], F32, "epsf")
        k.op("pool", lambda e: e.memset(epsc[:], RMS_EPS), writes=epsc.all)
        xt = [k.sb(ph, [128, 1024], F32, "fx%d" % i) for i in range(2)]
        sq = [k.sb(ph, [128, 1024], F32, "fsq%d" % i) for i in range(2)]
        ss = [k.sb(ph, [128, 1], F32, "fss%d" % i) for i in range(2)]
        xo = [k.sb(ph, [128, 1024], F32, "fo%d" % i) for i in range(2)]
        for tt in range(S // 128):
            b = tt % 2
            x, q, s, o = xt[b], sq[b], ss[b], xo[b]
            k.dma("sp", x[:], X[tt * 128:(tt + 1) * 128, :], writes=x.all)
            k.op("act", lambda e: e.activation(q[:], x[:], AF.Square, accum_out=s[:]),
                 reads=x.all, writes=q.all + s.all)
            k.op("act", lambda e: e.activation(s[:], s[:], AF.Ln, scale=1.0 / D, bias=epsc[:, 0:1]),
                 reads=s.all + epsc.all, writes=s.all)
            k.op("act", lambda e: e.activation(s[:], s[:], AF.Exp, scale=-0.5),
                 reads=s.all, writes=s.all)
            k.op("dve", lambda e: e.scalar_tensor_tensor(o[:], x[:], s[:, 0:1], gb[:], op0=ALU.mult, op1=ALU.mult),
                 reads=x.all + s.all + gb.all, writes=o.all)
            k.dma("pool", OUT[tt * 128:(tt + 1) * 128, :], o[:], reads=o.all)
        k.barrier()


def phase_proj(k, c, X, g1_ap, W_ap, outs, S):
    with contextlib.ExitStack() as ph:
        gcol = load_vec_col(k, ph, g1_ap, 8, "g1col")
        hT = k.sb(ph, [128, 8, S], BF16, "hnT", parts=S // 128)
        xt = [k.sb(ph, [128, 1024], F32, "pxt%d" % i) for i in range(8)]
        scr = norm_scratch(k, ph, 8)
        for blk in range(S // 1024):
            t0 = blk * 1024
            for tt in range(8):
                k.dma("sp", xt[tt][:], X[t0 + tt * 128:t0 + (tt + 1) * 128, :], writes=xt[tt].all)
            norm_block(k, c, xt, gcol, hT, hT.parts[blk * 8:(blk + 1) * 8], scr, psbase=0, col0=t0)
        wsl = [k.sb(ph, [128, 8, 512], BF16, "wsl%d" % i) for i in range(2)]
        stg = {}
        W_v = W_ap.rearrange("(c p) m -> p c m", p=128)
        si = 0
        ei = 0
        for o in outs:
            c0, n, mode, dst, dt = o["c0"], o["n"], o["mode"], o["dst"], o["dt"]
            for s0 in range(0, n, 512):
                sn = min(512, n - s0)
                w = wsl[si % 2]
                si += 1
                k.dma("pool", w[:, :, 0:sn], W_v[:, :, c0 + s0:c0 + s0 + sn], writes=w.all)
                if mode == "fm":
                    for m0 in range(0, sn, 128):
                        mn = min(128, sn - m0)
                        key = ("fm", dt)
                        if key not in stg:
                            stg[key] = [[k.sb(ph, [128, S], dt, "stgfm%d" % i) for i in range(2)], 0]
                        st = stg[key][0][stg[key][1] % 2]
                        stg[key][1] += 1
                        for tb in range(S // 512):
                            ps = k.ps[4 + ei % 4]
                            k.mm(ps.p(), [(ps[0:mn, :], w[:, kc, m0:m0 + mn], hT[:, kc, tb * 512:(tb + 1) * 512])
                                          for kc in range(8)],
                                 reads=hT.parts[tb * 4:(tb + 1) * 4] + w.all)
                            if ei % 2 == 0:
                                k.op("dve", lambda e: e.tensor_copy(st[0:mn, tb * 512:(tb + 1) * 512], ps[0:mn, :]),
                                     reads=ps.all, writes=st.all)
                            else:
                                k.op("act", lambda e: e.activation(st[0:mn, tb * 512:(tb + 1) * 512], ps[0:mn, :],
                                                                   AF.Copy),
                                     reads=ps.all, writes=st.all)
                            ei += 1
                        k.dma("sp", dst[s0 + m0:s0 + m0 + mn, :], st[0:mn, :], reads=st.all)
                else:
                    key = ("tm", dt)
                    if key not in stg:
                        stg[key] = [[k.sb(ph, [128, 512], dt, "stgtm%d" % i) for i in range(4)], 0]
                    for tt in range(S // 128):
                        st = stg[key][0][stg[key][1] % 4]
                        stg[key][1] += 1
                        ps = k.ps[4 + ei % 4]
                        k.mm(ps.p(), [(ps[:, 0:sn], hT[:, kc, tt * 128:(tt + 1) * 128], w[:, kc, 0:sn])
                                      for kc in range(8)],
                             reads=[hT.p(tt)] + w.all)
                        if ei % 2 == 0:
                            k.op("dve", lambda e: e.tensor_copy(st[:, 0:sn], ps[:, 0:sn]), reads=ps.all, writes=st.all)
                        else:
                            k.op("act", lambda e: e.activation(st[:, 0:sn], ps[:, 0:sn], AF.Copy),
                                 reads=ps.all, writes=st.all)
                        ei += 1
                        k.dma("sp", dst[tt * 128:(tt + 1) * 128, s0:s0 + sn], st[:, 0:sn], reads=st.all)
        k.barrier()


def phase_oproj(k, c, X, Xout, MIX, wo_ap, S):
    with contextlib.ExitStack() as ph:
        wo = k.sb(ph, [128, 8, 1024], BF16, "wo")
        k.dma("pool", wo[:], wo_ap.rearrange("(c p) m -> p c m", p=128), writes=wo.all)
        mt = [k.sb(ph, [128, 1024], F32, "mixt%d" % i) for i in range(2)]
        mT = [k.sb(ph, [128, 8, 128], BF16, "mixT%d" % i) for i in range(2)]
        xt = [k.sb(ph, [128, 1024], F32, "oxt%d" % i) for i in range(2)]
        xo = [k.sb(ph, [128, 1024], F32, "oxo%d" % i) for i in range(2)]
        for tt in range(S // 128):
            b = tt % 2
            k.dma("sp", mt[b][:], MIX[tt * 128:(tt + 1) * 128, :], writes=mt[b].all)
            k.dma("sp", xt[b][:], X[tt * 128:(tt + 1) * 128, :], writes=xt[b].all)
            for g4 in range(2):
                ps = k.ps[b * 2 + g4]
                for j in range(4):
                    ch = g4 * 4 + j
                    k.op("pe", lambda e: e.transpose(ps[:, j * 128:(j + 1) * 128],
                                                     mt[b][:, ch * 128:(ch + 1) * 128], c["idf"][:]),
                         reads=mt[b].all + c["idf"].all, writes=ps.all)
                if g4 == 0:
                    k.op("dve", lambda e: e.tensor_copy(mT[b][:, 0:4, :],
                                                        ps[:, :].rearrange("p (c t) -> p c t", c=4)),
                         reads=ps.all, writes=mT[b].all)
                else:
                    k.op("act", lambda e: e.activation(mT[b][:, 4:8, :],
                                                       ps[:, :].rearrange("p (c t) -> p c t", c=4), AF.Copy),
                         reads=ps.all, writes=mT[b].all)
            for chh in range(2):
                po = k.ps[4 + b * 2 + chh]
                k.mm(po.p(), [(po[:, :], mT[b][:, kc, :], wo[:, kc, chh * 512:(chh + 1) * 512]) for kc in range(8)],
                     reads=mT[b].all + wo.all)
                k.op("dve", lambda e: e.tensor_tensor(xo[b][:, chh * 512:(chh + 1) * 512],
                                                      xt[b][:, chh * 512:(chh + 1) * 512], po[:, :], op=ALU.add),
                     reads=xt[b].all + po.all, writes=xo[b].all)
            k.dma("pool", Xout[tt * 128:(tt + 1) * 128, :], xo[b][:], reads=xo[b].all)
        k.barrier()


def alibi_slopes12():
    def pow2(m):
        start = 2.0 ** (-8.0 / m)
        return [start ** (i + 1) for i in range(m)]
    s = pow2(8)
    s = s + pow2(16)[0::2][:4]
    return [float(np.float32(v)) for v in s]


class AttnBufs:
    def __init__(self, k, ph, n=3):
        self.n = n
        self.u = [k.sb(ph, [128, 512], F32, "au%d" % i) for i in range(n)]
        self.E = [k.sb(ph, [128, 512], BF16, "aE%d" % i) for i in range(n)]
        self.i = 0


def score_unit(k, ab, mms, reads, Dt=None, Dres=(), slope=0.0, bias=0.0, cols=(0, 512)):
    c0, c1 = cols
    i = ab.i % ab.n
    ab.i += 1
    ps, u, E = k.ps[i], ab.u[i], ab.E[i]
    k.mm(ps.p(), [(ps[:, c0:c1], l, r) for (l, r) in mms], reads=reads)
    if Dt is not None:
        k.op("dve", lambda e: e.scalar_tensor_tensor(u[:, c0:c1], Dt[:, c0:c1], float(slope), ps[:, c0:c1],
                                                     op0=ALU.mult, op1=ALU.add),
             reads=ps.all + list(Dres), writes=u.all)
        k.op("act", lambda e: e.activation(E[:, c0:c1], u[:, c0:c1], AF.Exp, scale=0.125, bias=float(bias)),
             reads=u.all, writes=E.all)
    else:
        k.op("act", lambda e: e.activation(E[:, c0:c1], ps[:, c0:c1], AF.Exp, scale=0.125),
             reads=ps.all, writes=E.all)
    return E


SKIP_EXP = 150.0


def run_pipeline(units, score_fn, pv_fn, la=2):
    Es = [None] * len(units)
    for i in range(len(units) + la):
        if i < len(units):
            Es[i] = score_fn(units[i])
        j = i - la
        if j >= 0:
            pv_fn(units[j], Es[j])
            Es[j] = None


def pv_accum(k, acc, accv, E, rhs, rhs_res, first, last, nsub=4, cols=None):
    qss = list(range(nsub)) if cols is None else list(range(cols[0] // 128, cols[1] // 128))
    mms = []
    for ii, qs in enumerate(qss):
        mms.append((accv[:, qs, :], E[:, qs * 128:(qs + 1) * 128], rhs,
                    bool(first and ii == 0), bool(last and ii == len(qss) - 1)))
    k.mmx(acc.p(), mms, reads=E.all + list(rhs_res))


def phase_memattn(k, c, MEM, gm_ap, wkv_ap, QMT, MIX, S):
    with contextlib.ExitStack() as ph:
        gcol = load_vec_col(k, ph, gm_ap, 8, "gmcol")
        mt = [k.sb(ph, [128, 1024], F32, "memt%d" % i) for i in range(2)]
        scr = norm_scratch(k, ph, 2)
        mnT = k.sb(ph, [128, 8, 256], BF16, "mnT", parts=2)
        for i in range(2):
            k.dma("sp", mt[i][:], MEM[i * 128:(i + 1) * 128, :], writes=mt[i].all)
        norm_block(k, c, mt, gcol, mnT, mnT.parts, scr, psbase=4)
        wkv = k.sb(ph, [128, 8, 512], BF16, "wkv")
        k.dma("pool", wkv[:], wkv_ap.rearrange("(c p) m -> p c m", p=128), writes=wkv.all)
        kmT = k.sb(ph, [64, 4, 256], BF16, "kmT")
        vm = k.sb(ph, [128, 2, 4, 65], BF16, "vmaug")
        k.op("pool", lambda e: e.memset(vm[:], 1.0), writes=vm.all)
        for h in range(4):
            ps = k.ps[4 + h % 2]
            k.mm(ps.p(), [(ps[0:64, 0:256], wkv[:, kc, h * 64:(h + 1) * 64], mnT[:, kc, :]) for kc in range(8)],
                 reads=mnT.all + wkv.all)
            k.op("dve", lambda e: e.tensor_copy(kmT[:, h, :], ps[0:64, 0:256]), reads=ps.all, writes=kmT.all)
        for m2 in range(2):
            ps = k.ps[6 + m2]
            k.mm(ps.p(), [(ps[:, 0:256], mnT[:, kc, m2 * 128:(m2 + 1) * 128], wkv[:, kc, 256:512])
                          for kc in range(8)], reads=mnT.all + wkv.all)
            k.op("dve", lambda e: e.tensor_copy(vm[:, m2, :, 0:64],
                                                ps[:, 0:256].rearrange("p (h d) -> p h d", h=4)),
                 reads=ps.all, writes=vm.all)
        ab = AttnBufs(k, ph)
        qT = [k.sb(ph, [64, S], BF16, "qmT%d" % i) for i in range(2)]
        cross = k.sb(ph, [128, S // 128, 256], F32, "crossall")
        rec = k.sb(ph, [128, 4, 1], F32, "mrec")
        ai = 0
        for h in range(4):
            q = qT[h % 2]
            k.dma("sp", q[:], QMT[h * 64:(h + 1) * 64, :], writes=q.all)
            for qt in range(S // 512):
                acc = k.ps[3 + ai % 2]
                ai += 1
                accv = acc[:, 0:260].rearrange("p (s w) -> p s w", s=4)
                for m2 in range(2):
                    E = score_unit(k, ab, [(kmT[:, h, m2 * 128:(m2 + 1) * 128], q[:, qt * 512:(qt + 1) * 512])],
                                   reads=kmT.all + q.all)
                    pv_accum(k, acc, accv, E, vm[:, m2, h, :], vm.all, first=(m2 == 0), last=(m2 == 1))
                k.op("dve", lambda e: e.reciprocal(rec[:], accv[:, :, 64:65]), reads=acc.all, writes=rec.all)
                k.op("dve", lambda e: e.tensor_tensor(cross[:, qt * 4:(qt + 1) * 4, h * 64:(h + 1) * 64],
                                                      accv[:, :, 0:64], rec[:].to_broadcast([128, 4, 64]),
                                                      op=ALU.mult),
                     reads=acc.all + rec.all, writes=cross.all)
        k.dma("sp", MIX[:, 768:1024].rearrange("(t p) c -> p t c", p=128), cross[:], reads=cross.all)
        k.barrier()


NEGBIG = -1.0e9


def phase_nsa(k, c, sc, gate_b_ap, pos_ap, w1_ap, w2_ap, MIX, S, cache=None):
    QT, KCT, VCT, KST, KWT, VS, VW, GL = (sc[n] for n in ("QT", "KCT", "VCT", "KST", "KWT", "VS", "VW", "GL"))
    slopes = alibi_slopes12()
    NQT = S // 512
    NKT = S // 128
    NTT = S // 128
    NCMP = (S - 32) // 16 + 1
    NNT = (NCMP + 127) // 128
    nc = k.nc
    with contextlib.ExitStack() as ph:
        cache = {} if cache is None else cache
        build = "done" not in cache

        def ctile(name, shape, dt):
            t = k.sb(ph, shape, dt, name)
            if not build:
                k.dma("sp", t[:], cache[name], writes=t.all)
            return t

        def cstore(name, t, shape, dt):
            cache[name] = nc.dram_tensor("nsac_" + name, list(shape), dt, kind="Internal").ap()
            k.dma("sp", cache[name], t[:], reads=t.all)

        Dwin = [ctile("Dwin%d" % j, [128, 512], F32) for j in range(8)]
        Dgen = ctile("Dgen", [128, 512], F32)
        Dcm = {}
        for qt in range(NQT):
            for nt in range(NNT):
                base = 512 * qt - 2048 * nt - 31
                if base + 511 < 0:
                    continue
                Dcm[(qt, nt)] = ctile("Dcm%d_%d" % (qt, nt), [128, 512], F32)
        ovl = ctile("ovl", [128, NNT, 64], F32)
        keepG = ctile("keepG", [128, 128], F32)
        addG = ctile("addG", [128, 128], F32)
        Ex = ctile("Ex", [64, NKT, 128], BF16)
        if build:
            for j in range(8):
                koff = 128 * j - 512
                t = Dwin[j]
                k.op("pool", lambda e: e.iota(t[:], pattern=[[1, 512]], base=-koff, channel_multiplier=-1,
                                              allow_small_or_imprecise_dtypes=True), writes=t.all)
                k.op("pool", lambda e: e.tensor_scalar(t[:], t[:], -8.0, None, op0=ALU.mult), reads=t.all, writes=t.all)
                k.op("pool", lambda e: e.affine_select(t[:], t[:], pattern=[[1, 512]], compare_op=ALU.is_ge,
                                                       fill=NEGBIG, base=-koff, channel_multiplier=-1),
                     reads=t.all, writes=t.all)
                k.op("pool", lambda e: e.affine_select(t[:], t[:], pattern=[[-1, 512]], compare_op=ALU.is_ge,
                                                       fill=NEGBIG, base=511 + koff, channel_multiplier=1),
                     reads=t.all, writes=t.all)
            k.op("pool", lambda e: e.iota(Dgen[:], pattern=[[1, 512]], base=0, channel_multiplier=-1,
                                          allow_small_or_imprecise_dtypes=True), writes=Dgen.all)
            k.op("pool", lambda e: e.tensor_scalar(Dgen[:], Dgen[:], -8.0, None, op0=ALU.mult),
                 reads=Dgen.all, writes=Dgen.all)
            for (qt, nt), t in Dcm.items():
                base = 512 * qt - 2048 * nt - 31
                k.op("pool", lambda e: e.iota(t[:], pattern=[[1, 512]], base=base, channel_multiplier=-16,
                                              allow_small_or_imprecise_dtypes=True), writes=t.all)
                k.op("pool", lambda e: e.tensor_scalar(t[:], t[:], -8.0, None, op0=ALU.mult),
                     reads=t.all, writes=t.all)
                k.op("pool", lambda e: e.affine_select(t[:], t[:], pattern=[[1, 512]], compare_op=ALU.is_ge,
                                                       fill=NEGBIG, base=base, channel_multiplier=-16),
                     reads=t.all, writes=t.all)
            k.op("pool", lambda e: e.memset(ovl[:], 1.0), writes=ovl.all)
            k.op("pool", lambda e: e.affine_select(ovl[:], ovl[:], pattern=[[128, NNT], [-4, 64]], compare_op=ALU.is_ge,
                                                   fill=0.0, base=1, channel_multiplier=1), reads=ovl.all, writes=ovl.all)
            k.op("pool", lambda e: e.affine_select(ovl[:], ovl[:], pattern=[[-128, NNT], [4, 64]], compare_op=ALU.is_ge,
                                                   fill=0.0, base=3, channel_multiplier=-1), reads=ovl.all, writes=ovl.all)
            k.op("pool", lambda e: e.memset(keepG[:], 1.0), writes=keepG.all)
            k.op("pool", lambda e: e.memset(addG[:], 1.0e4), writes=addG.all)
            for hp in range(2):
                rows = slice(hp * 64, hp * 64 + 64)
                k.op("pool", lambda e: e.affine_select(keepG[rows, :], keepG[rows, :], pattern=[[-1, 128]],
                                                       compare_op=ALU.is_ge, fill=0.0, base=62 + hp,
                                                       channel_multiplier=0), reads=keepG.all, writes=keepG.all)
                k.op("pool", lambda e: e.affine_select(addG[rows, :], addG[rows, :], pattern=[[1, 128]],
                                                       compare_op=ALU.is_ge, fill=0.0, base=-63 - hp,
                                                       channel_multiplier=0), reads=addG.all, writes=addG.all)
                k.op("pool", lambda e: e.affine_select(addG[rows, :], addG[rows, :], pattern=[[-1, 128]],
                                                       compare_op=ALU.is_ge, fill=0.0, base=64 + hp,
                                                       channel_multiplier=0), reads=addG.all, writes=addG.all)
            tmpst = contextlib.ExitStack()
            Exf = k.sb(tmpst, [64, NKT * 128], F32, "Exf")
            k.op("pool", lambda e: e.memset(Exf[:], 1.0), writes=Exf.all)
            k.op("pool", lambda e: e.affine_select(Exf[:].rearrange("p (a b c) -> p a b c", b=2, c=64),
                                                   Exf[:].rearrange("p (a b c) -> p a b c", b=2, c=64),
                                                   pattern=[[-2, NKT], [-1, 2], [0, 64]], compare_op=ALU.is_equal,
                                                   fill=0.0, base=0, channel_multiplier=1),
                 reads=Exf.all, writes=Exf.all)
            k.op("dve", lambda e: e.tensor_copy(Ex[:], Exf[:].rearrange("p (a c) -> p a c", c=128)),
                 reads=Exf.all, writes=Ex.all)
            k.barrier()
            tmpst.close()
            for j in range(8):
                cstore("Dwin%d" % j, Dwin[j], [128, 512], F32)
            cstore("Dgen", Dgen, [128, 512], F32)
            for (qt, nt), t in Dcm.items():
                cstore("Dcm%d_%d" % (qt, nt), t, [128, 512], F32)
            cstore("ovl", ovl, [128, NNT, 64], F32)
            cstore("keepG", keepG, [128, 128], F32)
            cstore("addG", addG, [128, 128], F32)
            cstore("Ex", Ex, [64, NKT, 128], BF16)
            cache["done"] = True
        gates = k.sb(ph, [128, NTT, 36], F32, "gates")
        gb = load_vec_bcast(k, ph, gate_b_ap, 36, "gateb")
        k.dma("sp", gates[:], GL.rearrange("(t p) c -> p t c", p=128), writes=gates.all)
        k.op("dve", lambda e: e.tensor_tensor(gates[:], gates[:], gb[:].rearrange("p (o c) -> p o c", o=1)
                                              .to_broadcast([128, NTT, 36]), op=ALU.add),
             reads=gates.all + gb.all, writes=gates.all)
        k.op("act", lambda e: e.activation(gates[:], gates[:], AF.Exp, scale=-1.0), reads=gates.all, writes=gates.all)
        k.op("dve", lambda e: e.tensor_scalar(gates[:], gates[:], 1.0, None, op0=ALU.add),
             reads=gates.all, writes=gates.all)
        k.op("dve", lambda e: e.reciprocal(gates[:], gates[:]), reads=gates.all, writes=gates.all)
        w1 = k.sb(ph, [64, 2, 32, 128], BF16, "cw1")
        w2 = k.sb(ph, [128, 2, 64], BF16, "cw2")
        posT = k.sb(ph, [64, 2, 32], BF16, "cposT")
        for kv in range(2):
            k.dma("pool", w1[:, kv, :, :], w1_ap[kv].rearrange("l d e -> d l e"), writes=w1.all)
            k.dma("pool", w2[:, kv, :], w2_ap[kv], writes=w2.all)
        with nc.allow_non_contiguous_dma(reason="tiny pos table"):
            k.dma("pool", posT[:], pos_ap.rearrange("v l d -> d v l"), writes=posT.all)
        cbias = k.sb(ph, [128, 2], F32, "cbias")
        for kv in range(2):
            ps = k.ps[6]
            k.mm(ps.p(), [(ps[:, 0:1], w1[:, kv, l, :], posT[:, kv, l:l + 1]) for l in range(32)],
                 reads=w1.all + posT.all)
            k.op("dve", lambda e: e.tensor_copy(cbias[:, kv:kv + 1], ps[:, 0:1]), reads=ps.all, writes=cbias.all)
        qT = k.sb(ph, [64, 3, S], BF16, "qT")
        ksT = k.sb(ph, [64, S], BF16, "ksT")
        kwT = k.sb(ph, [64, S], BF16, "kwT")
        cT = [k.sb(ph, [64, S], BF16, "cT%d" % i) for i in range(2)]
        vs = k.sb(ph, [128, NKT, 65], BF16, "vsaug")
        vw = k.sb(ph, [128, NKT, 65], BF16, "vwaug")
        k.op("pool", lambda e: e.memset(vs[:], 1.0), writes=vs.all)
        k.op("pool", lambda e: e.memset(vw[:], 1.0), writes=vw.all)
        kcbT = k.sb(ph, [64, NNT * 128], BF16, "kcbT")
        vcb = k.sb(ph, [128, NNT, 129], BF16, "vcbaug")
        k.op("pool", lambda e: e.memset(kcbT[:], 0.0), writes=kcbT.all)
        k.op("pool", lambda e: e.memset(vcb[:], 0.0), writes=vcb.all)
        k.op("pool", lambda e: e.memset(vcb[:, :, 128:129], 1.0), writes=vcb.all)
        k.op("dve", lambda e: e.tensor_copy(vcb[:, :, 64:128], ovl[:]), reads=ovl.all, writes=vcb.all)
        hx = k.sb(ph, [128, 256], F32, "hx")
        hy = k.sb(ph, [128, 256], F32, "hy")
        hact = k.sb(ph, [128, 256], BF16, "hact")
        k.op("pool", lambda e: e.memset(hact[:], 0.0), writes=hact.all)
        ab = AttnBufs(k, ph)
        oc = k.sb(ph, [128, 4, 3, 64], F32, "oc")
        imp = k.sb(ph, [128, 4, 64], F32, "imp")
        rec = k.sb(ph, [128, 4, 1], F32, "nrec")
        gs = k.sb(ph, [128, 4, 1], F32, "ngs")
        scq = k.sb(ph, [128, 4, 64], F32, "scq")
        wk = k.sb(ph, [128, 4, 64], F32, "wk")
        m8 = k.sb(ph, [128, 8], F32, "m8")
        nmT = k.sb(ph, [64, 512], BF16, "nmT")
        t1 = k.sb(ph, [128, 4, 64], F32, "t1")
        mixt = [k.sb(ph, [128, 4, 192], F32, "mixt%d" % i) for i in range(2)]
        mi = 0
        NC1 = NCMP
        for g in range(4):
            for r in range(3):
                k.dma("sp", qT[:, r, :], QT[(g * 3 + r) * 64:(g * 3 + r + 1) * 64, :], writes=qT.all)
            k.dma("sp", ksT[:], KST[g * 64:(g + 1) * 64, :], writes=ksT.all)
            k.dma("sp", kwT[:], KWT[g * 64:(g + 1) * 64, :], writes=kwT.all)
            k.dma("sp", cT[0][:], KCT[g * 64:(g + 1) * 64, :], writes=cT[0].all)
            k.dma("sp", cT[1][:], VCT[g * 64:(g + 1) * 64, :], writes=cT[1].all)
            with nc.allow_non_contiguous_dma(reason="v head slice"):
                k.dma("sp", vs[:, :, 0:64], VS[:, g * 64:(g + 1) * 64].rearrange("(t p) d -> p t d", p=128),
                      writes=vs.all)
                k.dma("sp", vw[:, :, 0:64], VW[:, g * 64:(g + 1) * 64].rearrange("(t p) d -> p t d", p=128),
                      writes=vw.all)
            for kv in range(2):
                src3 = cT[kv][:].rearrange("p (n s) -> p n s", s=16)
                ps = k.ps[6]
                k.mm(ps.p(), [(ps[:, 0:NC1], w1[:, kv, l, :], src3[:, (l // 16):(l // 16) + NC1, l % 16])
                              for l in range(32)], reads=w1.all + cT[kv].all)
                k.op("dve", lambda e: e.tensor_scalar(hx[:, 0:NC1], ps[:, 0:NC1], cbias[:, kv:kv + 1], None,
                                                      op0=ALU.add), reads=ps.all + cbias.all, writes=hx.all)
                k.op("dve", lambda e: e.tensor_tensor(hy[:, 0:NC1], hx[:, 0:NC1], hx[:, 0:NC1], op=ALU.mult),
                     reads=hx.all, writes=hy.all)
                k.op("dve", lambda e: e.tensor_scalar(hy[:, 0:NC1], hy[:, 0:NC1], 0.044715, 1.0,
                                                      op0=ALU.mult, op1=ALU.add), reads=hy.all, writes=hy.all)
                k.op("dve", lambda e: e.tensor_tensor(hy[:, 0:NC1], hy[:, 0:NC1], hx[:, 0:NC1], op=ALU.mult),
                     reads=hy.all + hx.all, writes=hy.all)
                k.op("act", lambda e: e.activation(hy[:, 0:NC1], hy[:, 0:NC1], AF.Exp, scale=-1.5957691216057308),
                     reads=hy.all, writes=hy.all)
                k.op("dve", lambda e: e.tensor_scalar(hy[:, 0:NC1], hy[:, 0:NC1], 1.0, None, op0=ALU.add),
                     reads=hy.all, writes=hy.all)
                k.op("dve", lambda e: e.reciprocal(hy[:, 0:NC1], hy[:, 0:NC1]), reads=hy.all, writes=hy.all)
                k.op("dve", lambda e: e.tensor_tensor(hact[:, 0:NC1], hx[:, 0:NC1], hy[:, 0:NC1], op=ALU.mult),
                     reads=hy.all + hx.all, writes=hact.all)
                if kv == 0:
                    ps2 = k.ps[7]
                    k.mm(ps2.p(), [(ps2[0:64, 0:NC1], w2[:, 0, :], hact[:, 0:NC1])], reads=w2.all + hact.all)
                    k.op("dve", lambda e: e.tensor_copy(kcbT[:, 0:NC1], ps2[0:64, 0:NC1]),
                         reads=ps2.all, writes=kcbT.all)
                else:
                    for nt in range(NNT):
                        nn = min(128, NC1 - nt * 128)
                        ps2 = k.ps[7]
                        k.mm(ps2.p(), [(ps2[0:nn, 0:64], hact[:, nt * 128:nt * 128 + nn], w2[:, 1, :])],
                             reads=w2.all + hact.all)
                        k.op("dve", lambda e: e.tensor_copy(vcb[0:nn, nt, 0:64], ps2[0:nn, 0:64]),
                             reads=ps2.all, writes=vcb.all)
            for qt in range(NQT):
                q0 = qt * 512
                accA, accB = k.ps[3], k.ps[4]
                vA = accA[:, 0:258].rearrange("p (s w) -> p s w", s=2)
                vB = accB[:, 0:258].rearrange("p (s w) -> p s w", s=2)
                nts = [nt for nt in range(NNT) if (qt, nt) in Dcm]

                def cmp_post(r):
                    for half, (acc, av) in enumerate(((accA, vA), (accB, vB))):
                        sl = slice(half * 2, half * 2 + 2)
                        k.op("dve", lambda e: e.tensor_scalar(rec[:, sl, :], av[:, :, 128:129], 1e-30, None,
                                                              op0=ALU.add), reads=acc.all, writes=rec.all)
                        k.op("dve", lambda e: e.reciprocal(rec[:, sl, :], rec[:, sl, :]), reads=rec.all, writes=rec.all)
                        k.op("dve", lambda e: e.tensor_tensor(oc[:, sl, r, :], av[:, :, 0:64],
                                                              rec[:, sl, :].to_broadcast([128, 2, 64]), op=ALU.mult),
                             reads=acc.all + rec.all, writes=oc.all)
                        if r == 0:
                            k.op("dve", lambda e: e.tensor_tensor(imp[:, sl, :], av[:, :, 64:128],
                                                                  rec[:, sl, :].to_broadcast([128, 2, 64]),
                                                                  op=ALU.mult),
                                 reads=acc.all + rec.all, writes=imp.all)
                        else:
                            k.op("dve", lambda e: e.tensor_tensor(wk[:, sl, :], av[:, :, 64:128],
                                                                  rec[:, sl, :].to_broadcast([128, 2, 64]),
                                                                  op=ALU.mult),
                                 reads=acc.all + rec.all, writes=wk.all)
                            k.op("dve", lambda e: e.tensor_tensor(imp[:, sl, :], imp[:, sl, :], wk[:, sl, :],
                                                                  op=ALU.add),
                                 reads=imp.all + wk.all, writes=imp.all)

                units = []
                for r in range(3):
                    for ii, nt in enumerate(nts):
                        units.append(dict(r=r, nt=nt, first=(ii == 0), last=(ii == len(nts) - 1)))

                def cmp_score(u):
                    Dt = Dcm[(qt, u["nt"])]
                    return score_unit(k, ab, [(kcbT[:, u["nt"] * 128:(u["nt"] + 1) * 128], qT[:, u["r"], q0:q0 + 512])],
                                      reads=kcbT.all + qT.all, Dt=Dt[:], Dres=Dt.all, slope=slopes[g * 3 + u["r"]],
                                      bias=0.0)

                def cmp_pv(u, E):
                    nt, first, last = u["nt"], u["first"], u["last"]
                    k.mmx(accA.p(), [(vA[:, qs, :], E[:, qs * 128:(qs + 1) * 128], vcb[:, nt, :],
                                      bool(first and qs == 0), bool(last and qs == 1)) for qs in range(2)],
                          reads=E.all + vcb.all)
                    k.mmx(accB.p(), [(vB[:, qs, :], E[:, (qs + 2) * 128:(qs + 3) * 128], vcb[:, nt, :],
                                      bool(first and qs == 0), bool(last and qs == 1)) for qs in range(2)],
                          reads=E.all + vcb.all)
                    if last:
                        cmp_post(u["r"])
                run_pipeline(units, cmp_score, cmp_pv)
                pst = k.ps[5]
                for qs in range(4):
                    tt = qt * 4 + qs
                    lo = 64 - 2 * tt
                    k.op("dve", lambda e: e.scalar_tensor_tensor(scq[:, qs, :], imp[:, qs, :], 1.0,
                                                                 keepG[:, lo:lo + 64], op0=ALU.add, op1=ALU.mult),
                         reads=imp.all + keepG.all, writes=scq.all)
                    k.op("dve", lambda e: e.tensor_tensor(scq[:, qs, :], scq[:, qs, :], addG[:, lo:lo + 64],
                                                          op=ALU.add), reads=scq.all + addG.all, writes=scq.all)
                    k.op("dve", lambda e: e.memset(scq[:, qs, 0:1], 1.0e4), reads=scq.all, writes=scq.all)
                    k.op("dve", lambda e: e.max(m8[:], scq[:, qs, :]), reads=scq.all, writes=m8.all)
                    k.op("dve", lambda e: e.match_replace(wk[:, qs, :], m8[:], scq[:, qs, :], 0.0),
                         reads=scq.all + m8.all, writes=wk.all)
                    k.op("dve", lambda e: e.max(m8[:], wk[:, qs, :]), reads=wk.all, writes=m8.all)
                    k.op("dve", lambda e: e.match_replace(wk[:, qs, :], m8[:], wk[:, qs, :], 0.0),
                         reads=wk.all + m8.all, writes=wk.all)
                    k.op("dve", lambda e: e.tensor_tensor(wk[:, qs, :], scq[:, qs, :], wk[:, qs, :], op=ALU.subtract),
                         reads=wk.all + scq.all, writes=wk.all)
                    k.op("dve", lambda e: e.tensor_scalar(wk[:, qs, :], wk[:, qs, :], 1.0, 1.0,
                                                          op0=ALU.min, op1=ALU.subtract), reads=wk.all, writes=wk.all)
                    k.op("pe", lambda e: e.transpose(pst[0:64, qs * 128:(qs + 1) * 128], wk[:, qs, :], c["idf"][:]),
                         reads=wk.all + c["idf"].all, writes=pst.all)
                k.op("act", lambda e: e.activation(nmT[:], pst[0:64, :], AF.Copy, scale=30000.0),
                     reads=pst.all, writes=nmT.all)
                mx = mixt[mi % 2]
                mi += 1
                accS, accW = k.ps[6], k.ps[7]
                vS = accS[:, 0:260].rearrange("p (s w) -> p s w", s=4)
                vW = accW[:, 0:260].rearrange("p (s w) -> p s w", s=4)
                gsl = slice(qt * 4, qt * 4 + 4)

                def combine(r):
                    h = g * 3 + r
                    k.op("dve", lambda e: e.reciprocal(rec[:], vS[:, :, 64:65]), reads=accS.all, writes=rec.all)
                    k.op("dve", lambda e: e.tensor_tensor(gs[:], rec[:], gates[:, gsl, 12 + h:13 + h], op=ALU.mult),
                         reads=rec.all + gates.all, writes=gs.all)
                    k.op("dve", lambda e: e.tensor_tensor(t1[:], vS[:, :, 0:64], gs[:].to_broadcast([128, 4, 64]),
                                                          op=ALU.mult), reads=accS.all + gs.all, writes=t1.all)
                    k.op("dve", lambda e: e.reciprocal(rec[:], vW[:, :, 64:65]), reads=accW.all, writes=rec.all)
                    k.op("dve", lambda e: e.tensor_tensor(gs[:], rec[:], gates[:, gsl, 24 + h:25 + h], op=ALU.mult),
                         reads=rec.all + gates.all, writes=gs.all)
                    k.op("dve", lambda e: e.tensor_tensor(wk[:], vW[:, :, 0:64], gs[:].to_broadcast([128, 4, 64]),
                                                          op=ALU.mult), reads=accW.all + gs.all, writes=wk.all)
                    k.op("pool", lambda e: e.tensor_tensor(t1[:], t1[:], wk[:], op=ALU.add),
                         reads=t1.all + wk.all, writes=t1.all)
                    k.op("pool", lambda e: e.tensor_tensor(wk[:], oc[:, :, r, :],
                                                           gates[:, gsl, h:h + 1].to_broadcast([128, 4, 64]),
                                                           op=ALU.mult), reads=oc.all + gates.all, writes=wk.all)
                    k.op("pool", lambda e: e.tensor_tensor(mx[:, :, r * 64:(r + 1) * 64], t1[:], wk[:], op=ALU.add),
                         reads=t1.all + wk.all, writes=mx.all)

                units = []
                for r in range(3):
                    sl_h = slopes[g * 3 + r]
                    skts = [kt for kt in range(4 * qt + 4)
                            if kt >= 4 * qt or sl_h * (q0 - (kt * 128 + 127)) < SKIP_EXP]
                    for ii, kt in enumerate(skts):
                        units.append(dict(br="s", r=r, kt=kt, first=(ii == 0), last=(ii == len(skts) - 1), post=False,
                                          cols=((128 * (kt - 4 * qt), 512) if kt >= 4 * qt else (0, 512))))
                    wkts = [kt for kt in range(4 * qt - 4, 4 * qt + 4) if kt >= 0]
                    for ii, kt in enumerate(wkts):
                        j = kt - 4 * qt + 4
                        units.append(dict(br="w", r=r, kt=kt, first=(ii == 0), last=(ii == len(wkts) - 1),
                                          post=(ii == len(wkts) - 1),
                                          cols=((0, 128 * (j + 1)) if j <= 3 else (128 * (j - 4), 512))))

                def sw_score(u):
                    r, kt = u["r"], u["kt"]
                    c0, c1 = u["cols"]
                    sl_h = slopes[g * 3 + r]
                    if u["br"] == "s":
                        if kt >= 4 * qt:
                            Dt, bias = Dwin[4 + kt - 4 * qt], 0.0
                        else:
                            Dt, bias = Dgen, -sl_h * (q0 - kt * 128)
                        return score_unit(k, ab, [(ksT[:, kt * 128:(kt + 1) * 128], qT[:, r, q0 + c0:q0 + c1]),
                                                  (Ex[:, kt, :], nmT[:, c0:c1])],
                                          reads=ksT.all + qT.all + Ex.all + nmT.all, Dt=Dt[:], Dres=Dt.all,
                                          slope=sl_h, bias=bias, cols=(c0, c1))
                    Dt = Dwin[kt - 4 * qt + 4]
                    return score_unit(k, ab, [(kwT[:, kt * 128:(kt + 1) * 128], qT[:, r, q0 + c0:q0 + c1])],
                                      reads=kwT.all + qT.all, Dt=Dt[:], Dres=Dt.all, slope=sl_h, bias=0.0,
                                      cols=(c0, c1))

                def sw_pv(u, E):
                    if u["br"] == "s":
                        pv_accum(k, accS, vS, E, vs[:, u["kt"], :], vs.all, first=u["first"], last=u["last"],
                                 cols=u["cols"])
                    else:
                        pv_accum(k, accW, vW, E, vw[:, u["kt"], :], vw.all, first=u["first"], last=u["last"],
                                 cols=u["cols"])
                    if u["post"]:
                        combine(u["r"])
                run_pipeline(units, sw_score, sw_pv)
                with nc.allow_non_contiguous_dma(reason="mix slice"):
                    k.dma("sp", MIX[q0:q0 + 512, g * 192:(g + 1) * 192].rearrange("(s p) c -> p s c", p=128),
                          mx[:], reads=mx.all)
        k.barrier()


GN_EPS = 64e-5
TBK = 256


def phase_rwkv(k, c, ZT, prm, MIX, S, dbg=99):
    nc = k.nc
    NB = S // TBK if dbg >= 99 else 1
    NCH = TBK // 64
    NPR = TBK // 128
    W6 = [128, 6, TBK]

    def V3(t):
        return t[:].rearrange("p (c t) -> p c t", c=6)

    with contextlib.ExitStack() as ph:
        def colvec(name, ap, n=6):
            return load_vec_col(k, ph, ap, n, name)
        mu_c = k.sb(ph, [128, 21], F32, "mu_c")
        with nc.allow_non_contiguous_dma(reason="small vector load"):
            k.dma("sp", mu_c[:, 0:20], prm["mu"][0:2560].rearrange("(c p) -> p c", p=128), writes=mu_c.all)
            k.dma("sp", mu_c[0:32, 20:21], prm["mu"][2560:2592].rearrange("(c p) -> p c", p=32), writes=mu_c.all)
        w0n = colvec("w0n", prm["w0"])
        a0n = colvec("a0n", prm["a0"])
        kk_c = colvec("kk_c", prm["k_k"])
        ka_c = colvec("ka_c", prm["k_a"])
        rk_c = colvec("rk_c", prm["r_k"].rearrange("h n -> (h n)"))
        omka = k.sb(ph, [128, 6], F32, "omka")
        k.op("dve", lambda e: e.tensor_scalar(omka[:], ka_c[:], -1.0, 1.0, op0=ALU.mult, op1=ALU.add),
             reads=ka_c.all, writes=omka.all)
        k.op("dve", lambda e: e.tensor_scalar(w0n[:], w0n[:], -1.0, None, op0=ALU.mult), reads=w0n.all, writes=w0n.all)
        k.op("dve", lambda e: e.tensor_scalar(a0n[:], a0n[:], -1.0, None, op0=ALU.mult), reads=a0n.all, writes=a0n.all)
        lnw = load_vec_bcast(k, ph, prm["lnx_w"], 768, "lnw")
        lnb = load_vec_bcast(k, ph, prm["lnx_b"], 768, "lnb")
        w2a2 = k.sb(ph, [128, 768], BF16, "w2a2")
        k.dma("pool", w2a2[0:64, :], prm["w2"], writes=w2a2.all)
        k.dma("pool", w2a2[64:128, :], prm["a2"], writes=w2a2.all)
        g2a = k.sb(ph, [128, 768], BF16, "g2a")
        g2b = k.sb(ph, [32, 768], BF16, "g2b")
        k.dma("pool", g2a[:], prm["g2"][0:128, :], writes=g2a.all)
        k.dma("pool", g2b[:], prm["g2"][128:160, :], writes=g2b.all)
        tiny = k.sb(ph, [128, 1], F32, "tiny")
        k.op("pool", lambda e: e.memset(tiny[:], 1e-24), writes=tiny.all)
        gneps = k.sb(ph, [128, 1], F32, "gneps")
        k.op("pool", lambda e: e.memset(gneps[:], GN_EPS), writes=gneps.all)
        msT = k.sb(ph, [128, 128], F32, "msT")
        miT = k.sb(ph, [128, 128], F32, "miT")
        msL = k.sb(ph, [128, 128], F32, "msL")
        for (t, pat, base, cm) in ((msT, [[1, 128]], -1, -1), (miT, [[1, 128]], 0, -1), (msL, [[-1, 128]], -1, 1)):
            k.op("pool", lambda e: e.memset(t[:], 1.0), writes=t.all)
            k.op("pool", lambda e: e.affine_select(t[:], t[:], pattern=pat, compare_op=ALU.is_ge, fill=0.0,
                                                   base=base, channel_multiplier=cm), reads=t.all, writes=t.all)
            k.op("pool", lambda e: e.memset(t[0:64, 64:128], 0.0), reads=t.all, writes=t.all)
            k.op("pool", lambda e: e.memset(t[64:128, 0:64], 0.0), reads=t.all, writes=t.all)
        MASK1 = k.sb(ph, [128, 2, 256], F32, "MASK1")
        MASK3 = k.sb(ph, [128, 4, 128], F32, "MASK3")
        for hh in range(2):
            k.op("pool", lambda e: e.tensor_copy(MASK1[:, hh, 0:128], msT[:]), reads=msT.all, writes=MASK1.all)
            k.op("pool", lambda e: e.tensor_copy(MASK1[:, hh, 128:256], miT[:]), reads=miT.all, writes=MASK1.all)
        for hh in range(4):
            k.op("pool", lambda e: e.tensor_copy(MASK3[:, hh, :], msL[:]), reads=msL.all, writes=MASK3.all)
        bones = k.sb(ph, [128, 128], F32, "bones")
        k.op("pool", lambda e: e.memset(bones[:], 1.0), writes=bones.all)
        k.op("pool", lambda e: e.memset(bones[0:64, 64:128], 0.0), reads=bones.all, writes=bones.all)
        k.op("pool", lambda e: e.memset(bones[64:128, 0:64], 0.0), reads=bones.all, writes=bones.all)
        bsel = k.sb(ph, [128, 2], F32, "bsel")
        k.op("pool", lambda e: e.memset(bsel[:], 0.0), writes=bsel.all)
        k.op("pool", lambda e: e.memset(bsel[0:64, 0:1], 1.0), reads=bsel.all, writes=bsel.all)
        k.op("pool", lambda e: e.memset(bsel[64:128, 1:2], 1.0), reads=bsel.all, writes=bsel.all)
        Mscan = k.sb(ph, [128, TBK], F32, "Mscan")
        k.op("pool", lambda e: e.memset(Mscan[:], 1.0), writes=Mscan.all)
        k.op("pool", lambda e: e.memset(Mscan[:].rearrange("p (n s) -> p n s", s=64)[:, :, 0:1], 0.0),
             reads=Mscan.all, writes=Mscan.all)
        ZL = k.sb(ph, [128, 6, TBK + 1], F32, "ZL")
        ZS = k.sb(ph, [128, 3, TBK + 1], F32, "ZS")
        bufs = [k.sb(ph, [128, 6 * TBK], F32, "rb%d" % i) for i in range(7)]
        b1, b2, b3, b4, b5, b6, b7 = bufs
        sm = [k.sb(ph, [128, TBK], F32, "rsm%d" % i) for i in range(3)]
        twz = k.sb(ph, [128, TBK], BF16, "twz")
        sgA = k.sb(ph, [128, TBK], BF16, "sgA")
        sgB = k.sb(ph, [32, TBK], BF16, "sgB")
        CL = k.sb(ph, [128, 6, NCH], F32, "CL")
        PC = k.sb(ph, [128, 6, NCH], F32, "PC")
        AR = k.sb(ph, [128, 6, 2, TBK], BF16, "AR")
        BK = k.sb(ph, [128, 6, 2, TBK], BF16, "BK")
        Vt = k.sb(ph, [128, NPR, 768], BF16, "Vt")
        BBt = k.sb(ph, [128, NPR, 768], BF16, "BBt")
        KBt = k.sb(ph, [128, NPR, 768], BF16, "KBt")
        Gt = k.sb(ph, [128, NPR, 768], F32, "Gt")
        bon = k.sb(ph, [128, NPR, 12], F32, "bon")
        T1 = k.sb(ph, [128, 6, 2, 256], BF16, "T1")
        T2 = k.sb(ph, [128, 6, 2, 256], BF16, "T2")
        X = [[k.sb(ph, [128, 3, 128], F32, "nX%d_%d" % (g_, i)) for i in range(2)] for g_ in range(4)]
        XT = [[k.sb(ph, [128, 3, 128], F32, "nXT%d_%d" % (g_, i)) for i in range(2)] for g_ in range(4)]
        TT = [[k.sb(ph, [128, 3, 128], F32, "nTT%d_%d" % (g_, i)) for i in range(2)] for g_ in range(4)]
        TTb = k.sb(ph, [128, 6, 2, 128], BF16, "TTb")
        U0 = k.sb(ph, [128, 768], F32, "U0")
        ST = k.sb(ph, [128, 6, 64], F32, "ST")
        STb = k.sb(ph, [128, 6, 64], BF16, "STb")
        k.op("pool", lambda e: e.memset(ST[:], 0.0), writes=ST.all)
        k.op("pool", lambda e: e.memset(STb[:], 0.0), writes=STb.all)
        Wsb = k.sb(ph, [128, 768], BF16, "Wsb")
        Usb = k.sb(ph, [128, 768], BF16, "Usb")
        Ysb = k.sb(ph, [128, 768], F32, "Ysb")
        Yw = [k.sb(ph, [128, 768], F32, "Yw%d" % i) for i in range(2)]
        st12 = [k.sb(ph, [128, 12], F32, "st12_%d" % i) for i in range(4)]
        Stmp = k.sb(ph, [128, 6, 64], F32, "Stmp")
        Yo = [k.sb(ph, [128, 768], F32, "Yo%d" % i) for i in range(2)]

        def load_group(dst, row0, nchunks, t0, rows=128):
            src = ZT[row0:row0 + nchunks * 128, :].rearrange("(c p) t -> p c t", p=128) if rows == 128 else None
            if t0 == 0:
                k.op("pool", lambda e: e.memset(dst[0:rows, 0:nchunks, 0:1], 0.0), writes=dst.all)
                k.dma("sp", dst[0:rows, 0:nchunks, 1:TBK + 1], src[:, :, 0:TBK], writes=dst.all)
            else:
                k.dma("sp", dst[0:rows, 0:nchunks, :], src[:, :, t0 - 1:t0 + TBK], writes=dst.all)

        def shift_mix(dst3, src, mu0, nchunks, tmp3, rows=128):
            k.op("dve", lambda e: e.tensor_tensor(tmp3[0:rows, 0:nchunks, :], src[0:rows, 0:nchunks, 0:TBK],
                                                  src[0:rows, 0:nchunks, 1:TBK + 1], op=ALU.subtract),
                 reads=src.all, writes=[tmp3_res[0]])
            for cc in range(nchunks):
                k.op("dve",
                     lambda e: e.scalar_tensor_tensor(dst3[0:rows, cc, :], tmp3[0:rows, cc, :],
                                                      mu_c[0:rows, mu0 + cc:mu0 + cc + 1],
                                                      src[0:rows, cc, 1:TBK + 1], op0=ALU.mult, op1=ALU.add),
                     reads=[tmp3_res[0]] + src.all + mu_c.all, writes=[dst3_res[0]])

        mixq = 0
        for blk in range(NB):
            t0 = blk * TBK
            if t0 == 0:
                k.op("pool", lambda e: e.memset(ZS[:, :, 0:1], 0.0), writes=ZS.all)
            lo = 1 if t0 == 0 else 0
            k.dma("sp", ZS[:, 0:2, lo:TBK + 1],
                  ZT[2304:2560, :].rearrange("(c p) t -> p c t", p=128)[:, :, t0 - 1 + lo:t0 + TBK], writes=ZS.all)
            k.dma("sp", ZS[0:32, 2, lo:TBK + 1], ZT[2560:2592, t0 - 1 + lo:t0 + TBK], writes=ZS.all)
            zsm = sm[0], sm[1], sm[2]
            for cc, rows in ((0, 128), (1, 128), (2, 32)):
                d = zsm[cc]
                k.op("dve", lambda e: e.tensor_tensor(d[0:rows, :], ZS[0:rows, cc, 0:TBK], ZS[0:rows, cc, 1:TBK + 1],
                                                      op=ALU.subtract), reads=ZS.all, writes=d.all)
                k.op("dve", lambda e: e.scalar_tensor_tensor(d[0:rows, :], d[0:rows, :],
                                                             mu_c[0:rows, 18 + cc:19 + cc], ZS[0:rows, cc, 1:TBK + 1],
                                                             op0=ALU.mult, op1=ALU.add),
                     reads=d.all + ZS.all + mu_c.all, writes=d.all)
            zwza = sm[0]
            k.op("act", lambda e: e.activation(zwza[0:64, :], zwza[0:64, :], AF.Exp, scale=2.0),
                 reads=zwza.all, writes=zwza.all)
            k.op("dve", lambda e: e.tensor_scalar(zwza[0:64, :], zwza[0:64, :], 1.0, None, op0=ALU.add),
                 reads=zwza.all, writes=zwza.all)
            k.op("dve", lambda e: e.reciprocal(zwza[0:64, :], zwza[0:64, :]), reads=zwza.all, writes=zwza.all)
            k.op("dve", lambda e: e.tensor_scalar(twz[0:64, :], zwza[0:64, :], -2.0, 1.0, op0=ALU.mult, op1=ALU.add),
                 reads=zwza.all, writes=twz.all)
            k.op("dve", lambda e: e.tensor_copy(twz[64:128, :], zwza[64:128, :]), reads=zwza.all, writes=twz.all)
            for (src_, dst_, rows) in ((sm[1], sgA, 128), (sm[2], sgB, 32)):
                k.op("act", lambda e: e.activation(src_[0:rows, :], src_[0:rows, :], AF.Exp, scale=-1.0),
                     reads=src_.all, writes=src_.all)
                k.op("dve", lambda e: e.tensor_scalar(src_[0:rows, :], src_[0:rows, :], 1.0, None, op0=ALU.add),
                     reads=src_.all, writes=src_.all)
                k.op("dve", lambda e: e.reciprocal(src_[0:rows, :], src_[0:rows, :]), reads=src_.all, writes=src_.all)
                k.op("dve", lambda e: e.tensor_copy(dst_[0:rows, :], src_[0:rows, :]), reads=src_.all, writes=dst_.all)
            if dbg <= 1:
                break
            LW, CUM, A3 = V3(b1), V3(b2), V3(b3)
            for (dst3, dres, rows, bcol) in ((LW, b1, slice(0, 64), w0n), (A3, b3, slice(64, 128), a0n)):
                for fc in range(6):
                    ps = k.ps[fc % 4]
                    k.mm(ps.p(), [(ps[:, 0:TBK], w2a2[rows, fc * 128:(fc + 1) * 128], twz[rows, :])],
                         reads=w2a2.all + twz.all)
                    k.op("act", lambda e: e.activation(dst3[:, fc, :], ps[:, 0:TBK], AF.Exp, scale=-1.0,
                                                       bias=bcol[:, fc:fc + 1]),
                         reads=ps.all + bcol.all, writes=dres.all)
                k.op("dve", lambda e: e.tensor_scalar(dres[:], dres[:], 1.0, None, op0=ALU.add),
                     reads=dres.all, writes=dres.all)
                k.op("dve", lambda e: e.reciprocal(dres[:], dres[:]), reads=dres.all, writes=dres.all)
            k.op("dve", lambda e: e.tensor_scalar(b1[:], b1[:], -0.6065306597126334, None, op0=ALU.mult),
                 reads=b1.all, writes=b1.all)
            for fc in range(6):
                k.op("dve", lambda e: e.tensor_tensor_scan(CUM[:, fc, :], Mscan[:], LW[:, fc, :], 0.0,
                                                           op0=ALU.mult, op1=ALU.add),
                     reads=Mscan.all + b1.all, writes=b2.all)
            CUM4 = b2[:].rearrange("p (c n s) -> p c n s", c=6, s=64)
            k.op("dve", lambda e: e.tensor_copy(CL[:], CUM4[:, :, :, 63]), reads=b2.all, writes=CL.all)
            k.op("act", lambda e: e.activation(PC[:], CL[:], AF.Exp), reads=CL.all, writes=PC.all)
            if dbg <= 2:
                break
            tmp3_res, dst3_res = [b7.p()], [b4.p()]
            load_group(ZL, 1536, 6, t0)
            shift_mix(V3(b4), ZL, 12, 6, V3(b7))
            VM = V3(b4)

            def to_token_major(src3, sres, dstT):
                for pr in range(NPR):
                    for half in range(2):
                        ps = k.ps[4 + (pr * 2 + half) % 4]
                        for j in range(3):
                            fc = half * 3 + j
                            k.op("pe", lambda e: e.transpose(ps[:, j * 128:(j + 1) * 128],
                                                             src3[:, fc, pr * 128:(pr + 1) * 128], c["idf"][:]),
                                 reads=sres.all + c["idf"].all, writes=ps.all)
                        if half == 0:
                            k.op("dve", lambda e: e.tensor_copy(dstT[:, pr, 0:384], ps[:, 0:384]),
                                 reads=ps.all, writes=dstT.all)
                        else:
                            k.op("act", lambda e: e.activation(dstT[:, pr, 384:768], ps[:, 0:384], AF.Copy),
                                 reads=ps.all, writes=dstT.all)
            to_token_major(VM, b4, Vt)
            if dbg <= 3:
                break
            dst3_res = [b4.p()]
            load_group(ZL, 768, 6, t0)
            shift_mix(V3(b4), ZL, 6, 6, V3(b7))
            KM, KK, B3, T7 = V3(b4), V3(b5), V3(b6), V3(b7)
            k.op("dve", lambda e: e.tensor_tensor(KK, KM, kk_c[:].rearrange("p (c o) -> p c o", o=1)
                                                  .to_broadcast(W6), op=ALU.mult),
                 reads=b4.all + kk_c.all, writes=b5.all)
            k.op("pool", lambda e: e.tensor_tensor(B3, KK, KK, op=ALU.mult), reads=b5.all, writes=b6.all)
            for fc in range(6):
                ps = k.ps[fc % 4]
                k.mm(ps.p(), [(ps[:, 0:TBK], bones[:], B3[:, fc, :])], reads=bones.all + b6.all)
                k.op("act", lambda e: e.activation(T7[:, fc, :], ps[:, 0:TBK], AF.Ln, bias=tiny[:, 0:1]),
                     reads=ps.all + tiny.all, writes=b7.all)
            k.op("act", lambda e: e.activation(b7[:], b7[:], AF.Exp, scale=-0.5), reads=b7.all, writes=b7.all)
            k.op("dve", lambda e: e.tensor_tensor(b5[:], b5[:], b7[:], op=ALU.mult),
                 reads=b5.all + b7.all, writes=b5.all)
            for fc in range(6):
                k.op("dve",
                     lambda e: e.tensor_scalar(T7[:, fc, :], A3[:, fc, :], ka_c[:, fc:fc + 1], omka[:, fc:fc + 1],
                                               op0=ALU.mult, op1=ALU.add),
                     reads=b3.all + ka_c.all + omka.all, writes=b7.all)
            k.op("dve", lambda e: e.tensor_tensor(b4[:], b4[:], b7[:], op=ALU.mult),
                 reads=b4.all + b7.all, writes=b4.all)
            k.op("pool", lambda e: e.tensor_tensor(b6[:], b5[:], b3[:], op=ALU.mult),
                 reads=b5.all + b3.all, writes=b6.all)
            if dbg <= 4:
                break
            BK4, AR4 = BK, AR
            k.op("act", lambda e: e.activation(b3[:], b2[:], AF.Exp, scale=-1.0), reads=b2.all, writes=b3.all)
            k.op("dve", lambda e: e.tensor_tensor(BK4[:, :, 0, :], B3, V3(b3), op=ALU.mult),
                 reads=b6.all + b3.all, writes=BK.all)
            k.op("pool", lambda e: e.tensor_tensor(BK4[:, :, 1, :], KM, V3(b3), op=ALU.mult),
                 reads=b4.all + b3.all, writes=BK.all)
            k.op("dve", lambda e: e.scalar_tensor_tensor(
                b3[:].rearrange("p (c n s) -> p c n s", c=6, s=64), CUM4, -1.0,
                CL[:].rearrange("p c (n o) -> p c n o", o=1).to_broadcast([128, 6, NCH, 64]),
                op0=ALU.mult, op1=ALU.add), reads=b2.all + CL.all, writes=b3.all)
            k.op("act", lambda e: e.activation(b3[:], b3[:], AF.Exp), reads=b3.all, writes=b3.all)
            k.op("dve", lambda e: e.tensor_tensor(b6[:], b6[:], b3[:], op=ALU.mult),
                 reads=b6.all + b3.all, writes=b6.all)
            k.op("pool", lambda e: e.tensor_tensor(b7[:], b4[:], b3[:], op=ALU.mult),
                 reads=b4.all + b3.all, writes=b7.all)
            to_token_major(V3(b6), b6, BBt)
            to_token_major(V3(b7), b7, KBt)
            k.op("dve", lambda e: e.tensor_tensor(b3[:], b2[:], b1[:], op=ALU.subtract),
                 reads=b2.all + b1.all, writes=b3.all)
            k.op("act", lambda e: e.activation(b3[:], b3[:], AF.Exp), reads=b3.all, writes=b3.all)
            k.op("dve", lambda e: e.scalar_tensor_tensor(AR4[:, :, 0, :], KK, -1.0, V3(b3), op0=ALU.mult, op1=ALU.mult),
                 reads=b5.all + b3.all, writes=AR.all)
            if dbg <= 5:
                break
            tmp3_res, dst3_res = [b7.p()], [b1.p()]
            load_group(ZL, 0, 6, t0)
            shift_mix(V3(b1), ZL, 0, 6, V3(b7))
            RM = V3(b1)
            k.op("act", lambda e: e.activation(b3[:], b2[:], AF.Exp), reads=b2.all, writes=b3.all)
            k.op("dve", lambda e: e.tensor_tensor(AR4[:, :, 1, :], RM, V3(b3), op=ALU.mult),
                 reads=b1.all + b3.all, writes=AR.all)
            k.op("pool", lambda e: e.tensor_tensor(b7[:], b1[:], b4[:], op=ALU.mult),
                 reads=b1.all + b4.all, writes=b7.all)
            k.op("dve", lambda e: e.tensor_tensor(V3(b7), V3(b7), rk_c[:].rearrange("p (c o) -> p c o", o=1)
                                                  .to_broadcast(W6), op=ALU.mult),
                 reads=b7.all + rk_c.all, writes=b7.all)
            for pr in range(NPR):
                ps = k.ps[pr % 4]
                k.mmx(ps.p(), [(ps[:, fc * 2:fc * 2 + 2], V3(b7)[:, fc, pr * 128:(pr + 1) * 128], bsel[:],
                                True, True) for fc in range(6)], reads=b7.all + bsel.all)
                k.op("dve", lambda e: e.tensor_copy(bon[:, pr, :], ps[:, 0:12]), reads=ps.all, writes=bon.all)
            if dbg <= 6:
                break
            for pr in range(NPR):
                for half in range(2):
                    ps = k.ps[4 + (pr * 2 + half) % 4]
                    k.mm(ps.p(), [(ps[:, 0:384], sgA[:, pr * 128:(pr + 1) * 128], g2a[:, half * 384:(half + 1) * 384]),
                                  (ps[:, 0:384], sgB[:, pr * 128:(pr + 1) * 128], g2b[:, half * 384:(half + 1) * 384])],
                         reads=sgA.all + sgB.all + g2a.all + g2b.all)
                    k.op("act", lambda e: e.activation(Gt[:, pr, half * 384:(half + 1) * 384], ps[:, 0:384], AF.Copy),
                         reads=ps.all, writes=Gt.all)
            if dbg <= 7:
                break
            for pr in range(NPR):
                tc = slice(pr * 128, (pr + 1) * 128)
                bi = 0
                for which, dstT in ((0, T1), (1, T2)):
                    for hp in range(2):
                        rows = slice(hp * 64, hp * 64 + 64)
                        for f0 in range(0, 6, 2):
                            ps = k.ps[bi % 4]
                            bi += 1
                            mms = []
                            for ff in range(2):
                                fc = f0 + ff
                                mms.append((ps[:, ff * 256:(ff + 1) * 256].rearrange("p (a t) -> p a t", a=2),
                                            BK[rows, fc, which, tc], AR[rows, fc, :, tc], True, True))
                            k.mmx(ps.p(), mms, reads=BK.all + AR.all)
                            k.op("dve", lambda e: e.tensor_tensor(dstT[:, f0:f0 + 2, hp, :],
                                                                  ps[:, :].rearrange("p (h c) -> p h c", h=2), MASK1[:],
                                                                  op=ALU.mult),
                                 reads=ps.all + MASK1.all, writes=dstT.all)
                if dbg <= 8:
                    break
                groups = [(hp, f0) for hp in range(2) for f0 in (0, 3)]
                cur = 0
                for gi, (hp, f0) in enumerate(groups):
                    rows = slice(hp * 64, hp * 64 + 64)
                    psL, psT = k.ps[2 * gi], k.ps[2 * gi + 1]
                    k.mmx(psL.p(), [(psL[:, ff * 128:(ff + 1) * 128], AR[rows, f0 + ff, 0, tc], BK[rows, f0 + ff, 0, tc],
                                     True, True) for ff in range(3)], reads=BK.all + AR.all)
                    k.mmx(psT.p(), [(psT[:, ff * 128:(ff + 1) * 128], BK[rows, f0 + ff, 0, tc], AR[rows, f0 + ff, 0, tc],
                                     True, True) for ff in range(3)], reads=BK.all + AR.all)
                for gi, (hp, f0) in enumerate(groups):
                    psL, psT = k.ps[2 * gi], k.ps[2 * gi + 1]
                    Xg, XTg, TTg = X[gi], XT[gi], TT[gi]
                    k.op("dve", lambda e: e.tensor_tensor(Xg[0][:], psL[:, 0:384].rearrange("p (h c) -> p h c", h=3),
                                                          MASK3[:, 0:3, :], op=ALU.mult),
                         reads=psL.all + MASK3.all, writes=Xg[0].all)
                    k.op("dve", lambda e: e.tensor_tensor(XTg[0][:], psT[:, 0:384].rearrange("p (h c) -> p h c", h=3),
                                                          msT[:].rearrange("p (o c) -> p o c", o=1)
                                                          .to_broadcast([128, 3, 128]), op=ALU.mult),
                         reads=psT.all + msT.all, writes=XTg[0].all)
                    k.op("pool", lambda e: e.tensor_tensor(TTg[0][:], XTg[0][:],
                                                           c["idf"][:].rearrange("p (o c) -> p o c", o=1)
                                                           .to_broadcast([128, 3, 128]), op=ALU.add),
                         reads=XTg[0].all + c["idf"].all, writes=TTg[0].all)
                for lvl in range(5):
                    nxt = 1 - cur
                    for gi in range(4):
                        pa, pb = k.ps[2 * gi], k.ps[2 * gi + 1]
                        Xg, XTg = X[gi], XT[gi]
                        k.mmx(pa.p(), [(pa[:, hh * 128:(hh + 1) * 128], XTg[cur][:, hh, :], Xg[cur][:, hh, :], True, True)
                                       for hh in range(3)], reads=Xg[cur].all + XTg[cur].all)
                        if lvl < 4:
                            k.mmx(pb.p(), [(pb[:, hh * 128:(hh + 1) * 128], Xg[cur][:, hh, :], XTg[cur][:, hh, :],
                                            True, True) for hh in range(3)], reads=Xg[cur].all + XTg[cur].all)
                    for gi in range(4):
                        pa, pb = k.ps[2 * gi], k.ps[2 * gi + 1]
                        Xg, XTg = X[gi], XT[gi]
                        if gi % 2 == 0:
                            k.op("dve", lambda e: e.tensor_copy(Xg[nxt][:], pa[:, 0:384].rearrange("p (h c) -> p h c", h=3)),
                                 reads=pa.all, writes=Xg[nxt].all)
                        else:
                            k.op("act", lambda e: e.activation(Xg[nxt][:], pa[:, 0:384].rearrange("p (h c) -> p h c", h=3),
                                                               AF.Copy), reads=pa.all, writes=Xg[nxt].all)
                        if lvl < 4:
                            if gi % 2 == 1:
                                k.op("dve", lambda e: e.tensor_copy(XTg[nxt][:],
                                                                    pb[:, 0:384].rearrange("p (h c) -> p h c", h=3)),
                                     reads=pb.all, writes=XTg[nxt].all)
                            else:
                                k.op("act", lambda e: e.activation(XTg[nxt][:],
                                                                   pb[:, 0:384].rearrange("p (h c) -> p h c", h=3),
                                                                   AF.Copy), reads=pb.all, writes=XTg[nxt].all)
                    for gi in range(4):
                        pc2 = k.ps[2 * gi]
                        Xg, TTg = X[gi], TT[gi]
                        k.mmx(pc2.p(), [(pc2[:, hh * 128:(hh + 1) * 128], Xg[nxt][:, hh, :], TTg[cur][:, hh, :],
                                         True, True) for hh in range(3)], reads=Xg[nxt].all + TTg[cur].all)
                    for gi in range(4):
                        pc2 = k.ps[2 * gi]
                        TTg = TT[gi]
                        k.op("dve", lambda e: e.tensor_tensor(TTg[nxt][:], TTg[cur][:],
                                                              pc2[:, 0:384].rearrange("p (h c) -> p h c", h=3),
                                                              op=ALU.add),
                             reads=pc2.all + TTg[cur].all, writes=TTg[nxt].all)
                    cur = nxt
                for gi, (hp, f0) in enumerate(groups):
                    TTg = TT[gi]
                    k.op("act" if gi % 2 == 0 else "pool",
                         (lambda e: e.activation(TTb[:, f0:f0 + 3, hp, :], TTg[cur][:], AF.Copy))
                         if gi % 2 == 0 else
                         (lambda e: e.tensor_copy(TTb[:, f0:f0 + 3, hp, :], TTg[cur][:])),
                         reads=TTg[cur].all, writes=TTb.all)
                if dbg <= 9:
                    break
                for ci in range(2):
                    prs = slice(ci * 64, ci * 64 + 64)
                    tl = slice(ci * 64, ci * 64 + 64)
                    pw = [k.ps[4], k.ps[5]]
                    for half in range(2):
                        k.mmx(pw[half].p(), [(pw[half][prs, hh * 64:(hh + 1) * 64],
                                              T2[prs, (half * 6 + hh) // 2, (half * 6 + hh) % 2, tl],
                                              Vt[prs, pr, (half * 6 + hh) * 64:(half * 6 + hh + 1) * 64],
                                              hh == 0, hh == 5) for hh in range(6)], reads=T2.all + Vt.all)
                        k.op("act", lambda e: e.activation(Wsb[prs, half * 384:(half + 1) * 384], pw[half][prs, 0:384],
                                                           AF.Copy), reads=pw[half].all, writes=Wsb.all)
                    pu = [k.ps[6], k.ps[7]]
                    for half in range(2):
                        k.mmx(pu[half].p(), [(pu[half][prs, hh * 64:(hh + 1) * 64],
                                              TTb[prs, (half * 6 + hh) // 2, (half * 6 + hh) % 2, tl],
                                              Wsb[prs, (half * 6 + hh) * 64:(half * 6 + hh + 1) * 64],
                                              hh == 0, hh == 5) for hh in range(6)], reads=TTb.all + Wsb.all)
                        k.op("dve", lambda e: e.tensor_copy(U0[prs, half * 384:(half + 1) * 384], pu[half][prs, 0:384]),
                             reads=pu[half].all, writes=U0.all)
                for ci in range(2):
                    ch = pr * 2 + ci
                    prs = slice(ci * 64, ci * 64 + 64)
                    tcc = slice(pr * 128 + ci * 64, pr * 128 + ci * 64 + 64)
                    tl = slice(ci * 64, ci * 64 + 64)
                    pw = [k.ps[0], k.ps[1]]
                    py = [k.ps[2], k.ps[3]]
                    for (pbk, which) in ((pw, 0), (py, 1)):
                        for hp in range(2):
                            rows = slice(hp * 64, hp * 64 + 64)
                            k.mmx(pbk[hp].p(), [(pbk[hp][prs, fc * 64:(fc + 1) * 64], AR[rows, fc, which, tcc],
                                                 STb[rows, fc, :], fc == 0, fc == 5) for fc in range(6)],
                                  reads=AR.all + STb.all)
                    Wv = Wsb[:].rearrange("p (f h v) -> p f h v", f=6, h=2)
                    for hp in range(2):
                        k.op("dve" if hp == 0 else "act",
                             (lambda e: e.tensor_copy(Wv[prs, :, hp, :], pw[hp][prs, 0:384].rearrange("p (f v) -> p f v", f=6)))
                             if hp == 0 else
                             (lambda e: e.activation(Wv[prs, :, hp, :], pw[hp][prs, 0:384].rearrange("p (f v) -> p f v", f=6),
                                                     AF.Copy)),
                             reads=pw[hp].all, writes=Wsb.all)
                    pu = [k.ps[4], k.ps[5]]
                    for half in range(2):
                        k.mmx(pu[half].p(), [(pu[half][prs, hh * 64:(hh + 1) * 64],
                                              TTb[prs, (half * 6 + hh) // 2, (half * 6 + hh) % 2, tl],
                                              Wsb[prs, (half * 6 + hh) * 64:(half * 6 + hh + 1) * 64],
                                              hh == 0, hh == 5) for hh in range(6)], reads=TTb.all + Wsb.all)
                        k.op("dve", lambda e: e.tensor_tensor(Usb[prs, half * 384:(half + 1) * 384],
                                                              U0[prs, half * 384:(half + 1) * 384], pu[half][prs, 0:384],
                                                              op=ALU.add),
                             reads=pu[half].all + U0.all, writes=Usb.all)
                    pya = [k.ps[6], k.ps[7]]
                    for half in range(2):
                        mms = []
                        for hh in range(6):
                            h = half * 6 + hh
                            o = pya[half][prs, hh * 64:(hh + 1) * 64]
                            mms.append((o, T1[prs, h // 2, h % 2, 128 + ci * 64:128 + ci * 64 + 64],
                                        Usb[prs, h * 64:(h + 1) * 64], hh == 0, False))
                            mms.append((o, T2[prs, h // 2, h % 2, 128 + ci * 64:128 + ci * 64 + 64],
                                        Vt[prs, pr, h * 64:(h + 1) * 64], False, hh == 5))
                        k.mmx(pya[half].p(), mms, reads=T1.all + T2.all + Vt.all + Usb.all)
                    Yv = Ysb[:].rearrange("p (f h v) -> p f h v", f=6, h=2)
                    for hp in range(2):
                        k.op("act", lambda e: e.activation(Yv[prs, :, hp, :],
                                                           py[hp][prs, 0:384].rearrange("p (f v) -> p f v", f=6), AF.Copy),
                             reads=py[hp].all, writes=Ysb.all)
                    for half in range(2):
                        k.op("dve", lambda e: e.tensor_tensor(Ysb[prs, half * 384:(half + 1) * 384],
                                                              Ysb[prs, half * 384:(half + 1) * 384], pya[half][prs, 0:384],
                                                              op=ALU.add),
                             reads=pya[half].all + Ysb.all, writes=Ysb.all)
                    pss = k.ps[0]
                    mms = []
                    for h in range(12):
                        fc, hp = h // 2, h % 2
                        rows = slice(hp * 64, hp * 64 + 64)
                        o = pss[rows, fc * 64:(fc + 1) * 64]
                        mms.append((o, BBt[prs, pr, h * 64:(h + 1) * 64], Usb[prs, h * 64:(h + 1) * 64], h < 2, False))
                        mms.append((o, KBt[prs, pr, h * 64:(h + 1) * 64], Vt[prs, pr, h * 64:(h + 1) * 64], False, h == 11))
                    k.mmx(pss.p(), mms, reads=BBt.all + KBt.all + Vt.all + Usb.all)
                    k.op("dve", lambda e: e.tensor_tensor(Stmp[:], ST[:], PC[:, :, ch:ch + 1].to_broadcast([128, 6, 64]),
                                                          op=ALU.mult), reads=ST.all + PC.all, writes=Stmp.all)
                    k.op("dve", lambda e: e.tensor_tensor(ST[:], Stmp[:], pss[:, 0:384].rearrange("p (c v) -> p c v", c=6),
                                                          op=ALU.add), reads=Stmp.all + pss.all, writes=ST.all)
                    k.op("act", lambda e: e.activation(STb[:], ST[:], AF.Copy), reads=ST.all, writes=STb.all)
                if dbg <= 10:
                    break
                yw, yo = Yw[pr % 2], Yo[pr % 2]
                s1, s2, mean, rstd = st12
                Y3 = Ysb[:].rearrange("p (h v) -> p h v", h=12)
                k.op("dve", lambda e: e.reduce_sum(s1[:], Y3, axis=AX.X), reads=Ysb.all, writes=s1.all)
                k.op("pool", lambda e: e.tensor_tensor(yw[:], Ysb[:], Ysb[:], op=ALU.mult), reads=Ysb.all, writes=yw.all)
                k.op("dve", lambda e: e.reduce_sum(s2[:], yw[:].rearrange("p (h v) -> p h v", h=12), axis=AX.X),
                     reads=yw.all, writes=s2.all)
                k.op("dve", lambda e: e.tensor_scalar(mean[:], s1[:], 1.0 / 64, None, op0=ALU.mult),
                     reads=s1.all, writes=mean.all)
                k.op("dve", lambda e: e.tensor_tensor(s1[:], mean[:], mean[:], op=ALU.mult), reads=mean.all, writes=s1.all)
                k.op("dve", lambda e: e.scalar_tensor_tensor(s2[:], s2[:], 1.0 / 64, s1[:], op0=ALU.mult, op1=ALU.subtract),
                     reads=s2.all + s1.all, writes=s2.all)
                k.op("act", lambda e: e.activation(rstd[:], s2[:], AF.Ln, bias=gneps[:, 0:1]),
                     reads=s2.all + gneps.all, writes=rstd.all)
                k.op("act", lambda e: e.activation(rstd[:], rstd[:], AF.Exp, scale=-0.5), reads=rstd.all, writes=rstd.all)
                yw3 = yw[:].rearrange("p (h v) -> p h v", h=12)
                k.op("dve", lambda e: e.tensor_tensor(yw3, Y3, mean[:].rearrange("p (h o) -> p h o", o=1)
                                                      .to_broadcast([128, 12, 64]), op=ALU.subtract),
                     reads=Ysb.all + mean.all, writes=yw.all)
                k.op("dve", lambda e: e.tensor_tensor(yw3, yw3, rstd[:].rearrange("p (h o) -> p h o", o=1)
                                                      .to_broadcast([128, 12, 64]), op=ALU.mult),
                     reads=yw.all + rstd.all, writes=yw.all)
                k.op("pool", lambda e: e.tensor_tensor(yw[:], yw[:], lnw[:], op=ALU.mult),
                     reads=yw.all + lnw.all, writes=yw.all)
                k.op("pool", lambda e: e.tensor_tensor(yw[:], yw[:], lnb[:], op=ALU.add),
                     reads=yw.all + lnb.all, writes=yw.all)
                yo3 = yo[:].rearrange("p (h v) -> p h v", h=12)
                k.op("dve", lambda e: e.tensor_tensor(yo3, Vt[:, pr, :].rearrange("p (h v) -> p h v", h=12),
                                                      bon[:, pr, :].rearrange("p (h o) -> p h o", o=1)
                                                      .to_broadcast([128, 12, 64]), op=ALU.mult),
                     reads=Vt.all + bon.all, writes=yo.all)
                k.op("dve", lambda e: e.tensor_tensor(yo[:], yo[:], yw[:], op=ALU.add),
                     reads=yo.all + yw.all, writes=yo.all)
                k.op("dve", lambda e: e.tensor_tensor(yo[:], yo[:], Gt[:, pr, :], op=ALU.mult),
                     reads=yo.all + Gt.all, writes=yo.all)
                with nc.allow_non_contiguous_dma(reason="mix slice"):
                    k.dma("sp", MIX[t0 + pr * 128:t0 + (pr + 1) * 128, 0:768], yo[:], reads=yo.all)
        k.barrier()


SEQ = 4096
DEPTH = 4
PARAM_SHAPES = {
    "norm1": (4, 1024), "norm_mem": (4, 1024), "w_mem_kv": (4, 1024, 512), "w_o": (4, 1024, 1024),
    "norm2": (4, 1024), "w_ffn_in": (4, 1024, 5632), "w_ffn_out": (4, 2816, 1024),
    "nsa_w_in": (2, 1024, 2596), "nsa_gate_b": (2, 36), "nsa_cmp_pos": (2, 2, 32, 64),
    "nsa_cmp_w1": (2, 2, 32, 64, 128), "nsa_cmp_w2": (2, 2, 128, 64),
    "rw_w_in": (2, 1024, 2848), "rw_mu": (2, 2592), "rw_w0": (2, 768), "rw_w2": (2, 64, 768),
    "rw_a0": (2, 768), "rw_a2": (2, 64, 768), "rw_g2": (2, 160, 768), "rw_k_k": (2, 768), "rw_k_a": (2, 768),
    "rw_r_k": (2, 12, 64), "rw_lnx_w": (2, 768), "rw_lnx_b": (2, 768), "final_norm": (1024,),
}


def build_program(S=SEQ, depth=DEPTH):
    nc = bass.Bass("TRN2", target_bir_lowering=False)
    X0 = nc.dram_tensor("x", [S, 1024], F32, kind="ExternalInput").ap()
    MEM = nc.dram_tensor("mem", [256, 1024], F32, kind="ExternalInput").ap()
    P = {n: nc.dram_tensor(n, list(sh), F32, kind="ExternalInput").ap() for n, sh in PARAM_SHAPES.items()}
    OUT = nc.dram_tensor("out", [S, 1024], F32, kind="ExternalOutput").ap()

    def SC(name, shape, dt):
        return nc.dram_tensor(name, list(shape), dt, kind="Internal").ap()
    XA, XB = SC("XA", (S, 1024), F32), SC("XB", (S, 1024), F32)
    MIX = SC("MIX", (S, 1024), F32)
    sc = dict(QT=SC("QT", (768, S), BF16), KCT=SC("KCT", (256, S), BF16), VCT=SC("VCT", (256, S), BF16),
              KST=SC("KST", (256, S), BF16), KWT=SC("KWT", (256, S), BF16), VS=SC("VS", (S, 256), BF16),
              VW=SC("VW", (S, 256), BF16), GL=SC("GL", (S, 36), F32), QMT=SC("QMT", (256, S), BF16))
    ZT = SC("ZT", (2592, S), F32)
    with contextlib.ExitStack() as st:
        k = K(nc, st)
        with contextlib.ExitStack() as ph0:
            c = make_consts(k, ph0)
            Xcur = X0
            nsa_cache = {}
            for i in range(depth):
                j = i // 2
                if i % 2 == 0:
                    outs = [dict(c0=0, n=768, mode="fm", dst=sc["QT"], dt=BF16),
                            dict(c0=768, n=256, mode="fm", dst=sc["KCT"], dt=BF16),
                            dict(c0=1024, n=256, mode="fm", dst=sc["VCT"], dt=BF16),
                            dict(c0=1280, n=256, mode="fm", dst=sc["KST"], dt=BF16),
                            dict(c0=1536, n=256, mode="tm", dst=sc["VS"], dt=BF16),
                            dict(c0=1792, n=256, mode="fm", dst=sc["KWT"], dt=BF16),
                            dict(c0=2048, n=256, mode="tm", dst=sc["VW"], dt=BF16),
                            dict(c0=2304, n=36, mode="tm", dst=sc["GL"], dt=F32),
                            dict(c0=2340, n=256, mode="fm", dst=sc["QMT"], dt=BF16)]
                    phase_proj(k, c, Xcur, P["norm1"][i], P["nsa_w_in"][j], outs, S)
                    phase_nsa(k, c, sc, P["nsa_gate_b"][j], P["nsa_cmp_pos"][j], P["nsa_cmp_w1"][j],
                              P["nsa_cmp_w2"][j], MIX, S, cache=nsa_cache)
                else:
                    outs = [dict(c0=0, n=2592, mode="fm", dst=ZT, dt=F32),
                            dict(c0=2592, n=256, mode="fm", dst=sc["QMT"], dt=BF16)]
                    phase_proj(k, c, Xcur, P["norm1"][i], P["rw_w_in"][j], outs, S)
                    prm = dict(mu=P["rw_mu"][j], w0=P["rw_w0"][j], w2=P["rw_w2"][j], a0=P["rw_a0"][j],
                               a2=P["rw_a2"][j], g2=P["rw_g2"][j], k_k=P["rw_k_k"][j], k_a=P["rw_k_a"][j],
                               r_k=P["rw_r_k"][j], lnx_w=P["rw_lnx_w"][j], lnx_b=P["rw_lnx_b"][j])
                    phase_rwkv(k, c, ZT, prm, MIX, S)
                phase_memattn(k, c, MEM, P["norm_mem"][i], P["w_mem_kv"][i], sc["QMT"], MIX, S)
                phase_oproj(k, c, Xcur, XA, MIX, P["w_o"][i], S)
                phase_ffn(k, c, XA, XB, P["norm2"][i], P["w_ffn_in"][i], P["w_ffn_out"][i], S)
                Xcur = XB
            phase_final_norm(k, c, Xcur, OUT, P["final_norm"], S)
    return nc


def kernel(**inputs):
    x = np.ascontiguousarray(inputs["x"], dtype=np.float32)
    mem = np.ascontiguousarray(inputs["mem"], dtype=np.float32)
    B = x.shape[0]
    nc = build_program()
    params = {n: np.ascontiguousarray(inputs[n], dtype=np.float32) for n in PARAM_SHAPES}
    in_maps = []
    for b in range(B):
        d = dict(params)
        d["x"] = x[b]
        d["mem"] = mem[b]
        in_maps.append(d)
    res = run_bass_kernel_spmd(nc, in_maps, core_ids=list(range(B)))
    return np.stack([np.asarray(r["out"], dtype=np.float32) for r in res.results], axis=0)
```
